# Optimizing a Trainium2 kernel written in Bass

```python
import math
import jax, jax.numpy as jnp
from jax import lax
import numpy as np

D_MODEL = 1024
BATCH = 2
SEQ = 8192
DEPTH = 1
DEC_BATCH = 128
DEC_SEQ = 1
PAST_LEN = 2048
PAGE_SIZE = 128

CONV_CH = D_MODEL // 2
CONV_WIDTH = 31
N_HEADS = 4
HEAD_DIM = 64
K_DIM = 2 * HEAD_DIM
V_DIM = 2 * HEAD_DIM
ATTN_CH = N_HEADS * V_DIM
MIX_WIDTH = CONV_CH + ATTN_CH
IN_WIDTH = 2 * CONV_CH + 2 * N_HEADS * K_DIM + N_HEADS * V_DIM
ROT_DIM = HEAD_DIM // 4
ROPE_THETA = 500000.0
D_FF = -(-8 * D_MODEL // (3 * 256)) * 256
Q_BLOCK = 128
NORM_EPS = 1e-6
NEG_INF = -1e30

kernel_name = "hybrid_conformerconv_diffattn_decode"


def rmsnorm(x, g):
    xf = x.astype(jnp.float32)
    y = xf * lax.rsqrt(jnp.mean(xf * xf, axis=-1, keepdims=True) + NORM_EPS)
    return (y * g.astype(jnp.float32)).astype(x.dtype)


def layernorm(x, g, b):
    xf = x.astype(jnp.float32)
    mu = jnp.mean(xf, axis=-1, keepdims=True)
    xc = xf - mu
    y = xc * lax.rsqrt(jnp.mean(xc * xc, axis=-1, keepdims=True) + NORM_EPS)
    return (y * g.astype(jnp.float32) + b.astype(jnp.float32)).astype(x.dtype)


def lambda_init_for(layer_idx):
    return 0.8 - 0.6 * math.exp(-0.3 * layer_idx)


def partial_rope(x, pos):
    half = ROT_DIM // 2
    inv_freq = ROPE_THETA ** (-jnp.arange(0, ROT_DIM, 2, dtype=jnp.float32) / ROT_DIM)
    ang = pos.astype(jnp.float32)[:, None] * inv_freq[None, :]
    cos = jnp.cos(ang)[None, :, None, None, :].astype(x.dtype)
    sin = jnp.sin(ang)[None, :, None, None, :].astype(x.dtype)
    x1, x2, rest = x[..., :half], x[..., half:ROT_DIM], x[..., ROT_DIM:]
    return jnp.concatenate([x1 * cos - x2 * sin, x2 * cos + x1 * sin, rest], axis=-1)


def causal_dwconv(u_ext, w, b):
    y = lax.conv_general_dilated(u_ext, w[:, None, :], window_strides=(1,), padding='VALID',
                                 dimension_numbers=('NWC', 'WIO', 'NWC'),
                                 feature_group_count=CONV_CH)
    return y + b


def diff_attn_core(q, k, v, q_pos, k_pos, lam):
    s = jnp.einsum('bqhcd,bkhcd->bhcqk', q, k).astype(jnp.float32) * (HEAD_DIM ** -0.5)
    mask = k_pos[None, :] <= q_pos[:, None]
    s = jnp.where(mask, s, NEG_INF)
    p = jax.nn.softmax(s, axis=-1)
    a = p[:, :, 0] - lam * p[:, :, 1]
    return jnp.einsum('bhqk,bkhd->bqhd', a.astype(v.dtype), v)


def hybrid_layer(x, c, conv_past, k_past, v_past, q_pos, k_pos, lam_init, blockwise,
                 g_mix, g_ffn, w_ada, b_ada, w_in, dw_w, dw_b, cln_g, cln_b,
                 lq1, lk1, lq2, lk2, subln_g, w_out, w_ffn_in, w_ffn_out):
    B, T, _ = x.shape
    mods = jnp.split(jax.nn.silu(c) @ w_ada + b_ada, 6, axis=-1)
    sh_m, sc_m, gt_m, sh_f, sc_f, gt_f = [m[:, None, :] for m in mods]

    h = rmsnorm(x, g_mix) * (1 + sc_m) + sh_m
    proj = h @ w_in
    conv_in, q, k, v = jnp.split(
        proj, [2 * CONV_CH, 2 * CONV_CH + N_HEADS * K_DIM, 2 * CONV_CH + 2 * N_HEADS * K_DIM], axis=-1)

    u = conv_in[..., :CONV_CH] * jax.nn.sigmoid(conv_in[..., CONV_CH:])
    u_ext = jnp.concatenate([conv_past, u], axis=1)
    conv_out = jax.nn.silu(layernorm(causal_dwconv(u_ext, dw_w, dw_b), cln_g, cln_b))
    new_conv = u_ext[:, -(CONV_WIDTH - 1):]

    q = partial_rope(q.reshape(B, T, N_HEADS, 2, HEAD_DIM), q_pos)
    k = partial_rope(k.reshape(B, T, N_HEADS, 2, HEAD_DIM), q_pos)
    v = v.reshape(B, T, N_HEADS, V_DIM)
    k_rows = k.reshape(B, T, N_HEADS, K_DIM)
    lam = (jnp.exp(jnp.sum(lq1.astype(jnp.float32) * lk1.astype(jnp.float32)))
           - jnp.exp(jnp.sum(lq2.astype(jnp.float32) * lk2.astype(jnp.float32))) + lam_init)
    if k_past is None:
        k_all, v_all = k, v
    else:
        k_all = jnp.concatenate([k_past.reshape(B, -1, N_HEADS, 2, HEAD_DIM), k], axis=1)
        v_all = jnp.concatenate([v_past, v], axis=1)
    if blockwise:
        nq = T // Q_BLOCK
        qb = q.reshape(B, nq, Q_BLOCK, N_HEADS, 2, HEAD_DIM).swapaxes(0, 1)
        qpb = q_pos.reshape(nq, Q_BLOCK)
        o = lax.map(lambda a: diff_attn_core(a[0], k_all, v_all, a[1], k_pos, lam), (qb, qpb))
        o = o.swapaxes(0, 1).reshape(B, T, N_HEADS, V_DIM)
    else:
        o = diff_attn_core(q, k_all, v_all, q_pos, k_pos, lam)
    o = (rmsnorm(o, subln_g) * (1 - lam_init)).reshape(B, T, ATTN_CH)

    mix = jnp.concatenate([conv_out, o], axis=-1) @ w_out
    x = x + gt_m * mix

    h2 = rmsnorm(x, g_ffn) * (1 + sc_f) + sh_f
    gg, uu = jnp.split(h2 @ w_ffn_in, 2, axis=-1)
    x = x + gt_f * ((jax.nn.silu(gg) * uu) @ w_ffn_out)
    return x, k_rows, v, new_conv


def setup_inputs(seed: int = 0) -> dict:
    key = jax.random.key(seed)
    ks = jax.random.split(key, 32)
    f32 = jnp.float32
    n_pages = PAST_LEN // PAGE_SIZE
    n_used = DEC_BATCH * n_pages
    n_pool = n_used + -(-n_used // 4)
    nrm = lambda k, shape, s: jax.random.normal(k, shape, f32) * s
    page_table = jax.random.permutation(ks[0], n_pool)[:n_used].reshape(DEC_BATCH, n_pages).astype(jnp.int32)
    return {
        "x_prompt": nrm(ks[1], (BATCH, SEQ, D_MODEL), 1.0),
        "x_sample": nrm(ks[2], (DEC_BATCH, DEC_SEQ, D_MODEL), 1.0),
        "cache_k": nrm(ks[3], (DEPTH, n_pool, PAGE_SIZE, N_HEADS, K_DIM), 1.0),
        "cache_v": nrm(ks[4], (DEPTH, n_pool, PAGE_SIZE, N_HEADS, V_DIM), 1.0),
        "state_conv": nrm(ks[5], (DEPTH, DEC_BATCH, CONV_WIDTH - 1, CONV_CH), 0.5),
        "page_table": page_table,
        "c_prompt": nrm(ks[6], (BATCH, D_MODEL), 1.0),
        "c_sample": nrm(ks[7], (DEC_BATCH, D_MODEL), 1.0),
        "norm_mix_g": 1.0 + nrm(ks[8], (DEPTH, D_MODEL), 0.05),
        "norm_ffn_g": 1.0 + nrm(ks[9], (DEPTH, D_MODEL), 0.05),
        "norm_final_g": 1.0 + nrm(ks[10], (D_MODEL,), 0.05),
        "w_ada": nrm(ks[11], (DEPTH, D_MODEL, 6 * D_MODEL), 0.5 * D_MODEL ** -0.5),
        "b_ada": nrm(ks[12], (DEPTH, 6 * D_MODEL), 0.02),
        "w_in": nrm(ks[13], (DEPTH, D_MODEL, IN_WIDTH), D_MODEL ** -0.5),
        "conv_dw_w": nrm(ks[14], (DEPTH, CONV_WIDTH, CONV_CH), CONV_WIDTH ** -0.5),
        "conv_dw_b": nrm(ks[15], (DEPTH, CONV_CH), 0.02),
        "conv_ln_g": 1.0 + nrm(ks[16], (DEPTH, CONV_CH), 0.05),
        "conv_ln_b": nrm(ks[17], (DEPTH, CONV_CH), 0.02),
        "lambda_q1": nrm(ks[18], (DEPTH, HEAD_DIM), 0.1),
        "lambda_k1": nrm(ks[19], (DEPTH, HEAD_DIM), 0.1),
        "lambda_q2": nrm(ks[20], (DEPTH, HEAD_DIM), 0.1),
        "lambda_k2": nrm(ks[21], (DEPTH, HEAD_DIM), 0.1),
        "subln_g": 1.0 + nrm(ks[22], (DEPTH, V_DIM), 0.05),
        "w_out": nrm(ks[23], (DEPTH, MIX_WIDTH, D_MODEL), MIX_WIDTH ** -0.5),
        "w_ffn_in": nrm(ks[24], (DEPTH, D_MODEL, 2 * D_FF), D_MODEL ** -0.5),
        "w_ffn_out": nrm(ks[25], (DEPTH, D_FF, D_MODEL), D_FF ** -0.5),
    }


def reference(x_prompt, x_sample, cache_k, cache_v, state_conv, page_table, c_prompt, c_sample,
              norm_mix_g, norm_ffn_g, norm_final_g, w_ada, b_ada, w_in, conv_dw_w, conv_dw_b,
              conv_ln_g, conv_ln_b, lambda_q1, lambda_k1, lambda_q2, lambda_k2, subln_g,
              w_out, w_ffn_in, w_ffn_out):
    B, S, _ = x_prompt.shape
    DB, T, _ = x_sample.shape
    n_pages = page_table.shape[1]
    past = n_pages * cache_k.shape[2]
    pos_p = jnp.arange(S, dtype=jnp.int32)
    pos_s = past + jnp.arange(T, dtype=jnp.int32)
    kpos_s = jnp.arange(past + T, dtype=jnp.int32)
    conv_zero = jnp.zeros((B, CONV_WIDTH - 1, CONV_CH), x_prompt.dtype)

    xp, xs = x_prompt, x_sample
    kp_l, vp_l, cp_l, ks_l, vs_l, cs_l = [], [], [], [], [], []
    for l in range(DEPTH):
        lam_init = lambda_init_for(l)
        w = (norm_mix_g[l], norm_ffn_g[l], w_ada[l], b_ada[l], w_in[l], conv_dw_w[l], conv_dw_b[l],
             conv_ln_g[l], conv_ln_b[l], lambda_q1[l], lambda_k1[l], lambda_q2[l], lambda_k2[l],
             subln_g[l], w_out[l], w_ffn_in[l], w_ffn_out[l])
        k_past = cache_k[l][page_table].reshape(DB, past, N_HEADS, K_DIM)
        v_past = cache_v[l][page_table].reshape(DB, past, N_HEADS, V_DIM)
        xp, kp, vp, cp = hybrid_layer(xp, c_prompt, conv_zero, None, None, pos_p, pos_p, lam_init, True, *w)
        xs, ksn, vsn, csn = hybrid_layer(xs, c_sample, state_conv[l], k_past, v_past, pos_s, kpos_s,
                                         lam_init, False, *w)
        kp_l.append(kp); vp_l.append(vp); cp_l.append(cp)
        ks_l.append(ksn); vs_l.append(vsn); cs_l.append(csn)

    y_prompt = rmsnorm(xp, norm_final_g)
    y_sample = rmsnorm(xs, norm_final_g)
    k_prompt = jnp.stack(kp_l)
    v_prompt = jnp.stack(vp_l)
    conv_prompt = jnp.stack(cp_l)
    k_sample = jnp.stack(ks_l)
    v_sample = jnp.stack(vs_l)
    conv_sample = jnp.stack(cs_l)
    return (y_prompt, y_sample, k_prompt, v_prompt, conv_prompt, k_sample, v_sample, conv_sample)
```

```python
import numpy as np
import ml_dtypes
from contextlib import ExitStack
import concourse.bass as bass
import concourse.mybir as mybir
from concourse.bass_utils import run_bass_kernel_spmd

F32 = mybir.dt.float32
BF16 = mybir.dt.bfloat16
I32 = mybir.dt.int32
AF = mybir.ActivationFunctionType
ALU = mybir.AluOpType
AX = mybir.AxisListType

ENGS = ("pe", "act", "dve", "pool", "sp")
D = 1024
NS = 64
NOWN = 16
DFF = 2816
EPS = 1e-6
LAM_INIT = 0.2
SCALE = 0.125
SENT = 1 << 30


class Buf:
    __slots__ = ("name", "w", "r")

    def __init__(self, name):
        self.name = name
        self.w = None
        self.r = []


class T:
    __slots__ = ("ap", "b")

    def __init__(self, ap, name):
        self.ap = ap
        self.b = Buf(name)


class Sched:
    def __init__(self, nc, stack):
        self.nc = nc
        self.stack = stack
        self.prog = {e: [] for e in ENGS}
        self.sems = {}
        self.cnt = {}
        self.seen = {e: {} for e in ENGS}
        for e in ENGS:
            self._sem("E_" + e)
        self.dma_rr = 0
        self.out_evs = []

    def _sem(self, key):
        if key not in self.sems:
            self.sems[key] = self.stack.enter_context(self.nc.semaphore(key))
            self.cnt[key] = 0
        return self.sems[key]

    def _deps(self, eng, reads, writes, ignore_key=None):
        need = {}

        def add(ev):
            if ev is None:
                return
            k, v = ev
            if k == ignore_key:
                return
            if need.get(k, 0) < v:
                need[k] = v
        for b in reads:
            add(b.w)
        for b in writes:
            add(b.w)
            for ev in b.r:
                add(ev)
        waits = []
        seen = self.seen[eng]
        for k, v in need.items():
            if seen.get(k, 0) < v:
                seen[k] = v
                waits.append((k, v))
        return waits

    @staticmethod
    def _bufs(xs):
        return [x.b if isinstance(x, T) else x for x in xs]

    def op(self, eng, fn, reads=(), writes=(), inc=True):
        reads = self._bufs(reads)
        writes = self._bufs(writes)
        key = "E_" + eng
        waits = self._deps(eng, reads, writes)
        if eng == "pe":
            waits = [(k, v) for (k, v) in waits if k != key]
        if inc:
            self.cnt[key] += 1
            ev = (key, self.cnt[key])
        else:
            ev = (key, self.cnt[key] + 1)
        self.prog[eng].append((waits, fn, key if inc else None, 1))
        for b in reads:
            b.r.append(ev)
        for b in writes:
            b.w = ev
            b.r = []
        return ev

    def dma(self, eng, fn, reads=(), writes=(), semkey=None, is_out=False, group=None):
        reads = self._bufs(reads)
        writes = self._bufs(writes)
        if group is not None:
            semkey = "G_" + group
        if semkey is None:
            semkey = "D_" + (writes[0].name if writes else reads[0].name)
        self._sem(semkey)
        waits = self._deps(eng, reads, writes, ignore_key=semkey)
        self.cnt[semkey] += 16
        ev = (semkey, SENT if group is not None else self.cnt[semkey])
        self.prog[eng].append((waits, fn, semkey, 16))
        for b in reads:
            b.r.append(ev)
        for b in writes:
            b.w = ev
            b.r = []
        if is_out:
            self.out_evs.append(ev)
        return ev

    def barrier(self):
        evs = [(k, v) for k, v in self.cnt.items() if v > 0]
        for e in ENGS:
            waits = []
            seen = self.seen[e]
            for k, v in evs:
                if e == "pe" and k == "E_pe":
                    continue
                if seen.get(k, 0) < v:
                    seen[k] = v
                    waits.append((k, v))
            if waits:
                self.prog[e].append((waits, None, None, 0))

    def replay(self):
        sems = self.sems
        prog = self.prog

        def run(name):
            def body(e):
                for waits, fn, inck, incv in prog[name]:
                    for k, v in waits:
                        e.wait_ge(sems[k], self.cnt[k] if v == SENT else v)
                    if fn is None:
                        continue
                    ins = fn(e)
                    if inck is not None:
                        ins.then_inc(sems[inck], incv)
            return body

        with self.nc.Block() as block:
            block.tensor(run("pe"))
            block.scalar(run("act"))
            block.vector(run("dve"))
            block.gpsimd(run("pool"))
            block.sync(run("sp"))


class Mem:
    def __init__(self, big, nwords):
        self.big = big
        self.n = nwords
        self.top = 0
        self.uid = 0

    def mark(self):
        return self.top

    def release(self, m):
        self.top = m

    def alloc(self, name, shape, dt, parts=128):
        free = int(np.prod(shape[1:]))
        words = free if dt in (F32, I32) else (free + 1) // 2
        words = (words + 7) // 8 * 8
        assert self.top + words <= self.n, f"SBUF overflow allocating {name}: {self.top}+{words}>{self.n}"
        ap = self.big[0:shape[0], self.top:self.top + words]
        self.top += words
        if dt == BF16:
            ap = ap.bitcast(BF16)[:, 0:free]
        elif dt == I32:
            ap = ap.bitcast(I32)[:, 0:free]
        else:
            ap = ap[:, 0:free]
        if len(shape) == 3:
            ap = ap.rearrange("p (a b) -> p a b", b=shape[2])
        elif len(shape) == 4:
            ap = ap.rearrange("p (a b c) -> p a b c", b=shape[2], c=shape[3])
        self.uid += 1
        return T(ap, f"{name}_{self.uid}")


def build_program(npool):
    nc = bass.Bass("TRN2", target_bir_lowering=False)

    def din(name, shape, dt=F32):
        return nc.dram_tensor(name, shape, dt, kind="ExternalInput").ap()

    def dout(name, shape, dt=F32):
        return nc.dram_tensor(name, shape, dt, kind="ExternalOutput").ap()

    def dscr(name, shape, dt=F32):
        return T(nc.dram_tensor(name, shape, dt, kind="Internal").ap(), name)

    xs = din("xs", [NS * 128, D])
    rope = din("rope", [128, NS * 16])
    valid_d = din("valid", [128, NS])
    xsm = din("xsm", [16, D])
    rope_s = din("rope_s", [16, 16])
    csm = din("csm", [16, D])
    cpr = din("cpr", [1, D])
    cache_k = din("cache_k", [npool * 128, 512])
    cache_v = din("cache_v", [npool * 128, 512])
    ptab = din("ptab", [1, 256], I32)
    lane_d = din("lane", [128, 1])
    stconv = din("stconv", [16, 30, 512])
    w_ada = din("w_ada", [D, 6 * D])
    b_ada = din("b_ada", [1, 6 * D])
    w_in = din("w_in", [D, 2560])
    w_out = din("w_out", [D, D])
    w_f1 = din("w_f1", [D, 2 * DFF])
    w_f2 = din("w_f2", [DFF, D])
    g_mix = din("g_mix", [1, D])
    g_ffn = din("g_ffn", [1, D])
    g_fin = din("g_fin", [1, D])
    dw_w = din("dw_w", [31, 512])
    dw_b = din("dw_b", [1, 512])
    cln_g = din("cln_g", [1, 512])
    cln_b = din("cln_b", [1, 512])
    lam4 = din("lam4", [1, 256])
    subg = din("subg", [1, 128])
    sel_d = din("sel", [30, 256])

    y_o = dout("y_o", [NOWN * 128, D])
    k_o = dout("k_o", [NOWN * 128, 512])
    v_o = dout("v_o", [NOWN * 128, 512])
    cp_o = dout("cp_o", [30, 512])
    ys_o = dout("ys_o", [16, D])
    ks_o = dout("ks_o", [16, 512])
    vs_o = dout("vs_o", [16, 512])
    cs_o = dout("cs_o", [16, 30, 512])

    kT_scr = dscr("kT_scr", [4, 128, NS * 128], BF16)
    v_scr = dscr("v_scr", [4, 128, NS * 130], BF16)
    mod_p = dscr("mod_p", [6, 128, D])
    mod_s = dscr("mod_s", [6, 16, D])
    x1_scr = dscr("x1_scr", [17 * 128, D])
    qs_scr = dscr("qs_scr", [16, 512])

    NW = 52992
    with ExitStack() as st:
        S = Sched(nc, st)
        big = st.enter_context(nc.sbuf_tensor("big", [128, NW], F32))
        M = Mem(big, NW)

        def psum_scope():
            return ExitStack()

        def pbank(pst, name, dt=F32):
            n = 512 if dt == F32 else 1024
            return T(pst.enter_context(nc.psum_tensor(name, [128, n], dt))[:], name)

        cast_rr = [0]

        def load_cast(dst_t, dst_ap, src_ap, stg_t, shape3):
            view = stg_t.ap.rearrange("p (a b) -> p a b", b=shape3[2])[:, 0:shape3[1], :]
            S.dma("sp", lambda e: e.dma_start(out=view, in_=src_ap), writes=[stg_t])
            eng = ("act", "dve")[cast_rr[0] % 2]
            cast_rr[0] += 1
            if eng == "act":
                S.op("act", lambda e: e.copy(out=dst_ap, in_=view), reads=[stg_t], writes=[dst_t])
            elif eng == "dve":
                S.op("dve", lambda e: e.tensor_copy(out=dst_ap, in_=view), reads=[stg_t], writes=[dst_t])
            else:
                S.op("pool", lambda e: e.tensor_copy(out=dst_ap, in_=view), reads=[stg_t], writes=[dst_t])

        ident_f = M.alloc("ident_f", [128, 128], F32)
        ident = M.alloc("ident", [128, 128], BF16)
        tri_f = M.alloc("tri_f", [128, 128], F32)
        tri = M.alloc("tri", [128, 128], BF16)
        ones_bf = M.alloc("ones_bf", [128, 128], BF16)
        valid = M.alloc("valid", [128, NS], F32)
        lane = M.alloc("lane", [128, 1], F32)
        mhalf = M.alloc("mhalf", [128, 8], F32)
        neglam = M.alloc("neglam", [128, 1], F32)
        gsub = M.alloc("gsub", [128, 128], F32)
        lam2 = M.alloc("lam2", [128, 2], F32)
        cvec = M.alloc("cvec", [128, 4, 34], F32)

        Mt = Mem(big, NW)
        Mt.top = NW - 1024
        lamt = Mt.alloc("lamt", [128, 256], F32)
        cv34 = Mt.alloc("cv34", [128, 512], F32)
        lprod = Mt.alloc("lprod", [128, 128], F32)
        S.op("pool", lambda e: e.memset(ident_f.ap, 0.0), writes=[ident_f])
        S.op("pool", lambda e: e.affine_select(out=ident_f.ap, in_=ident_f.ap, pattern=[[-1, 128]],
                                               compare_op=ALU.not_equal, fill=1.0, base=0, channel_multiplier=1),
             reads=[ident_f], writes=[ident_f])
        S.op("dve", lambda e: e.tensor_copy(out=ident.ap, in_=ident_f.ap), reads=[ident_f], writes=[ident])
        S.op("pool", lambda e: e.memset(tri_f.ap, 1.0), writes=[tri_f])
        S.op("pool", lambda e: e.affine_select(out=tri_f.ap, in_=tri_f.ap, pattern=[[1, 128]],
                                               compare_op=ALU.is_ge, fill=0.0, base=0, channel_multiplier=-1),
             reads=[tri_f], writes=[tri_f])
        S.op("dve", lambda e: e.tensor_copy(out=tri.ap, in_=tri_f.ap), reads=[tri_f], writes=[tri])
        S.op("pool", lambda e: e.memset(ones_bf.ap, 1.0), writes=[ones_bf])
        S.op("pool", lambda e: e.memset(mhalf.ap, -0.5), writes=[mhalf])
        S.dma("sp", lambda e: e.dma_start(out=valid.ap, in_=valid_d), writes=[valid], group="const")
        S.dma("sp", lambda e: e.dma_start(out=lane.ap, in_=lane_d), writes=[lane], group="const")
        S.dma("sp", lambda e: e.dma_start(out=gsub.ap, in_=subg.to_broadcast([128, 128])), writes=[gsub], group="const")
        S.dma("sp", lambda e: e.dma_start(out=lamt.ap, in_=lam4.to_broadcast([128, 256])), writes=[lamt], group="const")
        S.dma("sp", lambda e: e.dma_start(out=cv34.ap[0:31, :], in_=dw_w), writes=[cv34], group="const")
        S.dma("sp", lambda e: e.dma_start(out=cv34.ap[31:32, :], in_=dw_b), writes=[cv34], group="const")
        S.dma("sp", lambda e: e.dma_start(out=cv34.ap[32:33, :], in_=cln_g), writes=[cv34], group="const")
        S.dma("sp", lambda e: e.dma_start(out=cv34.ap[33:34, :], in_=cln_b), writes=[cv34], group="const")
        with ExitStack() as pst0:
            pc0 = T(pst0.enter_context(nc.psum_tensor("pc0", [128, 512], F32))[:], "pc0")
            for ct in range(4):
                S.op("pe", lambda e, ct=ct: e.transpose(out=pc0.ap[:, ct * 34:(ct + 1) * 34], in_=cv34.ap[0:34, ct * 128:(ct + 1) * 128],
                                                        identity=ident_f.ap[0:34, 0:34]), reads=[cv34, ident_f], writes=[pc0], inc=(ct == 3))
            S.op("act", lambda e: e.copy(out=cvec.ap, in_=pc0.ap[:, 0:136].rearrange("p (c w) -> p c w", w=34)), reads=[pc0], writes=[cvec])
            S.barrier()
        S.op("dve", lambda e: e.tensor_scalar(out=gsub.ap, in0=gsub.ap, scalar1=1.0 - LAM_INIT, scalar2=None, op0=ALU.mult),
             reads=[gsub], writes=[gsub])
        S.op("dve", lambda e: e.tensor_tensor(out=lprod.ap.rearrange("p (a b) -> p a b", b=64),
                                              in0=lamt.ap.rearrange("p (a t b) -> p a t b", t=2, b=64)[:, :, 0, :],
                                              in1=lamt.ap.rearrange("p (a t b) -> p a t b", t=2, b=64)[:, :, 1, :],
                                              op=ALU.mult), reads=[lamt], writes=[lprod])
        S.op("dve", lambda e: e.tensor_reduce(out=lam2.ap, in_=lprod.ap.rearrange("p (a b) -> p a b", b=64),
                                              axis=AX.X, op=ALU.add), reads=[lprod], writes=[lam2])
        S.op("act", lambda e: e.activation(out=lam2.ap, in_=lam2.ap, func=AF.Exp), reads=[lam2], writes=[lam2])
        S.op("dve", lambda e: e.scalar_tensor_tensor(out=neglam.ap, in0=lam2.ap[:, 1:2], scalar=-LAM_INIT, in1=lam2.ap[:, 0:1],
                                                    op0=ALU.add, op1=ALU.subtract), reads=[lam2], writes=[neglam])

        S.barrier()
        QT = M.alloc("QT", [128, 4, 2048], BF16)
        convT = M.alloc("convT", [128, 4, 2048], BF16)
        catS = M.alloc("catS", [128, 8, 16], BF16)
        pre_win_mark = M.mark()
        win = M.alloc("win", [128, 8, 2560], BF16)
        w_in_v = w_in.rearrange("(k p) n -> p k n", p=128)
        base_mark = M.mark()

        with psum_scope() as pst:
            pA = [pbank(pst, f"pA{i}") for i in range(4)]
            ctile = M.alloc("ctile", [128, D], F32)
            cbf = M.alloc("cbf", [128, D], BF16)
            lT = M.alloc("lT", [128, 8, 17], BF16)
            lT_p = M.alloc("lT_p", [128, 8, 128], BF16)
            gm_bc = M.alloc("gm_bc", [128, D], F32)
            gf_bc = M.alloc("gf_bc", [128, D], F32)
            wblk = [M.alloc(f"wblk{i}", [128, 8, 512], BF16) for i in range(2)]
            wst = [M.alloc(f"wst{i}", [128, 8, 512], F32) for i in range(2)]
            bblk = [M.alloc(f"bblk{i}", [128, 512], F32) for i in range(2)]
            tp = [M.alloc(f"tp{i}", [128, 512], F32) for i in range(2)]
            tsm = [M.alloc(f"tsm{i}", [128, 512], F32) for i in range(2)]
            pTa = pbank(pst, "pTa", BF16)
            S.dma("sp", lambda e: e.dma_start(out=ctile.ap[0:1, :], in_=cpr), writes=[ctile], group="phA")
            S.dma("sp", lambda e: e.dma_start(out=ctile.ap[1:17, :], in_=csm), writes=[ctile], group="phA")
            S.dma("sp", lambda e: e.dma_start(out=gm_bc.ap, in_=g_mix.to_broadcast([128, D])), writes=[gm_bc], group="phA")
            S.dma("sp", lambda e: e.dma_start(out=gf_bc.ap, in_=g_ffn.to_broadcast([128, D])), writes=[gf_bc], group="phA")
            S.op("act", lambda e: e.activation(out=cbf.ap[0:17, :], in_=ctile.ap[0:17, :], func=AF.Silu), reads=[ctile], writes=[cbf])
            for kc in range(8):
                S.op("pe", lambda e, kc=kc: e.transpose(out=pTa.ap[:, kc * 32:kc * 32 + 17], in_=cbf.ap[0:17, kc * 128:(kc + 1) * 128],
                                                        identity=ident.ap[0:17, 0:17]), reads=[cbf, ident], writes=[pTa], inc=(kc == 7))
            S.op("act", lambda e: e.copy(out=lT.ap, in_=pTa.ap[:, 0:256].rearrange("p (c t) -> p c t", t=32)[:, :, 0:17]),
                 reads=[pTa], writes=[lT])
            S.op("dve", lambda e: e.tensor_copy(out=lT_p.ap, in_=lT.ap[:, :, 0:1].to_broadcast([128, 8, 128])),
                 reads=[lT], writes=[lT_p])
            w_ada_v = w_ada.rearrange("(k p) n -> p k n", p=128)
            def a_load_w(nb):
                c0 = nb * 512
                ws = wst[nb % 2]
                S.dma("sp", lambda e: e.dma_start(out=ws.ap, in_=w_ada_v[:, :, c0:c0 + 512]), writes=[ws])

            def a_load_b(nb):
                c0 = nb * 512
                bb = bblk[nb % 2]
                S.dma("sp", lambda e: e.dma_start(out=bb.ap, in_=b_ada[0:1, c0:c0 + 512].to_broadcast([128, 512])), writes=[bb])

            for nb0 in range(2):
                a_load_w(nb0)
                a_load_b(nb0)
            out_dmas = []
            for nb in range(12):
                wb = wblk[nb % 2]
                bb = bblk[nb % 2]
                ws = wst[nb % 2]
                if nb % 2 == 0:
                    S.op("act", lambda e, ws=ws, wb=wb: e.copy(out=wb.ap, in_=ws.ap), reads=[ws], writes=[wb])
                else:
                    S.op("dve", lambda e, ws=ws, wb=wb: e.tensor_copy(out=wb.ap, in_=ws.ap), reads=[ws], writes=[wb])
                if nb + 2 < 12:
                    a_load_w(nb + 2)
                pp = pA[(nb % 2) * 2]
                ps_ = pA[(nb % 2) * 2 + 1]
                for kc in range(8):
                    S.op("pe", lambda e, pp=pp, wb=wb, kc=kc: e.matmul(pp.ap, lhsT=lT_p.ap[:, kc, :], rhs=wb.ap[:, kc, :],
                                                                     start=(kc == 0), stop=(kc == 7)),
                         reads=[lT_p, wb], writes=[pp], inc=(kc == 7))
                for kc in range(8):
                    S.op("pe", lambda e, ps_=ps_, wb=wb, kc=kc: e.matmul(ps_.ap[0:16, :], lhsT=lT.ap[:, kc, 1:17], rhs=wb.ap[:, kc, :],
                                                                       start=(kc == 0), stop=(kc == 7)),
                         reads=[lT, wb], writes=[ps_], inc=(kc == 7))
                mi = nb // 2
                h0 = (nb % 2) * 512
                for (pz, stg, np_, scr) in ((pp, tp[nb % 2], 128, mod_p), (ps_, tsm[nb % 2], 16, mod_s)):
                    if mi in (1, 4):
                        gb = gm_bc if mi == 1 else gf_bc
                        S.op("dve", lambda e, pz=pz, stg=stg, bb=bb, np_=np_: e.scalar_tensor_tensor(
                            out=stg.ap[0:np_, :], in0=pz.ap[0:np_, :], scalar=1.0, in1=bb.ap[0:np_, :],
                            op0=ALU.add, op1=ALU.add), reads=[pz, bb], writes=[stg])
                        S.op("pool", lambda e, stg=stg, gb=gb, np_=np_, h0=h0: e.tensor_tensor(
                            out=stg.ap[0:np_, :], in0=stg.ap[0:np_, :], in1=gb.ap[0:np_, h0:h0 + 512], op=ALU.mult),
                            reads=[stg, gb], writes=[stg])
                    else:
                        S.op("dve", lambda e, pz=pz, stg=stg, bb=bb, np_=np_: e.tensor_tensor(
                            out=stg.ap[0:np_, :], in0=pz.ap[0:np_, :], in1=bb.ap[0:np_, :], op=ALU.add),
                            reads=[pz, bb], writes=[stg])
                    S.dma("sp", lambda e, stg=stg, np_=np_, scr=scr, mi=mi, h0=h0: e.dma_start(
                        out=scr.ap[mi, 0:np_, h0:h0 + 512], in_=stg.ap[0:np_, :]), reads=[stg])
                if nb + 2 < 12:
                    a_load_b(nb + 2)
            wstf = [T(w_.ap.rearrange("p a b -> p (a b)"), f"wstf{i_}") for i_, w_ in enumerate(wst)]
            for j in range(5):
                wstf[j % 2].b = wst[j % 2].b
                load_cast(win, win.ap[:, :, j * 512:(j + 1) * 512], w_in_v[:, :, j * 512:(j + 1) * 512], wstf[j % 2], (128, 8, 512))
            S.barrier()
        M.release(base_mark)

        own = {}
        halo = {}
        for m in range(4):
            halo[11 + 16 * m] = m
            for t in range(4):
                own[12 + 16 * m + t] = (m, t)

        with psum_scope() as pst:
            pT = [pbank(pst, "pT0", BF16)]
            pX2 = pbank(pst, "pX2", BF16)
            pK = pbank(pst, "pK")
            pV = pbank(pst, "pV")
            pQ = pbank(pst, "pQ")
            pCa = pbank(pst, "pCa")
            pCg = pbank(pst, "pCg")
            pX = pbank(pst, "pX", BF16)

            Am_p = M.alloc("Am_p", [128, D], F32)
            shm_p = M.alloc("shm_p", [128, D], F32)
            diag = M.alloc("diag", [128, 4, 31, 128], BF16)
            xb = [M.alloc(f"xb{i}", [128, D], F32) for i in range(2)]
            rope_all = M.alloc("rope_all", [128, NS, 16], F32)
            rp_s = M.alloc("rp_s", [128, 16], F32)
            ssq = [M.alloc(f"ssq{i}", [128, 1], F32) for i in range(2)]
            rstd = [M.alloc(f"rstd{i}", [128, 1], F32) for i in range(2)]
            hpre = M.alloc("hpre", [128, D], F32)
            hb = [M.alloc(f"hb{i}", [128, D], BF16) for i in range(2)]
            hT = [M.alloc(f"hT{i}", [128, 8, 128], BF16) for i in range(2)]
            k32 = [M.alloc(f"k32{i}", [128, 512], F32) for i in range(2)]
            v32 = [M.alloc(f"v32{i}", [128, 512], F32) for i in range(2)]
            q32 = M.alloc("q32", [128, 512], F32)
            sg32 = M.alloc("sg32", [128, 512], F32)
            u32 = [M.alloc(f"u32{i}", [128, 512], F32) for i in range(2)]
            rtmp = M.alloc("rtmp", [128, 4, 8, 8], F32)
            kb2 = [M.alloc(f"kb{i}", [128, 512], BF16) for i in range(2)]
            qb2 = [M.alloc(f"qb{i}", [128, 512], BF16) for i in range(2)]
            ub2 = [M.alloc(f"ub{i}", [128, 512], BF16) for i in range(2)]
            ub = ub2[0]
            bstat = [M.alloc(f"bstat{i}", [128, 12], F32) for i in range(2)]
            mv = [M.alloc(f"mv{i}", [128, 2], F32) for i in range(2)]
            kst = [M.alloc(f"kst{i}", [128, 4, 512], BF16) for i in range(2)]
            vst = [M.alloc(f"vst{i}", [128, 4, 4 * 130], BF16) for i in range(2)]
            uT = M.alloc("uT", [128, 4, 640], BF16)
            conv_mark = M.mark()
            ycv = M.alloc("ycv", [128, 4, 512], F32)
            ycb = M.alloc("ycb", [128, 4, 512], BF16)
            y2b = M.alloc("y2b", [128, 4, 512], BF16)
            mean = M.alloc("mean", [128, 512], F32)
            var = M.alloc("var", [128, 512], F32)
            ztmp = M.alloc("ztmp", [128, 512], F32)
            ztmp2 = M.alloc("ztmp2", [128, 512], F32)

            S.dma("sp", lambda e: e.dma_start(out=rope_all.ap, in_=rope.rearrange("p (s f) -> p s f", f=16)), writes=[rope_all])
            S.dma("sp", lambda e: e.dma_start(out=Am_p.ap, in_=mod_p.ap[1]), writes=[Am_p])
            S.dma("sp", lambda e: e.dma_start(out=shm_p.ap, in_=mod_p.ap[0]), writes=[shm_p])
            for ct in range(4):
                for w in range(31):
                    eng = "pool" if (w % 2) else "dve"
                    S.op(eng, lambda e, ct=ct, w=w: e.tensor_scalar(out=diag.ap[:, ct, w, :], in0=ident_f.ap,
                                                                     scalar1=cvec.ap[:, ct, w:w + 1], scalar2=None,
                                                                     op0=ALU.mult),
                         reads=[ident_f, cvec], writes=[diag])

            def rmsnorm_h(np_, x_t, A_t, sh_t, par):
                sq, rs, hbt = ssq[par], rstd[par], hb[par]
                S.op("act", lambda e: e.activation(out=hpre.ap[0:np_, :], in_=x_t.ap[0:np_, :], func=AF.Square,
                                                   accum_out=sq.ap[0:np_, :]), reads=[x_t], writes=[hpre, sq])
                S.op("dve", lambda e: e.tensor_scalar(out=sq.ap[0:np_, :], in0=sq.ap[0:np_, :], scalar1=1.0 / D, scalar2=EPS,
                                                      op0=ALU.mult, op1=ALU.add), reads=[sq], writes=[sq])
                S.op("act", lambda e: e.activation(out=sq.ap[0:np_, :], in_=sq.ap[0:np_, :], func=AF.Sqrt), reads=[sq], writes=[sq])
                S.op("dve", lambda e: e.reciprocal(out=rs.ap[0:np_, :], in_=sq.ap[0:np_, :]), reads=[sq], writes=[rs])
                S.op("dve", lambda e: e.scalar_tensor_tensor(out=hpre.ap[0:np_, :], in0=x_t.ap[0:np_, :], scalar=rs.ap[0:np_, 0:1],
                                                            in1=A_t.ap[0:np_, :], op0=ALU.mult, op1=ALU.mult),
                     reads=[x_t, rs, A_t], writes=[hpre])
                S.op("pool", lambda e: e.tensor_tensor(out=hbt.ap[0:np_, :], in0=hpre.ap[0:np_, :], in1=sh_t.ap[0:np_, :],
                                                       op=ALU.add), reads=[hpre, sh_t], writes=[hbt])
                return hbt

            def transpose8(np_, src_bf, dst_t, dst_ap, pbankT, nchunk=8, copy_eng="act"):
                for kc in range(nchunk):
                    S.op("pe", lambda e, kc=kc: e.transpose(out=pbankT.ap[:, kc * 128:kc * 128 + np_],
                                                            in_=src_bf.ap[0:np_, kc * 128:(kc + 1) * 128],
                                                            identity=ident.ap[0:np_, 0:np_]),
                         reads=[src_bf, ident], writes=[pbankT], inc=(kc == nchunk - 1))
                src = pbankT.ap[:, 0:nchunk * 128].rearrange("p (c t) -> p c t", t=128)[:, :, 0:np_]
                if copy_eng == "act":
                    S.op("act", lambda e: e.copy(out=dst_ap, in_=src), reads=[pbankT], writes=[dst_t])
                else:
                    S.op("dve", lambda e: e.tensor_copy(out=dst_ap, in_=src), reads=[pbankT], writes=[dst_t])

            def proj(np_, hTt, j, pb):
                for kc in range(8):
                    S.op("pe", lambda e, kc=kc: e.matmul(pb.ap[0:np_, :], lhsT=hTt.ap[:, kc, 0:np_],
                                                         rhs=win.ap[:, kc, j * 512:(j + 1) * 512],
                                                         start=(kc == 0), stop=(kc == 7)),
                         reads=[hTt, win], writes=[pb], inc=(kc == 7))

            def rope_inplace(np_, t32, rp_t, rp_ap):
                v = t32.ap[0:np_, :].rearrange("p (g d) -> p g d", d=64)
                x1 = v[:, :, 0:8]
                x2 = v[:, :, 8:16]
                cos = rp_ap[0:np_, 0:8].unsqueeze(1).to_broadcast([np_, 8, 8])
                sin = rp_ap[0:np_, 8:16].unsqueeze(1).to_broadcast([np_, 8, 8])
                tm = rtmp.ap[0:np_]
                S.op("dve", lambda e: e.tensor_tensor(out=tm[:, 0], in0=x1, in1=cos, op=ALU.mult), reads=[t32, rp_t], writes=[rtmp])
                S.op("dve", lambda e: e.tensor_tensor(out=tm[:, 1], in0=x2, in1=sin, op=ALU.mult), reads=[t32, rp_t], writes=[rtmp])
                S.op("dve", lambda e: e.tensor_tensor(out=tm[:, 2], in0=x2, in1=cos, op=ALU.mult), reads=[t32, rp_t], writes=[rtmp])
                S.op("dve", lambda e: e.tensor_tensor(out=tm[:, 3], in0=x1, in1=sin, op=ALU.mult), reads=[t32, rp_t], writes=[rtmp])
                S.op("dve", lambda e: e.tensor_tensor(out=x1, in0=tm[:, 0], in1=tm[:, 1], op=ALU.subtract), reads=[rtmp], writes=[t32])
                S.op("dve", lambda e: e.tensor_tensor(out=x2, in0=tm[:, 2], in1=tm[:, 3], op=ALU.add), reads=[rtmp], writes=[t32])

            def st0(s):
                xt = xb[s % 2]
                S.dma("sp", lambda e: e.dma_start(out=xt.ap, in_=xs[s * 128:(s + 1) * 128, :]), writes=[xt])

            def st1(s):
                xt, bs, mvt, sq = xb[s % 2], bstat[s % 2], mv[s % 2], ssq[s % 2]
                S.op("dve", lambda e: e.bn_stats(out=bs.ap[:, 0:6], in_=xt.ap[:, 0:512]), reads=[xt], writes=[bs])
                S.op("dve", lambda e: e.bn_stats(out=bs.ap[:, 6:12], in_=xt.ap[:, 512:1024]), reads=[xt], writes=[bs])
                S.op("dve", lambda e: e.bn_aggr(out=mvt.ap, in_=bs.ap), reads=[bs], writes=[mvt])
                S.op("dve", lambda e: e.scalar_tensor_tensor(out=sq.ap, in0=mvt.ap[:, 0:1], scalar=mvt.ap[:, 0:1], in1=mvt.ap[:, 1:2],
                                                            op0=ALU.mult, op1=ALU.add), reads=[mvt], writes=[sq])
                S.op("dve", lambda e: e.tensor_scalar(out=sq.ap, in0=sq.ap, scalar1=EPS, scalar2=None, op0=ALU.add),
                     reads=[sq], writes=[sq])

            def st2(s):
                xt, sq, rs, hbt = xb[s % 2], ssq[s % 2], rstd[s % 2], hb[s % 2]
                S.op("act", lambda e: e.activation(out=sq.ap, in_=sq.ap, func=AF.Sqrt), reads=[sq], writes=[sq])
                S.op("dve", lambda e: e.reciprocal(out=rs.ap, in_=sq.ap), reads=[sq], writes=[rs])
                S.op("dve", lambda e: e.scalar_tensor_tensor(out=hpre.ap, in0=xt.ap, scalar=rs.ap[:, 0:1], in1=Am_p.ap,
                                                            op0=ALU.mult, op1=ALU.mult), reads=[xt, rs, Am_p], writes=[hpre])
                S.op("pool", lambda e: e.tensor_tensor(out=hbt.ap, in0=hpre.ap, in1=shm_p.ap, op=ALU.add),
                     reads=[hpre, shm_p], writes=[hbt])

            def st3(s):
                transpose8(128, hb[s % 2], hT[s % 2], hT[s % 2].ap, pT[0])

            def st4(s):
                hTt, k3, v3 = hT[s % 2], k32[s % 2], v32[s % 2]
                proj(128, hTt, 3, pK)
                S.op("act", lambda e: e.copy(out=k3.ap, in_=pK.ap), reads=[pK], writes=[k3])
                proj(128, hTt, 4, pV)
                g4, grp = s % 4, s // 4
                vstg = vst[grp % 2]
                vdst = vstg.ap[:, :, g4 * 130:g4 * 130 + 129]
                S.op("act", lambda e: e.activation(out=vdst[:, :, 0:128], in_=pV.ap.rearrange("p (h d) -> p h d", d=128),
                                                   func=AF.Identity, scale=valid.ap[:, s:s + 1]), reads=[pV, valid], writes=[vstg])
                S.op("pool", lambda e: e.tensor_copy(
                    out=vdst[:, :, 128:129], in_=valid.ap[:, s:s + 1].unsqueeze(1).to_broadcast([128, 4, 1])),
                    reads=[valid], writes=[vstg])
                if s in own:
                    S.op("act", lambda e: e.copy(out=v3.ap, in_=pV.ap), reads=[pV], writes=[v3])
                if s in own:
                    proj(128, hTt, 2, pQ)
                    S.op("act", lambda e: e.copy(out=q32.ap, in_=pQ.ap), reads=[pQ], writes=[q32])
                if s in own or s in halo:
                    proj(128, hTt, 0, pCa)
                    proj(128, hTt, 1, pCg)
                    S.op("act", lambda e: e.activation(out=sg32.ap, in_=pCg.ap, func=AF.Sigmoid), reads=[pCg], writes=[sg32])

            def st5(s):
                k3, v3, kbt = k32[s % 2], v32[s % 2], kb2[s % 2]
                rpa = rope_all.ap[:, s, :]
                g4, grp = s % 4, s // 4
                vstg = vst[grp % 2]
                rope_inplace(128, k3, rope_all, rpa)
                S.op("dve", lambda e: e.tensor_copy(out=kbt.ap, in_=k3.ap), reads=[k3], writes=[kbt])
                if s in own:
                    i_own = own[s][0] * 4 + own[s][1]
                    S.dma("sp", lambda e: e.dma_start(out=k_o[i_own * 128:(i_own + 1) * 128, :], in_=k3.ap),
                          reads=[k3], semkey="D_ko", is_out=True)
                    S.dma("sp", lambda e: e.dma_start(out=v_o[i_own * 128:(i_own + 1) * 128, :], in_=v3.ap),
                          reads=[v3], semkey="D_vo", is_out=True)
                    qbt = qb2[s % 2]
                    rope_inplace(128, q32, rope_all, rpa)
                    S.op("dve", lambda e: e.tensor_copy(out=qbt.ap, in_=q32.ap), reads=[q32], writes=[qbt])
                if s in own or s in halo:
                    ut, ubt = u32[s % 2], ub2[s % 2]
                    S.op("dve", lambda e: e.scalar_tensor_tensor(out=ut.ap, in0=pCa.ap, scalar=valid.ap[:, s:s + 1], in1=sg32.ap,
                                                                op0=ALU.mult, op1=ALU.mult), reads=[pCa, sg32, valid], writes=[ut])
                    S.op("dve", lambda e: e.tensor_copy(out=ubt.ap, in_=ut.ap), reads=[ut], writes=[ubt])
                    if s in own and own[s] == (3, 3):
                        S.dma("sp", lambda e: e.dma_start(out=cp_o, in_=ut.ap[98:128, :]), reads=[ut], semkey="D_cpo", is_out=True)

            def st6(s):
                kbt = kb2[s % 2]
                g4, grp = s % 4, s // 4
                kstg = kst[grp % 2]
                for h in range(4):
                    S.op("pe", lambda e, h=h: e.transpose(out=pX.ap[:, h * 128:(h + 1) * 128], in_=kbt.ap[:, h * 128:(h + 1) * 128],
                                                          identity=ident.ap), reads=[kbt, ident], writes=[pX], inc=(h == 3))
                S.op("act", lambda e: e.copy(out=kstg.ap[:, :, g4 * 128:(g4 + 1) * 128],
                                             in_=pX.ap[:, 0:512].rearrange("p (h t) -> p h t", t=128)), reads=[pX], writes=[kstg])
                if s in own:
                    m, t = own[s]
                    qbt = qb2[s % 2]
                    for h in range(4):
                        S.op("pe", lambda e, h=h: e.transpose(out=pX2.ap[:, h * 128:(h + 1) * 128], in_=qbt.ap[:, h * 128:(h + 1) * 128],
                                                              identity=ident.ap), reads=[qbt, ident], writes=[pX2], inc=(h == 3))
                    tok0 = (m * 4 + t) * 128
                    S.op("act", lambda e: e.copy(out=QT.ap[:, :, tok0:tok0 + 128],
                                                 in_=pX2.ap[:, 0:512].rearrange("p (h t) -> p h t", t=128)), reads=[pX2], writes=[QT])

            def st7(s):
                g4, grp = s % 4, s // 4
                kstg, vstg = kst[grp % 2], vst[grp % 2]
                if g4 == 3:
                    S.dma("sp", lambda e: e.dma_start(
                        out=kT_scr.ap[:, :, grp * 512:(grp + 1) * 512].rearrange("h p t -> p h t"), in_=kstg.ap),
                        reads=[kstg], semkey=f"D_kst{grp % 2}")
                    S.dma("sp", lambda e: e.dma_start(
                        out=v_scr.ap[:, :, grp * 520:(grp + 1) * 520].rearrange("h p t -> p h t"), in_=vstg.ap),
                        reads=[vstg], semkey=f"D_vst{grp % 2}")
                if s in own or s in halo:
                    ubt = ub2[s % 2]
                    for ct in range(4):
                        S.op("pe", lambda e, ct=ct: e.transpose(out=pX.ap[:, ct * 128:(ct + 1) * 128],
                                                                in_=ubt.ap[:, ct * 128:(ct + 1) * 128], identity=ident.ap),
                             reads=[ubt, ident], writes=[pX], inc=(ct == 3))
                    pos = 0 if s in halo else (1 + own[s][1])
                    S.op("act", lambda e: e.copy(out=uT.ap[:, :, pos * 128:(pos + 1) * 128],
                                                 in_=pX.ap[:, 0:512].rearrange("p (c t) -> p c t", t=128)), reads=[pX], writes=[uT])

            def st8(s):
                if not (s in own and own[s][1] == 3):
                    return
                m = own[s][0]
                cbanks = [pK, pV, pQ, pCa]
                for ct in range(4):
                    pb = cbanks[ct]
                    for w in range(31):
                        S.op("pe", lambda e, ct=ct, w=w, pb=pb: e.matmul(pb.ap, lhsT=diag.ap[:, ct, w, :],
                                                                       rhs=uT.ap[:, ct, 98 + w:98 + w + 512],
                                                                       start=(w == 0), stop=(w == 30)),
                             reads=[diag, uT], writes=[pb], inc=(w == 30))
                    S.op("act", lambda e, ct=ct, pb=pb: e.activation(out=ycv.ap[:, ct, :], in_=pb.ap, func=AF.Identity,
                                                                   bias=cvec.ap[:, ct, 31:32]), reads=[pb, cvec], writes=[ycv])
                    S.op("dve", lambda e, ct=ct: e.tensor_copy(out=ycb.ap[:, ct, :], in_=ycv.ap[:, ct, :]), reads=[ycv], writes=[ycb])
                    S.op("pool", lambda e, ct=ct: e.tensor_tensor(out=y2b.ap[:, ct, :], in0=ycv.ap[:, ct, :], in1=ycv.ap[:, ct, :],
                                                                  op=ALU.mult), reads=[ycv], writes=[y2b])
                for ct in range(4):
                    S.op("pe", lambda e, ct=ct: e.matmul(pCg.ap, lhsT=ones_bf.ap, rhs=ycb.ap[:, ct, :],
                                                         start=(ct == 0), stop=(ct == 3)), reads=[ones_bf, ycb], writes=[pCg],
                         inc=(ct == 3))
                for ct in range(4):
                    S.op("pe", lambda e, ct=ct: e.matmul(pK.ap, lhsT=ones_bf.ap, rhs=y2b.ap[:, ct, :],
                                                         start=(ct == 0), stop=(ct == 3)), reads=[ones_bf, y2b], writes=[pK],
                         inc=(ct == 3))
                S.op("dve", lambda e: e.tensor_scalar(out=mean.ap, in0=pCg.ap, scalar1=1.0 / 512, scalar2=None, op0=ALU.mult),
                     reads=[pCg], writes=[mean])
                S.op("dve", lambda e: e.tensor_tensor(out=ztmp.ap, in0=mean.ap, in1=mean.ap, op=ALU.mult), reads=[mean], writes=[ztmp])
                S.op("dve", lambda e: e.scalar_tensor_tensor(out=var.ap, in0=pK.ap, scalar=1.0 / 512, in1=ztmp.ap,
                                                            op0=ALU.mult, op1=ALU.subtract), reads=[pK, ztmp], writes=[var])
                S.op("dve", lambda e: e.tensor_scalar(out=var.ap, in0=var.ap, scalar1=EPS, scalar2=None, op0=ALU.add),
                     reads=[var], writes=[var])
                S.op("act", lambda e: e.activation(out=var.ap, in_=var.ap, func=AF.Sqrt), reads=[var], writes=[var])
                S.op("dve", lambda e: e.reciprocal(out=var.ap, in_=var.ap), reads=[var], writes=[var])
                for ct in range(4):
                    zt = ztmp if ct % 2 == 0 else ztmp2
                    S.op("dve", lambda e, ct=ct, zt=zt: e.tensor_tensor(out=zt.ap, in0=ycv.ap[:, ct, :], in1=mean.ap, op=ALU.subtract),
                         reads=[ycv, mean], writes=[zt])
                    S.op("pool", lambda e, zt=zt: e.tensor_tensor(out=zt.ap, in0=zt.ap, in1=var.ap, op=ALU.mult),
                         reads=[zt, var], writes=[zt])
                    S.op("act", lambda e, ct=ct, m=m, zt=zt: e.activation(out=convT.ap[:, ct, m * 512:(m + 1) * 512], in_=zt.ap,
                                                                        func=AF.Silu, scale=cvec.ap[:, ct, 32:33],
                                                                        bias=cvec.ap[:, ct, 33:34]),
                         reads=[zt, cvec], writes=[convT])

            stages = [st0, st1, st2, st3, st4, st5, st6, st7, st8]
            for it in range(NS + len(stages) - 1):
                for k in range(len(stages) - 1, -1, -1):
                    sl = it - k
                    if 0 <= sl < NS:
                        stages[k](sl)

            SP = 16
            xsb = xb[0]
            rps = rp_s
            S.dma("sp", lambda e: e.dma_start(out=xsb.ap[0:SP, :], in_=xsm), writes=[xsb])
            S.dma("sp", lambda e: e.dma_start(out=rps.ap[0:SP, :], in_=rope_s), writes=[rps])
            S.dma("sp", lambda e: e.dma_start(out=Am_p.ap[0:16, :], in_=mod_s.ap[1]), writes=[Am_p])
            S.dma("sp", lambda e: e.dma_start(out=shm_p.ap[0:16, :], in_=mod_s.ap[0]), writes=[shm_p])
            hbs = rmsnorm_h(SP, xsb, Am_p, shm_p, 0)
            hTs = hT[0]
            transpose8(SP, hbs, hTs, hTs.ap[:, :, 0:SP], pT[0])
            ks3, vs3 = k32[0], v32[0]
            proj(SP, hTs, 3, pK)
            S.op("act", lambda e: e.copy(out=ks3.ap[0:SP, :], in_=pK.ap[0:SP, :]), reads=[pK], writes=[ks3])
            rope_inplace(SP, ks3, rps, rps.ap)
            S.dma("sp", lambda e: e.dma_start(out=ks_o, in_=ks3.ap[0:SP, :]), reads=[ks3], semkey="D_kso", is_out=True)
            proj(SP, hTs, 4, pV)
            S.op("act", lambda e: e.copy(out=vs3.ap[0:SP, :], in_=pV.ap[0:SP, :]), reads=[pV], writes=[vs3])
            S.dma("sp", lambda e: e.dma_start(out=vs_o, in_=vs3.ap[0:SP, :]), reads=[vs3], semkey="D_vso", is_out=True)
            proj(SP, hTs, 2, pQ)
            S.op("act", lambda e: e.copy(out=q32.ap[0:SP, :], in_=pQ.ap[0:SP, :]), reads=[pQ], writes=[q32])
            rope_inplace(SP, q32, rps, rps.ap)
            S.dma("sp", lambda e: e.dma_start(out=qs_scr.ap, in_=q32.ap[0:SP, :]), reads=[q32])
            us3 = u32[0]
            proj(SP, hTs, 0, pCa)
            proj(SP, hTs, 1, pCg)
            S.op("act", lambda e: e.activation(out=sg32.ap[0:SP, :], in_=pCg.ap[0:SP, :], func=AF.Sigmoid), reads=[pCg], writes=[sg32])
            S.op("dve", lambda e: e.tensor_tensor(out=us3.ap[0:SP, :], in0=pCa.ap[0:SP, :], in1=sg32.ap[0:SP, :], op=ALU.mult),
                 reads=[pCa, sg32], writes=[us3])
            S.barrier()
            M.release(conv_mark)
            S.dma("sp", lambda e: e.dma_start(out=cs_o[:, 29, :], in_=us3.ap[0:SP, :]), reads=[us3], semkey="D_cso2", is_out=True)
            stc = [M.alloc("stc0", [128, 4, 512], F32)] * 2
            dwt = M.alloc("dwt", [128, 512], F32)
            selt = M.alloc("selt", [128, 16, 16], F32)
            dw30 = M.alloc("dw30", [128, 512], F32)
            dwb_bc = M.alloc("dwb_bc", [128, 512], F32)
            clg_bc = M.alloc("clg_bc", [128, 512], F32)
            clb_bc = M.alloc("clb_bc", [128, 512], F32)
            S.dma("sp", lambda e: e.dma_start(out=dwt.ap[0:30, :], in_=dw_w[0:30, :]), writes=[dwt], group="sconv")
            S.dma("sp", lambda e: e.dma_start(out=selt.ap[0:30], in_=sel_d.rearrange("w (a b) -> w a b", b=16)), writes=[selt], group="sconv")
            S.dma("sp", lambda e: e.dma_start(out=dw30.ap[0:SP, :], in_=dw_w[30:31, :].to_broadcast([SP, 512])), writes=[dw30], group="sconv")
            S.dma("sp", lambda e: e.dma_start(out=dwb_bc.ap[0:SP, :], in_=dw_b.to_broadcast([SP, 512])), writes=[dwb_bc], group="sconv")
            S.dma("sp", lambda e: e.dma_start(out=clg_bc.ap[0:SP, :], in_=cln_g.to_broadcast([SP, 512])), writes=[clg_bc], group="sconv")
            S.dma("sp", lambda e: e.dma_start(out=clb_bc.ap[0:SP, :], in_=cln_b.to_broadcast([SP, 512])), writes=[clb_bc], group="sconv")
            stv = stconv.rearrange("s w c -> w s c")
            for q4 in range(4):
                stq = stc[q4 % 2]
                S.dma("sp", lambda e, stq=stq, q4=q4: e.dma_start(out=stq.ap[0:30], in_=stv[:, q4 * 4:(q4 + 1) * 4, :]), writes=[stq])
                S.dma("sp", lambda e, stq=stq, q4=q4: e.dma_start(out=cs_o[q4 * 4:(q4 + 1) * 4, 0:29, :].rearrange("s w c -> w s c"),
                                                                  in_=stq.ap[1:30]), reads=[stq], semkey="D_cso1", is_out=True)
                S.op("dve", lambda e, stq=stq: e.tensor_tensor(out=stq.ap[0:30], in0=stq.ap[0:30],
                                                               in1=dwt.ap[0:30, :].unsqueeze(1).to_broadcast([30, 4, 512]), op=ALU.mult),
                     reads=[stq, dwt], writes=[stq])
                for s4 in range(4):
                    sm = q4 * 4 + s4
                    S.op("pe", lambda e, sm=sm, s4=s4, stq=stq: e.matmul(pK.ap[0:SP, :], lhsT=selt.ap[0:30, sm, :], rhs=stq.ap[0:30, s4, :],
                                                                       start=(sm == 0), stop=(sm == 15)),
                         reads=[selt, stq], writes=[pK], inc=(s4 == 3))
            cvs = M.alloc("cvs", [128, 512], F32)
            S.op("dve", lambda e: e.tensor_tensor(out=cvs.ap[0:SP, :], in0=us3.ap[0:SP, :], in1=dw30.ap[0:SP, :], op=ALU.mult),
                 reads=[us3, dw30], writes=[cvs])
            S.op("dve", lambda e: e.tensor_tensor(out=cvs.ap[0:SP, :], in0=cvs.ap[0:SP, :], in1=pK.ap[0:SP, :], op=ALU.add),
                 reads=[cvs, pK], writes=[cvs])
            S.op("dve", lambda e: e.tensor_tensor(out=cvs.ap[0:SP, :], in0=cvs.ap[0:SP, :], in1=dwb_bc.ap[0:SP, :], op=ALU.add),
                 reads=[cvs, dwb_bc], writes=[cvs])
            bst = M.alloc("bst", [128, 6], F32)
            bag = M.alloc("bag", [128, 2], F32)
            S.op("dve", lambda e: e.bn_stats(out=bst.ap[0:SP, :], in_=cvs.ap[0:SP, :]), reads=[cvs], writes=[bst])
            S.op("dve", lambda e: e.bn_aggr(out=bag.ap[0:SP, :], in_=bst.ap[0:SP, :]), reads=[bst], writes=[bag])
            S.op("dve", lambda e: e.tensor_scalar(out=bag.ap[0:SP, 1:2], in0=bag.ap[0:SP, 1:2], scalar1=EPS, scalar2=None, op0=ALU.add),
                 reads=[bag], writes=[bag])
            S.op("act", lambda e: e.activation(out=bag.ap[0:SP, 1:2], in_=bag.ap[0:SP, 1:2], func=AF.Sqrt), reads=[bag], writes=[bag])
            S.op("dve", lambda e: e.reciprocal(out=bag.ap[0:SP, 1:2], in_=bag.ap[0:SP, 1:2]), reads=[bag], writes=[bag])
            S.op("dve", lambda e: e.tensor_scalar(out=cvs.ap[0:SP, :], in0=cvs.ap[0:SP, :], scalar1=bag.ap[0:SP, 0:1],
                                                  scalar2=bag.ap[0:SP, 1:2], op0=ALU.subtract, op1=ALU.mult), reads=[cvs, bag], writes=[cvs])
            S.op("dve", lambda e: e.tensor_tensor(out=cvs.ap[0:SP, :], in0=cvs.ap[0:SP, :], in1=clg_bc.ap[0:SP, :], op=ALU.mult),
                 reads=[cvs, clg_bc], writes=[cvs])
            S.op("dve", lambda e: e.tensor_tensor(out=cvs.ap[0:SP, :], in0=cvs.ap[0:SP, :], in1=clb_bc.ap[0:SP, :], op=ALU.add),
                 reads=[cvs, clb_bc], writes=[cvs])
            S.op("act", lambda e: e.activation(out=ub.ap[0:SP, :], in_=cvs.ap[0:SP, :], func=AF.Silu), reads=[cvs], writes=[ub])
            transpose8(SP, ub, catS, catS.ap[:, 0:4, :], pX, nchunk=4)
            S.barrier()
        M.release(pre_win_mark)
        oT = M.alloc("oT", [128, 4, 2048], BF16)
        base_mark = M.mark()

        with psum_scope() as pst:
            pS = [pbank(pst, f"pS{i}") for i in range(3)]
            pOA = pbank(pst, "pOA")
            pOB = pbank(pst, "pOB")
            pXc = pbank(pst, "pXc", BF16)
            pSA = pbank(pst, "pSA")
            pSB = pbank(pst, "pSB")

            KTb = [M.alloc(f"KTb{i}", [128, NS * 128], BF16) for i in range(2)]
            Vb = [M.alloc(f"Vb{i}", [128, NS, 130], BF16) for i in range(2)]
            pb_ = [M.alloc(f"pb{i}", [128, 512], BF16) for i in range(3)]
            oacc = M.alloc("oacc", [128, 4, 129], F32)
            o1n = M.alloc("o1n", [128, 4, 128], F32)
            rden = M.alloc("rden", [128, 4], F32)
            ssn = M.alloc("ssn", [128, 4], F32)
            ojunk = M.alloc("ojunk", [128, 128], F32)
            obf = M.alloc("obf", [128, 4, 128], BF16)

            SP = 16
            qs32 = M.alloc("qs32", [128, 512], F32)
            ksm32 = M.alloc("ksm32", [128, 512], F32)
            vsm32 = M.alloc("vsm32", [128, 512], F32)
            vsa = M.alloc("vsa", [128, 4, 129], BF16)
            pself = M.alloc("pself", [128, 8], F32)
            lself = M.alloc("lself", [128, 4, 64], BF16)
            ptf = M.alloc("ptf", [128, 256], F32)
            pti = M.alloc("pti", [128, 256], I32)
            idx = M.alloc("idx", [128, 256], I32)
            qbc = [M.alloc(f"qbc{i}", [128, 512], F32) for i in range(2)]
            NG = 4
            kpg = [M.alloc(f"kpg{i}", [128, NG, 512], BF16) for i in range(4)]
            vpg = [M.alloc(f"vpg{i}", [128, NG, 512], BF16) for i in range(4)]
            prod = M.alloc("prod", [128, NG, 512], F32)
            scs = [M.alloc(f"scs{i}", [128, 16, 8], F32) for i in range(2)]
            Pz = [M.alloc(f"Pz{i}", [128, 16, 4, 64], BF16) for i in range(2)]
            osm = M.alloc("osm", [128, 4, 129], F32)
            osm2 = M.alloc("osm2", [128, 4, 129], F32)
            osn = M.alloc("osn", [128, 4, 128], F32)

            S.dma("sp", lambda e: e.dma_start(out=qs32.ap[0:SP, :], in_=qs_scr.ap), writes=[qs32], group="sattn")
            S.dma("sp", lambda e: e.dma_start(out=ksm32.ap[0:SP, :], in_=ks_o), writes=[ksm32], group="sattn")
            S.dma("sp", lambda e: e.dma_start(out=vsm32.ap[0:SP, :], in_=vs_o), writes=[vsm32], group="sattn")
            S.dma("sp", lambda e: e.dma_start(out=pti.ap, in_=ptab.to_broadcast([128, 256])), writes=[pti], group="sattn")
            S.op("dve", lambda e: e.tensor_copy(out=ptf.ap, in_=pti.ap), reads=[pti], writes=[ptf])
            S.op("dve", lambda e: e.tensor_scalar(out=ptf.ap, in0=ptf.ap, scalar1=128.0, scalar2=lane.ap[:, 0:1],
                                                  op0=ALU.mult, op1=ALU.add), reads=[ptf, lane], writes=[ptf])
            S.op("dve", lambda e: e.tensor_copy(out=idx.ap, in_=ptf.ap), reads=[ptf], writes=[idx])
            for i in range(2):
                S.op("pool", lambda e, i=i: e.memset(Pz[i].ap, 0.0), writes=[Pz[i]])
            S.op("dve", lambda e: e.memset(pSA.ap, 0.0), writes=[pSA])
            S.op("dve", lambda e: e.memset(pSB.ap, 0.0), writes=[pSB])
            S.op("dve", lambda e: e.tensor_tensor(out=prod.ap[0:SP, 0, :], in0=qs32.ap[0:SP, :], in1=ksm32.ap[0:SP, :], op=ALU.mult),
                 reads=[qs32, ksm32], writes=[prod])
            S.op("dve", lambda e: e.tensor_reduce(out=pself.ap[0:SP, :], in_=prod.ap[0:SP, 0, :].rearrange("p (g d) -> p g d", d=64),
                                                  axis=AX.X, op=ALU.add), reads=[prod], writes=[pself])
            S.op("act", lambda e: e.activation(out=pself.ap[0:SP, :], in_=pself.ap[0:SP, :], func=AF.Exp, scale=SCALE),
                 reads=[pself], writes=[pself])
            vsa2 = vsa.ap.rearrange("p h d -> p (h d)")[:, 0:512]
            S.op("dve", lambda e: e.tensor_copy(out=vsa2[0:SP, :], in_=vsm32.ap[0:SP, :]), reads=[vsm32], writes=[vsa])
            S.op("pool", lambda e: e.memset(lself.ap[0:SP], 0.0), writes=[lself])
            for h in range(4):
                for c in range(2):
                    S.op("dve", lambda e, h=h, c=c: e.tensor_scalar(out=lself.ap[0:SP, h, c * 32:c * 32 + 16], in0=ident_f.ap[0:SP, 0:SP],
                                                                    scalar1=pself.ap[0:SP, 2 * h + c:2 * h + c + 1], scalar2=None,
                                                                    op0=ALU.mult), reads=[ident_f, pself], writes=[lself])
            for hp in range(2):
                lt = lself.ap[0:SP, 2 * hp:2 * hp + 2, :].rearrange("p h x -> p (h x)")
                S.op("pe", lambda e, hp=hp, lt=lt: e.matmul(pSA.ap[:, hp * 256:(hp + 1) * 256], lhsT=lt,
                                                            rhs=vsa2[0:SP, hp * 256:(hp + 1) * 256], start=False, stop=False,
                                                            skip_group_check=True), reads=[lself, vsa], writes=[pSA], inc=False)
                S.op("pe", lambda e, hp=hp, lt=lt: e.matmul(pSB.ap[:, hp:hp + 1], lhsT=lt, rhs=ones_bf.ap[0:SP, 0:1], start=False, stop=False,
                                                            skip_group_check=True), reads=[lself, ones_bf], writes=[pSB], inc=(hp == 1))

            jobs = [(sm, g) for sm in range(16) for g in range(16 // NG)]

            def s_stage0(ji):
                sm, g = jobs[ji]
                if g == 0:
                    qb_t = qbc[sm % 2]
                    S.dma("sp", lambda e: e.dma_start(out=qb_t.ap, in_=qs_scr.ap[sm:sm + 1, :].to_broadcast([128, 512])),
                          writes=[qb_t])
                kg = kpg[ji % 4]
                vg = vpg[ji % 4]
                for pg in range(NG):
                    col = sm * 16 + g * NG + pg
                    S.dma("pool", lambda e, kg=kg, pg=pg, col=col: e.indirect_dma_start(
                        out=kg.ap[:, pg, :], out_offset=None, in_=cache_k,
                        in_offset=bass.IndirectOffsetOnAxis(ap=idx.ap[:, col:col + 1], axis=0)),
                        reads=[idx], writes=[kg], semkey=f"D_kpg{ji % 4}")
                    S.dma("pool", lambda e, vg=vg, pg=pg, col=col: e.indirect_dma_start(
                        out=vg.ap[:, pg, :], out_offset=None, in_=cache_v,
                        in_offset=bass.IndirectOffsetOnAxis(ap=idx.ap[:, col:col + 1], axis=0)),
                        reads=[idx], writes=[vg], semkey=f"D_vpg{ji % 4}")

            def s_stage1(ji):
                sm, g = jobs[ji]
                kg = kpg[ji % 4]
                qb_t = qbc[sm % 2]
                sc = scs[sm % 2]
                pz = Pz[sm % 2]
                S.op("dve", lambda e: e.tensor_tensor(
                    out=prod.ap, in0=kg.ap, in1=qb_t.ap.unsqueeze(1).to_broadcast([128, NG, 512]), op=ALU.mult),
                    reads=[kg, qb_t], writes=[prod])
                S.op("dve", lambda e: e.tensor_reduce(
                    out=sc.ap[:, g * NG:(g + 1) * NG, :], in_=prod.ap.rearrange("p n (g d) -> p n g d", d=64),
                    axis=AX.X, op=ALU.add), reads=[prod], writes=[sc])
                S.op("act", lambda e: e.activation(
                    out=pz.ap[:, g * NG:(g + 1) * NG, :, :].rearrange("p n h (c s) -> p n h c s", s=32)[:, :, :, :, sm],
                    in_=sc.ap[:, g * NG:(g + 1) * NG, :].rearrange("p n (h c) -> p n h c", c=2),
                    func=AF.Exp, scale=SCALE), reads=[sc], writes=[pz])

            def s_stage2(ji):
                sm, g = jobs[ji]
                vg = vpg[ji % 4]
                pz = Pz[sm % 2]
                for pg in range(NG):
                    for hp in range(2):
                        lt = pz.ap[:, g * NG + pg, 2 * hp:2 * hp + 2, :].rearrange("p h x -> p (h x)")
                        S.op("pe", lambda e, pg=pg, hp=hp, lt=lt: e.matmul(
                            pSA.ap[:, hp * 256:(hp + 1) * 256], lhsT=lt, rhs=vg.ap[:, pg, hp * 256:(hp + 1) * 256],
                            start=False, stop=False, skip_group_check=True), reads=[pz, vg], writes=[pSA], inc=False)
                        S.op("pe", lambda e, hp=hp, lt=lt: e.matmul(
                            pSB.ap[:, hp:hp + 1], lhsT=lt, rhs=ones_bf.ap[:, 0:1],
                            start=False, stop=False, skip_group_check=True), reads=[pz, ones_bf], writes=[pSB], inc=(hp == 1))
                if g == 16 // NG - 1:
                    S.op("pool", lambda e: e.memset(
                        pz.ap.rearrange("p n h (c s) -> p n h c s", s=32)[:, :, :, :, sm], 0.0), writes=[pz])

            stick = [0]

            def sample_tick():
                t = stick[0]
                stick[0] += 1
                if t < len(jobs):
                    s_stage0(t)
                if 0 <= t - 1 < len(jobs):
                    s_stage1(t - 1)
                if 0 <= t - 2 < len(jobs):
                    s_stage2(t - 2)

            NTICK = len(jobs) + 2

            tiles = []
            for h in range(4):
                for m in range(4):
                    for c in range(2):
                        nfull = 12 + 16 * m
                        for si in range(nfull + 4):
                            tiles.append(dict(h=h, m=m, c=c, si=si, t=(si - nfull if si >= nfull else 0), diag=(si >= nfull),
                                              first=(si == 0), last=(si == nfull + 3)))
            NT = len(tiles)
            LA = 2
            tick_every = max(1, NT // NTICK)
            deferred = []

            def later(step, delay, fn):
                deferred.append((step + delay, fn))

            def run_due(step):
                k = 0
                while k < len(deferred):
                    if deferred[k][0] <= step:
                        deferred.pop(k)[1]()
                    else:
                        k += 1

            def oslot(qb_i):
                return (pOA, pOA.ap[:, qb_i * 129:(qb_i + 1) * 129]) if qb_i < 3 else (pOB, pOB.ap[:, 0:129])

            loaded_h = [-1]
            qz_raw = ksm32.ap[:, 0:512].bitcast(BF16)
            qz_raw2 = vsm32.ap[:, 0:512].bitcast(BF16)
            qz = [[T(qz_raw[:, 0:512], "qz00"), T(qz_raw[:, 512:1024], "qz01")],
                  [T(qz_raw2[:, 0:512], "qz10"), T(qz_raw2[:, 512:1024], "qz11")]]
            for c_ in range(2):
                for p_ in range(2):
                    S.op("pool", lambda e, c_=c_, p_=p_: e.memset(qz[c_][p_].ap, 0.0),
                         writes=[qz[c_][p_], ksm32, vsm32, vsa, pself, prod])

            def emit_qk(i):
                tl = tiles[i]
                h, m, c, si, q0 = tl["h"], tl["m"], tl["c"], tl["si"], tl["t"] * 128
                if loaded_h[0] < h:
                    loaded_h[0] = h
                    KT_, Vh_ = KTb[h % 2], Vb[h % 2]
                    S.dma("sp", lambda e: e.dma_start(out=KT_.ap, in_=kT_scr.ap[h]), writes=[KT_])
                    S.dma("sp", lambda e: e.dma_start(out=Vh_.ap, in_=v_scr.ap[h].rearrange("p (s d) -> p s d", d=130)), writes=[Vh_])
                KT = KTb[h % 2]
                psb = pS[i % 3]
                gpar = (h * 4 + m) % 2
                if tl["first"] and c == 0:
                    for c_ in range(2):
                        S.op("pool", lambda e, c_=c_: e.tensor_copy(out=qz[c_][gpar].ap[64 * c_:64 * c_ + 64, :],
                                                                    in_=QT.ap[64 * c_:64 * c_ + 64, h, m * 512:(m + 1) * 512]),
                             reads=[QT], writes=[qz[c_][gpar]])
                qzt = qz[c][gpar]
                S.op("pe", lambda e: e.matmul(
                    psb.ap[:, q0:512], lhsT=KT.ap[:, si * 128:(si + 1) * 128],
                    rhs=qzt.ap[:, q0:512], start=True, stop=True),
                    reads=[KT, qzt], writes=[psb])

            def finish_group(step, h, m, c):
                def f0():
                    S.op("act", lambda e: e.copy(out=oacc.ap[:, 0:3, :], in_=pOA.ap[:, 0:387].rearrange("p (q d) -> p q d", d=129)),
                         reads=[pOA], writes=[oacc])
                    S.op("act", lambda e: e.copy(out=oacc.ap[:, 3, :], in_=pOB.ap[:, 0:129]), reads=[pOB], writes=[oacc])
                f0()

                def f1():
                    S.op("dve", lambda e: e.reciprocal(out=rden.ap, in_=oacc.ap[:, :, 128]), reads=[oacc], writes=[rden])
                    if c == 0:
                        S.op("dve", lambda e: e.tensor_tensor(out=o1n.ap, in0=oacc.ap[:, :, 0:128],
                                                              in1=rden.ap.unsqueeze(2).to_broadcast([128, 4, 128]), op=ALU.mult),
                             reads=[oacc, rden], writes=[o1n])
                    else:
                        S.op("dve", lambda e: e.tensor_tensor(out=oacc.ap[:, :, 0:128], in0=oacc.ap[:, :, 0:128],
                                                              in1=rden.ap.unsqueeze(2).to_broadcast([128, 4, 128]), op=ALU.mult),
                             reads=[oacc, rden], writes=[oacc])
                        S.op("dve", lambda e: e.scalar_tensor_tensor(out=o1n.ap, in0=oacc.ap[:, :, 0:128], scalar=neglam.ap[:, 0:1],
                                                                    in1=o1n.ap, op0=ALU.mult, op1=ALU.add),
                             reads=[oacc, neglam, o1n], writes=[o1n])
                later(step, 2, f1)
                if c == 0:
                    return

                def f2():
                    for qb_i in range(4):
                        S.op("act", lambda e, qb_i=qb_i: e.activation(out=ojunk.ap, in_=o1n.ap[:, qb_i, :], func=AF.Square,
                                                                      accum_out=ssn.ap[:, qb_i:qb_i + 1]),
                             reads=[o1n], writes=[ojunk, ssn])
                later(step, 4, f2)

                def f3():
                    S.op("dve", lambda e: e.tensor_scalar(out=ssn.ap, in0=ssn.ap, scalar1=1.0 / 128, scalar2=EPS,
                                                          op0=ALU.mult, op1=ALU.add), reads=[ssn], writes=[ssn])
                later(step, 6, f3)

                def f4():
                    S.op("act", lambda e: e.activation(out=ssn.ap, in_=ssn.ap, func=AF.Sqrt), reads=[ssn], writes=[ssn])
                later(step, 8, f4)

                def f5():
                    S.op("dve", lambda e: e.reciprocal(out=ssn.ap, in_=ssn.ap), reads=[ssn], writes=[ssn])
                    for qb_i in range(4):
                        S.op("dve", lambda e, qb_i=qb_i: e.scalar_tensor_tensor(
                            out=obf.ap[:, qb_i, :], in0=o1n.ap[:, qb_i, :], scalar=ssn.ap[:, qb_i:qb_i + 1], in1=gsub.ap,
                            op0=ALU.mult, op1=ALU.mult), reads=[o1n, ssn, gsub], writes=[obf])
                later(step, 10, f5)

                def f6():
                    for qb_i in range(4):
                        S.op("pe", lambda e, qb_i=qb_i: e.transpose(out=pXc.ap[:, qb_i * 128:(qb_i + 1) * 128], in_=obf.ap[:, qb_i, :],
                                                                    identity=ident.ap), reads=[obf, ident], writes=[pXc],
                             inc=(qb_i == 3))
                later(step, 13, f6)

                def f7():
                    S.op("dve", lambda e: e.tensor_copy(out=oT.ap[:, h, m * 512:(m + 1) * 512], in_=pXc.ap[:, 0:512]),
                         reads=[pXc], writes=[oT])
                later(step, 15, f7)

            def emit_tile(j):
                tl = tiles[j]
                h, m, c, si, t = tl["h"], tl["m"], tl["c"], tl["si"], tl["t"]
                q0 = t * 128
                Vh = Vb[h % 2]
                psb = pS[j % 3]
                pbt = pb_[j % 3]
                if tl["first"]:
                    S.op("dve", lambda e: e.memset(pOA.ap, 0.0), writes=[pOA])
                    S.op("dve", lambda e: e.memset(pOB.ap[:, 0:129], 0.0), writes=[pOB])
                S.op("act", lambda e: e.activation(out=pbt.ap[:, q0:512], in_=psb.ap[:, q0:512], func=AF.Exp, scale=SCALE),
                     reads=[psb], writes=[pbt])
                if tl["diag"]:
                    S.op("dve", lambda e: e.tensor_tensor(out=pbt.ap[:, q0:q0 + 128], in0=pbt.ap[:, q0:q0 + 128],
                                                          in1=tri.ap, op=ALU.mult), reads=[pbt, tri], writes=[pbt])
                for qb_i in range(t, 4):
                    bk, ap_ = oslot(qb_i)
                    S.op("pe", lambda e, ap_=ap_, qb_i=qb_i: e.matmul(
                        ap_, lhsT=pbt.ap[:, qb_i * 128:(qb_i + 1) * 128], rhs=Vh.ap[:, si, 0:129], start=False, stop=False,
                        skip_group_check=True), reads=[pbt, Vh], writes=[bk], inc=(qb_i == 3))
                if tl["last"]:
                    finish_group(j, h, m, c)

            for i in range(NT + LA):
                if i < NT:
                    emit_qk(i)
                j = i - LA
                if j >= 0:
                    emit_tile(j)
                    run_due(j)
                    if j % tick_every == 0 and stick[0] < NTICK:
                        sample_tick()
            while deferred:
                run_due(10 ** 9)
            while stick[0] < NTICK:
                sample_tick()
            for h in range(4):
                hp, hl = h // 2, h % 2
                for c in range(2):
                    r0 = hl * 64 + c * 32
                    dst = osm if c == 0 else osm2
                    eng = "act" if c == 0 else "dve"
                    src_n = pSA.ap[r0:r0 + SP, hp * 256 + hl * 128:hp * 256 + hl * 128 + 128]
                    src_d = pSB.ap[r0:r0 + SP, hp:hp + 1]
                    if eng == "act":
                        S.op("act", lambda e, dst=dst, h=h, src_n=src_n: e.copy(out=dst.ap[0:SP, h, 0:128], in_=src_n), reads=[pSA], writes=[dst])
                        S.op("act", lambda e, dst=dst, h=h, src_d=src_d: e.copy(out=dst.ap[0:SP, h, 128:129], in_=src_d), reads=[pSB], writes=[dst])
                    else:
                        S.op("dve", lambda e, dst=dst, h=h, src_n=src_n: e.tensor_copy(out=dst.ap[0:SP, h, 0:128], in_=src_n), reads=[pSA], writes=[dst])
                        S.op("dve", lambda e, dst=dst, h=h, src_d=src_d: e.tensor_copy(out=dst.ap[0:SP, h, 128:129], in_=src_d), reads=[pSB], writes=[dst])
            S.op("dve", lambda e: e.reciprocal(out=rden.ap[0:SP, :], in_=osm.ap[0:SP, :, 128]), reads=[osm], writes=[rden])
            S.op("dve", lambda e: e.tensor_tensor(out=osn.ap[0:SP], in0=osm.ap[0:SP, :, 0:128],
                                                  in1=rden.ap[0:SP].unsqueeze(2).to_broadcast([SP, 4, 128]), op=ALU.mult),
                 reads=[osm, rden], writes=[osn])
            S.op("dve", lambda e: e.reciprocal(out=rden.ap[0:SP, :], in_=osm2.ap[0:SP, :, 128]), reads=[osm2], writes=[rden])
            S.op("dve", lambda e: e.tensor_tensor(out=osm2.ap[0:SP, :, 0:128], in0=osm2.ap[0:SP, :, 0:128],
                                                  in1=rden.ap[0:SP].unsqueeze(2).to_broadcast([SP, 4, 128]), op=ALU.mult),
                 reads=[osm2, rden], writes=[osm2])
            S.op("dve", lambda e: e.scalar_tensor_tensor(out=osn.ap[0:SP], in0=osm2.ap[0:SP, :, 0:128], scalar=neglam.ap[0:SP, 0:1],
                                                        in1=osn.ap[0:SP], op0=ALU.mult, op1=ALU.add),
                 reads=[osm2, neglam, osn], writes=[osn])
            for h in range(4):
                S.op("act", lambda e, h=h: e.activation(out=ojunk.ap[0:SP, :], in_=osn.ap[0:SP, h, :], func=AF.Square,
                                                        accum_out=ssn.ap[0:SP, h:h + 1]), reads=[osn], writes=[ojunk, ssn])
            S.op("dve", lambda e: e.tensor_scalar(out=ssn.ap[0:SP, :], in0=ssn.ap[0:SP, :], scalar1=1.0 / 128, scalar2=EPS,
                                                  op0=ALU.mult, op1=ALU.add), reads=[ssn], writes=[ssn])
            S.op("act", lambda e: e.activation(out=ssn.ap[0:SP, :], in_=ssn.ap[0:SP, :], func=AF.Sqrt), reads=[ssn], writes=[ssn])
            S.op("dve", lambda e: e.reciprocal(out=ssn.ap[0:SP, :], in_=ssn.ap[0:SP, :]), reads=[ssn], writes=[ssn])
            for h in range(4):
                S.op("dve", lambda e, h=h: e.scalar_tensor_tensor(out=obf.ap[0:SP, h, :], in0=osn.ap[0:SP, h, :],
                                                                  scalar=ssn.ap[0:SP, h:h + 1], in1=gsub.ap[0:SP, :],
                                                                  op0=ALU.mult, op1=ALU.mult), reads=[osn, ssn, gsub], writes=[obf])
            for h in range(4):
                S.op("pe", lambda e, h=h: e.transpose(out=pXc.ap[:, h * 128:h * 128 + SP], in_=obf.ap[0:SP, h, :],
                                                      identity=ident.ap[0:SP, 0:SP]), reads=[obf, ident], writes=[pXc], inc=(h == 3))
            S.op("dve", lambda e: e.tensor_copy(out=catS.ap[:, 4:8, :],
                                                in_=pXc.ap[:, 0:512].rearrange("p (h t) -> p h t", t=128)[:, :, 0:SP]),
                 reads=[pXc], writes=[catS])
            S.barrier()
        M.release(base_mark)

        with psum_scope() as pst:
            pM = [pbank(pst, f"pM{i}") for i in range(4)]
            wo = M.alloc("wo", [128, 8, D], BF16)
            gtm_p = M.alloc("gtm_p", [128, D], F32)
            gtm_s = M.alloc("gtm_s", [128, D], F32)
            xd = [M.alloc(f"xd{i}", [128, D], F32) for i in range(2)]
            x1t = [M.alloc(f"x1t{i}", [128, D], F32) for i in range(2)]
            w_out_v = w_out.rearrange("(k p) n -> p k n", p=128)
            wos = [M.alloc(f"wos{i_}", [128, 4096], F32) for i_ in range(2)]
            for j in range(2):
                load_cast(wo, wo.ap[:, :, j * 512:(j + 1) * 512], w_out_v[:, :, j * 512:(j + 1) * 512], wos[j], (128, 8, 512))
            S.dma("sp", lambda e: e.dma_start(out=gtm_p.ap, in_=mod_p.ap[2]), writes=[gtm_p], group="d1mods")
            S.dma("sp", lambda e: e.dma_start(out=gtm_s.ap[0:16, :], in_=mod_s.ap[2]), writes=[gtm_s], group="d1mods")
            for i in range(17):
                np_ = 128 if i < 16 else 16
                par = i % 2
                xt = xd[par]
                x1 = x1t[par]
                if i < 16:
                    m, t = i // 4, i % 4
                    sl = 12 + 16 * m + t
                    S.dma("sp", lambda e, xt=xt, sl=sl: e.dma_start(out=xt.ap, in_=xs[sl * 128:(sl + 1) * 128, :]), writes=[xt])
                    gt = gtm_p
                else:
                    S.dma("sp", lambda e, xt=xt: e.dma_start(out=xt.ap[0:16, :], in_=xsm), writes=[xt])
                    gt = gtm_s
                for nb in range(2):
                    pm = pM[par * 2 + nb]
                    for fc in range(8):
                        if i < 16:
                            lt = (convT.ap[:, fc, i * 128:(i + 1) * 128] if fc < 4 else oT.ap[:, fc - 4, i * 128:(i + 1) * 128])
                            rd = convT if fc < 4 else oT
                        else:
                            lt = catS.ap[:, fc, :]
                            rd = catS
                        S.op("pe", lambda e, pm=pm, lt=lt, fc=fc, nb=nb, np_=np_: e.matmul(
                            pm.ap[0:np_, :], lhsT=lt, rhs=wo.ap[:, fc, nb * 512:(nb + 1) * 512], start=(fc == 0), stop=(fc == 7)),
                            reads=[rd, wo], writes=[pm], inc=(fc == 7))
                    S.op("dve", lambda e, pm=pm, x1=x1, gt=gt, nb=nb, np_=np_: e.tensor_tensor(
                        out=x1.ap[0:np_, nb * 512:(nb + 1) * 512], in0=pm.ap[0:np_, :], in1=gt.ap[0:np_, nb * 512:(nb + 1) * 512],
                        op=ALU.mult), reads=[pm, gt], writes=[x1])
                S.op("pool", lambda e, x1=x1, xt=xt, np_=np_: e.tensor_tensor(out=x1.ap[0:np_, :], in0=x1.ap[0:np_, :], in1=xt.ap[0:np_, :],
                                                                            op=ALU.add), reads=[x1, xt], writes=[x1])
                S.dma("sp", lambda e, x1=x1, i=i, np_=np_: e.dma_start(out=x1_scr.ap[i * 128:i * 128 + np_, :], in_=x1.ap[0:np_, :]),
                      reads=[x1], semkey=f"D_x1t{par}")
            S.barrier()
        M.release(0)

        with psum_scope() as pst:
            pT2 = pbank(pst, "pT2", BF16)
            pG = [pbank(pst, f"pG{i}") for i in range(2)]
            pU = [pbank(pst, f"pU{i}") for i in range(2)]
            pAT = pbank(pst, "pAT", BF16)
            pO = [pbank(pst, f"pO{i}") for i in range(2)]

            identb = M.alloc("identb", [128, 128], BF16)
            identf2 = M.alloc("identf2", [128, 128], F32)
            mh2 = M.alloc("mh2", [128, 1], F32)
            S.op("pool", lambda e: e.memset(identf2.ap, 0.0), writes=[identf2])
            S.op("pool", lambda e: e.affine_select(out=identf2.ap, in_=identf2.ap, pattern=[[-1, 128]],
                                                   compare_op=ALU.not_equal, fill=1.0, base=0, channel_multiplier=1),
                 reads=[identf2], writes=[identf2])
            S.op("dve", lambda e: e.tensor_copy(out=identb.ap, in_=identf2.ap), reads=[identf2], writes=[identb])
            S.op("pool", lambda e: e.memset(mh2.ap, -0.5), writes=[mh2])
            wf1 = M.alloc("wf1", [128, 8, 2 * DFF], BF16)
            wf2 = M.alloc("wf2", [128, 22, D], BF16)
            w_f1_v = w_f1.rearrange("(k p) n -> p k n", p=128)
            w_f2_v = w_f2.rearrange("(k p) n -> p k n", p=128)
            stg_mark = M.mark()
            wfs = [M.alloc(f"wfs{i_}", [128, 4096], F32) for i_ in range(3)]
            for j in range(11):
                load_cast(wf1, wf1.ap[:, :, j * 512:(j + 1) * 512], w_f1_v[:, :, j * 512:(j + 1) * 512], wfs[j % 3], (128, 8, 512))
            for j in range(6):
                k0, k1 = j * 4, min(22, j * 4 + 4)
                load_cast(wf2, wf2.ap[:, k0:k1, :], w_f2_v[:, k0:k1, :], wfs[(11 + j) % 3], (128, k1 - k0, 1024))
            S.barrier()
            M.release(stg_mark)
            Af_t = M.alloc("Af_t", [128, D], F32)
            shf_t = M.alloc("shf_t", [128, D], F32)
            gtf_t = M.alloc("gtf_t", [128, D], F32)
            gfin = M.alloc("gfin", [128, D], F32)
            x1b = [M.alloc(f"x1b{i}", [128, D], F32) for i in range(2)]
            x1e = M.alloc("x1e", [128, D], F32)
            sq2 = [M.alloc(f"sq2{i}", [128, 1], F32) for i in range(2)]
            rs2 = [M.alloc(f"rs2{i}", [128, 1], F32) for i in range(2)]
            sq3 = [M.alloc(f"sq3{i}", [128, 1], F32) for i in range(2)]
            rs3 = [M.alloc(f"rs3{i}", [128, 1], F32) for i in range(2)]
            hp2 = M.alloc("hp2", [128, D], F32)
            h2b = [M.alloc(f"h2b{i}", [128, D], BF16) for i in range(2)]
            h2T = [M.alloc(f"h2T{i}", [128, 8, 128], BF16) for i in range(2)]
            sgt = [M.alloc(f"sgt{i}", [128, 352], F32) for i in range(2)]
            actb = [M.alloc(f"actb{i}", [128, DFF], BF16) for i in range(2)]
            actT = [M.alloc(f"actT{i}", [128, 22, 128], BF16) for i in range(2)]
            x2t = [M.alloc(f"x2t{i}", [128, D], F32) for i in range(2)]

            S.dma("sp", lambda e: e.dma_start(out=Af_t.ap, in_=mod_p.ap[4]), writes=[Af_t], group="d2mods")
            S.dma("sp", lambda e: e.dma_start(out=shf_t.ap, in_=mod_p.ap[3]), writes=[shf_t], group="d2mods")
            S.dma("sp", lambda e: e.dma_start(out=gtf_t.ap, in_=mod_p.ap[5]), writes=[gtf_t], group="d2mods")
            S.dma("sp", lambda e: e.dma_start(out=gfin.ap, in_=g_fin.to_broadcast([128, D])), writes=[gfin], group="d2mods")

            NTL = 17

            def npof(i):
                return 128 if i < 16 else 16

            def e0(i):
                np_, x1 = npof(i), x1b[i % 2]
                S.dma("sp", lambda e: e.dma_start(out=x1.ap[0:np_, :], in_=x1_scr.ap[i * 128:i * 128 + np_, :]), writes=[x1])

            def e1(i):
                np_, x1, sq, jk = npof(i), x1b[i % 2], sq2[i % 2], h2b[i % 2]
                S.op("act", lambda e: e.activation(out=jk.ap[0:np_, :], in_=x1.ap[0:np_, :], func=AF.Square,
                                                   accum_out=sq.ap[0:np_, :]), reads=[x1], writes=[jk, sq])
                S.op("dve", lambda e: e.tensor_scalar(out=sq.ap[0:np_, :], in0=sq.ap[0:np_, :], scalar1=1.0 / D, scalar2=EPS,
                                                      op0=ALU.mult, op1=ALU.add), reads=[sq], writes=[sq])

            def e2(i):
                np_, x1, sq, rs, hbt = npof(i), x1b[i % 2], sq2[i % 2], rs2[i % 2], h2b[i % 2]
                if i == 16:
                    S.dma("sp", lambda e: e.dma_start(out=Af_t.ap[0:16, :], in_=mod_s.ap[4]), writes=[Af_t], semkey="D_afs")
                    S.dma("sp", lambda e: e.dma_start(out=shf_t.ap[0:16, :], in_=mod_s.ap[3]), writes=[shf_t], semkey="D_shs")
                S.op("act", lambda e: e.activation(out=sq.ap[0:np_, :], in_=sq.ap[0:np_, :], func=AF.Sqrt), reads=[sq], writes=[sq])
                S.op("dve", lambda e: e.reciprocal(out=rs.ap[0:np_, :], in_=sq.ap[0:np_, :]), reads=[sq], writes=[rs])
                S.op("dve", lambda e: e.scalar_tensor_tensor(out=hp2.ap[0:np_, :], in0=x1.ap[0:np_, :], scalar=rs.ap[0:np_, 0:1],
                                                            in1=Af_t.ap[0:np_, :], op0=ALU.mult, op1=ALU.mult),
                     reads=[x1, rs, Af_t], writes=[hp2])
                S.op("pool", lambda e: e.tensor_tensor(out=hbt.ap[0:np_, :], in0=hp2.ap[0:np_, :], in1=shf_t.ap[0:np_, :], op=ALU.add),
                     reads=[hp2, shf_t], writes=[hbt])

            def e3(i):
                np_, hbt, hTt = npof(i), h2b[i % 2], h2T[i % 2]
                for kc in range(8):
                    S.op("pe", lambda e, kc=kc: e.transpose(out=pT2.ap[:, kc * 128:kc * 128 + np_],
                                                            in_=hbt.ap[0:np_, kc * 128:(kc + 1) * 128],
                                                            identity=identb.ap[0:np_, 0:np_]),
                         reads=[hbt, identb], writes=[pT2], inc=(kc == 7))
                S.op("act", lambda e: e.copy(out=hTt.ap[:, :, 0:np_],
                                             in_=pT2.ap[:, 0:1024].rearrange("p (c t) -> p c t", t=128)[:, :, 0:np_]),
                     reads=[pT2], writes=[hTt])

            def e4(i):
                np_, hTt, ab = npof(i), h2T[i % 2], actb[i % 2]
                for blk in range(8):
                    pg_, pu_ = pG[blk % 2], pU[blk % 2]
                    c0 = blk * 352
                    for kc in range(8):
                        S.op("pe", lambda e, kc=kc, pg_=pg_, c0=c0: e.matmul(
                            pg_.ap[0:np_, 0:352], lhsT=hTt.ap[:, kc, 0:np_], rhs=wf1.ap[:, kc, c0:c0 + 352],
                            start=(kc == 0), stop=(kc == 7)), reads=[hTt, wf1], writes=[pg_], inc=(kc == 7))
                    for kc in range(8):
                        S.op("pe", lambda e, kc=kc, pu_=pu_, c0=c0: e.matmul(
                            pu_.ap[0:np_, 0:352], lhsT=hTt.ap[:, kc, 0:np_], rhs=wf1.ap[:, kc, DFF + c0:DFF + c0 + 352],
                            start=(kc == 0), stop=(kc == 7)), reads=[hTt, wf1], writes=[pu_], inc=(kc == 7))
                    sg = sgt[blk % 2]
                    S.op("act", lambda e, sg=sg, pg_=pg_: e.activation(out=sg.ap[0:np_, :], in_=pg_.ap[0:np_, 0:352], func=AF.Silu),
                         reads=[pg_], writes=[sg])
                    S.op("dve", lambda e, sg=sg, pu_=pu_, c0=c0: e.tensor_tensor(
                        out=ab.ap[0:np_, c0:c0 + 352], in0=pu_.ap[0:np_, 0:352], in1=sg.ap[0:np_, :], op=ALU.mult),
                        reads=[pu_, sg], writes=[ab])

            def e5(i):
                np_, ab, aT = npof(i), actb[i % 2], actT[i % 2]
                S.dma("sp", lambda e: e.dma_start(out=x1e.ap[0:np_, :], in_=x1_scr.ap[i * 128:i * 128 + np_, :]), writes=[x1e])
                if i == 16:
                    S.dma("sp", lambda e: e.dma_start(out=gtf_t.ap[0:16, :], in_=mod_s.ap[5]), writes=[gtf_t], semkey="D_gts")
                for r0 in range(0, 22, 8):
                    nr = min(8, 22 - r0)
                    for k2 in range(nr):
                        fc = r0 + k2
                        S.op("pe", lambda e, fc=fc, k2=k2: e.transpose(
                            out=pAT.ap[:, k2 * 128:k2 * 128 + np_], in_=ab.ap[0:np_, fc * 128:(fc + 1) * 128],
                            identity=identb.ap[0:np_, 0:np_]), reads=[ab, identb], writes=[pAT], inc=(k2 == nr - 1))
                    src = pAT.ap[:, 0:nr * 128].rearrange("p (c t) -> p c t", t=128)[:, :, 0:np_]
                    dst = aT.ap[:, r0:r0 + nr, 0:np_]
                    if (r0 // 8) % 2 == 0:
                        S.op("act", lambda e, src=src, dst=dst: e.copy(out=dst, in_=src), reads=[pAT], writes=[aT])
                    else:
                        S.op("dve", lambda e, src=src, dst=dst: e.tensor_copy(out=dst, in_=src), reads=[pAT], writes=[aT])

            def e6(i):
                np_, aT, x2 = npof(i), actT[i % 2], x2t[i % 2]
                for nb in range(2):
                    po = pO[nb]
                    for fc in range(22):
                        S.op("pe", lambda e, fc=fc, po=po, nb=nb: e.matmul(
                            po.ap[0:np_, :], lhsT=aT.ap[:, fc, 0:np_], rhs=wf2.ap[:, fc, nb * 512:(nb + 1) * 512],
                            start=(fc == 0), stop=(fc == 21)), reads=[aT, wf2], writes=[po], inc=(fc == 21))
                    S.op("dve", lambda e, po=po, nb=nb: e.tensor_tensor(
                        out=x2.ap[0:np_, nb * 512:(nb + 1) * 512], in0=po.ap[0:np_, :], in1=gtf_t.ap[0:np_, nb * 512:(nb + 1) * 512],
                        op=ALU.mult), reads=[po, gtf_t], writes=[x2])
                S.op("pool", lambda e: e.tensor_tensor(out=x2.ap[0:np_, :], in0=x2.ap[0:np_, :], in1=x1e.ap[0:np_, :], op=ALU.add),
                     reads=[x2, x1e], writes=[x2])

            def e7(i):
                np_, x2, sq = npof(i), x2t[i % 2], sq3[i % 2]
                S.op("act", lambda e: e.activation(out=hp2.ap[0:np_, :], in_=x2.ap[0:np_, :], func=AF.Square,
                                                   accum_out=sq.ap[0:np_, :]), reads=[x2], writes=[hp2, sq])
                S.op("dve", lambda e: e.tensor_scalar(out=sq.ap[0:np_, :], in0=sq.ap[0:np_, :], scalar1=1.0 / D, scalar2=EPS,
                                                      op0=ALU.mult, op1=ALU.add), reads=[sq], writes=[sq])

            def e8(i):
                np_, x2, sq, rs = npof(i), x2t[i % 2], sq3[i % 2], rs3[i % 2]
                S.op("act", lambda e: e.activation(out=sq.ap[0:np_, :], in_=sq.ap[0:np_, :], func=AF.Sqrt), reads=[sq], writes=[sq])
                S.op("dve", lambda e: e.reciprocal(out=rs.ap[0:np_, :], in_=sq.ap[0:np_, :]), reads=[sq], writes=[rs])
                S.op("dve", lambda e: e.scalar_tensor_tensor(out=x2.ap[0:np_, :], in0=x2.ap[0:np_, :], scalar=rs.ap[0:np_, 0:1],
                                                            in1=gfin.ap[0:np_, :], op0=ALU.mult, op1=ALU.mult),
                     reads=[x2, rs, gfin], writes=[x2])
                if i < 16:
                    S.dma("sp", lambda e: e.dma_start(out=y_o[i * 128:(i + 1) * 128, :], in_=x2.ap), reads=[x2],
                          semkey="D_yo", is_out=True)
                else:
                    S.dma("sp", lambda e: e.dma_start(out=ys_o, in_=x2.ap[0:16, :]), reads=[x2], semkey="D_yso", is_out=True)

            estages = [e0, e1, e2, e3, e4, e5, e6, e7, e8]
            for it in range(NTL + len(estages) - 1):
                for k in range(len(estages) - 1, -1, -1):
                    sl = it - k
                    if 0 <= sl < NTL:
                        estages[k](sl)

        need = {}
        for k, v in S.out_evs:
            need[k] = max(need.get(k, 0), v)
        S.prog["sp"].append((list(need.items()), None, None, 0))
        S.replay()
    return nc


_CACHE = {}


def _rope_table(pos):
    inv = (500000.0 ** (-np.arange(0, 16, 2, dtype=np.float32) / 16)).astype(np.float32)
    ang = pos.astype(np.float32)[:, None] * inv[None, :]
    return np.concatenate([np.cos(ang), np.sin(ang)], axis=1).astype(np.float32)


def kernel(x_prompt, x_sample, cache_k, cache_v, state_conv, page_table, c_prompt, c_sample,
           norm_mix_g, norm_ffn_g, norm_final_g, w_ada, b_ada, w_in, conv_dw_w, conv_dw_b,
           conv_ln_g, conv_ln_b, lambda_q1, lambda_k1, lambda_q2, lambda_k2, subln_g,
           w_out, w_ffn_in, w_ffn_out):
    f = lambda a: np.ascontiguousarray(np.asarray(a), dtype=np.float32)
    x_prompt = f(x_prompt)
    npool = int(np.asarray(cache_k).shape[1])
    past = int(np.asarray(page_table).shape[1]) * 128
    if npool not in _CACHE:
        _CACHE[npool] = build_program(npool)
    nc = _CACHE[npool]
    ck = f(cache_k).reshape(npool * 128, 512)
    cv = f(cache_v).reshape(npool * 128, 512)
    sel = np.zeros((30, 16, 16), np.float32)
    for s in range(16):
        sel[:, s, s] = 1.0
    shared = {
        "cache_k": ck, "cache_v": cv,
        "w_ada": f(w_ada)[0], "b_ada": f(b_ada), "w_in": f(w_in)[0], "w_out": f(w_out)[0],
        "w_f1": f(w_ffn_in)[0], "w_f2": f(w_ffn_out)[0],
        "g_mix": f(norm_mix_g), "g_ffn": f(norm_ffn_g), "g_fin": f(norm_final_g).reshape(1, D),
        "dw_w": f(conv_dw_w)[0], "dw_b": f(conv_dw_b), "cln_g": f(conv_ln_g), "cln_b": f(conv_ln_b),
        "lam4": np.concatenate([f(lambda_q1), f(lambda_k1), f(lambda_q2), f(lambda_k2)], axis=1),
        "subg": f(subln_g), "sel": sel.reshape(30, 256),
        "lane": np.arange(128, dtype=np.float32).reshape(128, 1),
        "rope_s": np.repeat(_rope_table(np.array([past])), 16, axis=0),
    }
    pt = np.asarray(page_table).astype(np.int32)
    in_maps = []
    tile_of = []
    for c in range(8):
        b, j = c // 4, c % 4
        npad = 12 - 4 * j
        xs = np.zeros((NS, 128, D), np.float32)
        pos = np.zeros((NS, 128), np.float32)
        val = np.zeros((128, NS), np.float32)
        xt = x_prompt[b].reshape(64, 128, D)
        gl = []
        for s in range(NS):
            g = s - npad
            gl.append(g)
            if g >= 0:
                xs[s] = xt[g]
                pos[s] = g * 128 + np.arange(128)
                val[:, s] = 1.0
        tile_of.append(gl)
        m = dict(shared)
        m.update({
            "xs": xs.reshape(NS * 128, D), "rope": np.ascontiguousarray(_rope_table(pos.reshape(-1)).reshape(NS, 128, 16).transpose(1, 0, 2)).reshape(128, NS * 16), "valid": val,
            "xsm": f(x_sample)[16 * c:16 * c + 16, 0, :], "csm": f(c_sample)[16 * c:16 * c + 16],
            "cpr": f(c_prompt)[b:b + 1], "ptab": pt[16 * c:16 * c + 16].reshape(1, 256),
            "stconv": f(state_conv)[0, 16 * c:16 * c + 16],
        })
        in_maps.append(m)
    res = run_bass_kernel_spmd(nc, in_maps, core_ids=list(range(8))).results
    B, SEQ = x_prompt.shape[0], x_prompt.shape[1]
    y_prompt = np.zeros((B, SEQ, D), np.float32)
    k_prompt = np.zeros((1, B, SEQ, 4, 128), np.float32)
    v_prompt = np.zeros((1, B, SEQ, 4, 128), np.float32)
    conv_prompt = np.zeros((1, B, 30, 512), np.float32)
    y_sample = np.zeros((128, 1, D), np.float32)
    k_sample = np.zeros((1, 128, 1, 4, 128), np.float32)
    v_sample = np.zeros((1, 128, 1, 4, 128), np.float32)
    conv_sample = np.zeros((1, 128, 30, 512), np.float32)
    for c in range(8):
        b, j = c // 4, c % 4
        r = res[c]
        for m in range(4):
            for t in range(4):
                i = m * 4 + t
                g = tile_of[c][12 + 16 * m + t]
                y_prompt[b, g * 128:(g + 1) * 128] = r["y_o"][i * 128:(i + 1) * 128]
                k_prompt[0, b, g * 128:(g + 1) * 128] = r["k_o"][i * 128:(i + 1) * 128].reshape(128, 4, 128)
                v_prompt[0, b, g * 128:(g + 1) * 128] = r["v_o"][i * 128:(i + 1) * 128].reshape(128, 4, 128)
        if j == 3:
            conv_prompt[0, b] = r["cp_o"]
        y_sample[16 * c:16 * c + 16, 0] = r["ys_o"]
        k_sample[0, 16 * c:16 * c + 16, 0] = r["ks_o"].reshape(16, 4, 128)
        v_sample[0, 16 * c:16 * c + 16, 0] = r["vs_o"].reshape(16, 4, 128)
        conv_sample[0, 16 * c:16 * c + 16] = r["cs_o"]
    return (y_prompt, y_sample, k_prompt, v_prompt, conv_prompt, k_sample, v_sample, conv_sample)
```

```python
import numpy as np
import ml_dtypes
from contextlib import ExitStack
import concourse.bass as bass
import concourse.mybir as mybir
from concourse.bass_utils import run_bass_kernel_spmd

F32 = mybir.dt.float32
BF16 = mybir.dt.bfloat16
I32 = mybir.dt.int32
AF = mybir.ActivationFunctionType
ALU = mybir.AluOpType
AX = mybir.AxisListType

ENGS = ("pe", "act", "dve", "pool", "sp")
D = 1024
NS = 64
NOWN = 16
DFF = 2816
EPS = 1e-6
LAM_INIT = 0.2
SCALE = 0.125
SENT = 1 << 30


class Buf:
    __slots__ = ("name", "w", "r")

    def __init__(self, name):
        self.name = name
        self.w = None
        self.r = []


class T:
    __slots__ = ("ap", "b")

    def __init__(self, ap, name):
        self.ap = ap
        self.b = Buf(name)


class Sched:
    def __init__(self, nc, stack):
        self.nc = nc
        self.stack = stack
        self.prog = {e: [] for e in ENGS}
        self.sems = {}
        self.cnt = {}
        self.seen = {e: {} for e in ENGS}
        for e in ENGS:
            self._sem("E_" + e)
        self.dma_rr = 0
        self.out_evs = []

    def _sem(self, key):
        if key not in self.sems:
            self.sems[key] = self.stack.enter_context(self.nc.semaphore(key))
            self.cnt[key] = 0
        return self.sems[key]

    def _deps(self, eng, reads, writes, ignore_key=None):
        need = {}

        def add(ev):
            if ev is None:
                return
            k, v = ev
            if k == ignore_key:
                return
            if need.get(k, 0) < v:
                need[k] = v
        for b in reads:
            add(b.w)
        for b in writes:
            add(b.w)
            for ev in b.r:
                add(ev)
        waits = []
        seen = self.seen[eng]
        for k, v in need.items():
            if seen.get(k, 0) < v:
                seen[k] = v
                waits.append((k, v))
        return waits

    @staticmethod
    def _bufs(xs):
        return [x.b if isinstance(x, T) else x for x in xs]

    def op(self, eng, fn, reads=(), writes=(), inc=True):
        reads = self._bufs(reads)
        writes = self._bufs(writes)
        key = "E_" + eng
        waits = self._deps(eng, reads, writes)
        if eng == "pe":
            waits = [(k, v) for (k, v) in waits if k != key]
        if inc:
            self.cnt[key] += 1
            ev = (key, self.cnt[key])
        else:
            ev = (key, self.cnt[key] + 1)
        self.prog[eng].append((waits, fn, key if inc else None, 1))
        for b in reads:
            b.r.append(ev)
        for b in writes:
            b.w = ev
            b.r = []
        return ev

    def dma(self, eng, fn, reads=(), writes=(), semkey=None, is_out=False, group=None):
        reads = self._bufs(reads)
        writes = self._bufs(writes)
        if group is not None:
            semkey = "G_" + group
        if semkey is None:
            semkey = "D_" + (writes[0].name if writes else reads[0].name)
        self._sem(semkey)
        waits = self._deps(eng, reads, writes, ignore_key=semkey)
        self.cnt[semkey] += 16
        ev = (semkey, SENT if group is not None else self.cnt[semkey])
        self.prog[eng].append((waits, fn, semkey, 16))
        for b in reads:
            b.r.append(ev)
        for b in writes:
            b.w = ev
            b.r = []
        if is_out:
            self.out_evs.append(ev)
        return ev

    def barrier(self):
        evs = [(k, v) for k, v in self.cnt.items() if v > 0]
        for e in ENGS:
            waits = []
            seen = self.seen[e]
            for k, v in evs:
                if e == "pe" and k == "E_pe":
                    continue
                if seen.get(k, 0) < v:
                    seen[k] = v
                    waits.append((k, v))
            if waits:
                self.prog[e].append((waits, None, None, 0))

    def replay(self):
        sems = self.sems
        prog = self.prog

        def run(name):
            def body(e):
                for waits, fn, inck, incv in prog[name]:
                    for k, v in waits:
                        e.wait_ge(sems[k], self.cnt[k] if v == SENT else v)
                    if fn is None:
                        continue
                    ins = fn(e)
                    if inck is not None:
                        ins.then_inc(sems[inck], incv)
            return body

        with self.nc.Block() as block:
            block.tensor(run("pe"))
            block.scalar(run("act"))
            block.vector(run("dve"))
            block.gpsimd(run("pool"))
            block.sync(run("sp"))


class Mem:
    def __init__(self, big, nwords):
        self.big = big
        self.n = nwords
        self.top = 0
        self.uid = 0

    def mark(self):
        return self.top

    def release(self, m):
        self.top = m

    def alloc(self, name, shape, dt, parts=128):
        free = int(np.prod(shape[1:]))
        words = free if dt in (F32, I32) else (free + 1) // 2
        words = (words + 7) // 8 * 8
        assert self.top + words <= self.n, f"SBUF overflow allocating {name}: {self.top}+{words}>{self.n}"
        ap = self.big[0:shape[0], self.top:self.top + words]
        self.top += words
        if dt == BF16:
            ap = ap.bitcast(BF16)[:, 0:free]
        elif dt == I32:
            ap = ap.bitcast(I32)[:, 0:free]
        else:
            ap = ap[:, 0:free]
        if len(shape) == 3:
            ap = ap.rearrange("p (a b) -> p a b", b=shape[2])
        elif len(shape) == 4:
            ap = ap.rearrange("p (a b c) -> p a b c", b=shape[2], c=shape[3])
        self.uid += 1
        return T(ap, f"{name}_{self.uid}")


def build_program(npool):
    nc = bass.Bass("TRN2", target_bir_lowering=False)

    def din(name, shape, dt=F32):
        return nc.dram_tensor(name, shape, dt, kind="ExternalInput").ap()

    def dout(name, shape, dt=F32):
        return nc.dram_tensor(name, shape, dt, kind="ExternalOutput").ap()

    def dscr(name, shape, dt=F32):
        return T(nc.dram_tensor(name, shape, dt, kind="Internal").ap(), name)

    xs = din("xs", [NS * 128, D])
    rope = din("rope", [128, NS * 16])
    valid_d = din("valid", [128, NS])
    xsm = din("xsm", [16, D])
    rope_s = din("rope_s", [16, 16])
    csm = din("csm", [16, D])
    cpr = din("cpr", [1, D])
    cache_k = din("cache_k", [npool * 128, 512])
    cache_v = din("cache_v", [npool * 128, 512])
    ptab = din("ptab", [1, 256], I32)
    lane_d = din("lane", [128, 1])
    stconv = din("stconv", [16, 30, 512])
    w_ada = din("w_ada", [D, 6 * D])
    b_ada = din("b_ada", [1, 6 * D])
    w_in = din("w_in", [D, 2560])
    w_out = din("w_out", [D, D])
    w_f1 = din("w_f1", [D, 2 * DFF])
    w_f2 = din("w_f2", [DFF, D])
    g_mix = din("g_mix", [1, D])
    g_ffn = din("g_ffn", [1, D])
    g_fin = din("g_fin", [1, D])
    dw_w = din("dw_w", [31, 512])
    dw_b = din("dw_b", [1, 512])
    cln_g = din("cln_g", [1, 512])
    cln_b = din("cln_b", [1, 512])
    lam4 = din("lam4", [1, 256])
    subg = din("subg", [1, 128])
    sel_d = din("sel", [30, 256])

    y_o = dout("y_o", [NOWN * 128, D])
    k_o = dout("k_o", [NOWN * 128, 512])
    v_o = dout("v_o", [NOWN * 128, 512])
    cp_o = dout("cp_o", [30, 512])
    ys_o = dout("ys_o", [16, D])
    ks_o = dout("ks_o", [16, 512])
    vs_o = dout("vs_o", [16, 512])
    cs_o = dout("cs_o", [16, 30, 512])

    kT_scr = dscr("kT_scr", [4, 128, NS * 128], BF16)
    v_scr = dscr("v_scr", [4, 128, NS * 130], BF16)
    mod_p = dscr("mod_p", [6, 128, D])
    mod_s = dscr("mod_s", [6, 16, D])
    x1_scr = dscr("x1_scr", [17 * 128, D])
    qs_scr = dscr("qs_scr", [16, 512])

    NW = 52992
    with ExitStack() as st:
        S = Sched(nc, st)
        big = st.enter_context(nc.sbuf_tensor("big", [128, NW], F32))
        M = Mem(big, NW)

        def psum_scope():
            return ExitStack()

        def pbank(pst, name, dt=F32):
            n = 512 if dt == F32 else 1024
            return T(pst.enter_context(nc.psum_tensor(name, [128, n], dt))[:], name)

        cast_rr = [0]

        def load_cast(dst_t, dst_ap, src_ap, stg_t, shape3):
            view = stg_t.ap.rearrange("p (a b) -> p a b", b=shape3[2])[:, 0:shape3[1], :]
            S.dma("sp", lambda e: e.dma_start(out=view, in_=src_ap), writes=[stg_t])
            eng = ("act", "dve")[cast_rr[0] % 2]
            cast_rr[0] += 1
            if eng == "act":
                S.op("act", lambda e: e.copy(out=dst_ap, in_=view), reads=[stg_t], writes=[dst_t])
            elif eng == "dve":
                S.op("dve", lambda e: e.tensor_copy(out=dst_ap, in_=view), reads=[stg_t], writes=[dst_t])
            else:
                S.op("pool", lambda e: e.tensor_copy(out=dst_ap, in_=view), reads=[stg_t], writes=[dst_t])

        ident_f = M.alloc("ident_f", [128, 128], F32)
        ident = M.alloc("ident", [128, 128], BF16)
        tri_f = M.alloc("tri_f", [128, 128], F32)
        tri = M.alloc("tri", [128, 128], BF16)
        ones_bf = M.alloc("ones_bf", [128, 128], BF16)
        valid = M.alloc("valid", [128, NS], F32)
        lane = M.alloc("lane", [128, 1], F32)
        mhalf = M.alloc("mhalf", [128, 8], F32)
        neglam = M.alloc("neglam", [128, 1], F32)
        gsub = M.alloc("gsub", [128, 128], F32)
        lam2 = M.alloc("lam2", [128, 2], F32)
        cvec = M.alloc("cvec", [128, 4, 34], F32)

        Mt = Mem(big, NW)
        Mt.top = NW - 1024
        lamt = Mt.alloc("lamt", [128, 256], F32)
        cv34 = Mt.alloc("cv34", [128, 512], F32)
        lprod = Mt.alloc("lprod", [128, 128], F32)
        S.op("pool", lambda e: e.memset(ident_f.ap, 0.0), writes=[ident_f])
        S.op("pool", lambda e: e.affine_select(out=ident_f.ap, in_=ident_f.ap, pattern=[[-1, 128]],
                                               compare_op=ALU.not_equal, fill=1.0, base=0, channel_multiplier=1),
             reads=[ident_f], writes=[ident_f])
        S.op("dve", lambda e: e.tensor_copy(out=ident.ap, in_=ident_f.ap), reads=[ident_f], writes=[ident])
        S.op("pool", lambda e: e.memset(tri_f.ap, 1.0), writes=[tri_f])
        S.op("pool", lambda e: e.affine_select(out=tri_f.ap, in_=tri_f.ap, pattern=[[1, 128]],
                                               compare_op=ALU.is_ge, fill=0.0, base=0, channel_multiplier=-1),
             reads=[tri_f], writes=[tri_f])
        S.op("dve", lambda e: e.tensor_copy(out=tri.ap, in_=tri_f.ap), reads=[tri_f], writes=[tri])
        S.op("pool", lambda e: e.memset(ones_bf.ap, 1.0), writes=[ones_bf])
        S.op("pool", lambda e: e.memset(mhalf.ap, -0.5), writes=[mhalf])
        S.dma("sp", lambda e: e.dma_start(out=valid.ap, in_=valid_d), writes=[valid], group="const")
        S.dma("sp", lambda e: e.dma_start(out=lane.ap, in_=lane_d), writes=[lane], group="const")
        S.dma("sp", lambda e: e.dma_start(out=gsub.ap, in_=subg.to_broadcast([128, 128])), writes=[gsub], group="const")
        S.dma("sp", lambda e: e.dma_start(out=lamt.ap, in_=lam4.to_broadcast([128, 256])), writes=[lamt], group="const")
        S.dma("sp", lambda e: e.dma_start(out=cv34.ap[0:31, :], in_=dw_w), writes=[cv34], group="const")
        S.dma("sp", lambda e: e.dma_start(out=cv34.ap[31:32, :], in_=dw_b), writes=[cv34], group="const")
        S.dma("sp", lambda e: e.dma_start(out=cv34.ap[32:33, :], in_=cln_g), writes=[cv34], group="const")
        S.dma("sp", lambda e: e.dma_start(out=cv34.ap[33:34, :], in_=cln_b), writes=[cv34], group="const")
        with ExitStack() as pst0:
            pc0 = T(pst0.enter_context(nc.psum_tensor("pc0", [128, 512], F32))[:], "pc0")
            for ct in range(4):
                S.op("pe", lambda e, ct=ct: e.transpose(out=pc0.ap[:, ct * 34:(ct + 1) * 34], in_=cv34.ap[0:34, ct * 128:(ct + 1) * 128],
                                                        identity=ident_f.ap[0:34, 0:34]), reads=[cv34, ident_f], writes=[pc0], inc=(ct == 3))
            S.op("act", lambda e: e.copy(out=cvec.ap, in_=pc0.ap[:, 0:136].rearrange("p (c w) -> p c w", w=34)), reads=[pc0], writes=[cvec])
            S.barrier()
        S.op("dve", lambda e: e.tensor_scalar(out=gsub.ap, in0=gsub.ap, scalar1=1.0 - LAM_INIT, scalar2=None, op0=ALU.mult),
             reads=[gsub], writes=[gsub])
        S.op("dve", lambda e: e.tensor_tensor(out=lprod.ap.rearrange("p (a b) -> p a b", b=64),
                                              in0=lamt.ap.rearrange("p (a t b) -> p a t b", t=2, b=64)[:, :, 0, :],
                                              in1=lamt.ap.rearrange("p (a t b) -> p a t b", t=2, b=64)[:, :, 1, :],
                                              op=ALU.mult), reads=[lamt], writes=[lprod])
        S.op("dve", lambda e: e.tensor_reduce(out=lam2.ap, in_=lprod.ap.rearrange("p (a b) -> p a b", b=64),
                                              axis=AX.X, op=ALU.add), reads=[lprod], writes=[lam2])
        S.op("act", lambda e: e.activation(out=lam2.ap, in_=lam2.ap, func=AF.Exp), reads=[lam2], writes=[lam2])
        S.op("dve", lambda e: e.scalar_tensor_tensor(out=neglam.ap, in0=lam2.ap[:, 1:2], scalar=-LAM_INIT, in1=lam2.ap[:, 0:1],
                                                    op0=ALU.add, op1=ALU.subtract), reads=[lam2], writes=[neglam])

        S.barrier()
        QT = M.alloc("QT", [128, 4, 2048], BF16)
        convT = M.alloc("convT", [128, 4, 2048], BF16)
        catS = M.alloc("catS", [128, 8, 16], BF16)
        pre_win_mark = M.mark()
        win = M.alloc("win", [128, 8, 2560], BF16)
        w_in_v = w_in.rearrange("(k p) n -> p k n", p=128)
        base_mark = M.mark()

        with psum_scope() as pst:
            pA = [pbank(pst, f"pA{i}") for i in range(4)]
            ctile = M.alloc("ctile", [128, D], F32)
            cbf = M.alloc("cbf", [128, D], BF16)
            lT = M.alloc("lT", [128, 8, 17], BF16)
            lT_p = M.alloc("lT_p", [128, 8, 128], BF16)
            gm_bc = M.alloc("gm_bc", [128, D], F32)
            gf_bc = M.alloc("gf_bc", [128, D], F32)
            wblk = [M.alloc(f"wblk{i}", [128, 8, 512], BF16) for i in range(2)]
            wst = [M.alloc(f"wst{i}", [128, 8, 512], F32) for i in range(2)]
            bblk = [M.alloc(f"bblk{i}", [128, 512], F32) for i in range(2)]
            tp = [M.alloc(f"tp{i}", [128, 512], F32) for i in range(2)]
            tsm = [M.alloc(f"tsm{i}", [128, 512], F32) for i in range(2)]
            pTa = pbank(pst, "pTa", BF16)
            S.dma("sp", lambda e: e.dma_start(out=ctile.ap[0:1, :], in_=cpr), writes=[ctile], group="phA")
            S.dma("sp", lambda e: e.dma_start(out=ctile.ap[1:17, :], in_=csm), writes=[ctile], group="phA")
            S.dma("sp", lambda e: e.dma_start(out=gm_bc.ap, in_=g_mix.to_broadcast([128, D])), writes=[gm_bc], group="phA")
            S.dma("sp", lambda e: e.dma_start(out=gf_bc.ap, in_=g_ffn.to_broadcast([128, D])), writes=[gf_bc], group="phA")
            S.op("act", lambda e: e.activation(out=cbf.ap[0:17, :], in_=ctile.ap[0:17, :], func=AF.Silu), reads=[ctile], writes=[cbf])
            for kc in range(8):
                S.op("pe", lambda e, kc=kc: e.transpose(out=pTa.ap[:, kc * 32:kc * 32 + 17], in_=cbf.ap[0:17, kc * 128:(kc + 1) * 128],
                                                        identity=ident.ap[0:17, 0:17]), reads=[cbf, ident], writes=[pTa], inc=(kc == 7))
            S.op("act", lambda e: e.copy(out=lT.ap, in_=pTa.ap[:, 0:256].rearrange("p (c t) -> p c t", t=32)[:, :, 0:17]),
                 reads=[pTa], writes=[lT])
            S.op("dve", lambda e: e.tensor_copy(out=lT_p.ap, in_=lT.ap[:, :, 0:1].to_broadcast([128, 8, 128])),
                 reads=[lT], writes=[lT_p])
            w_ada_v = w_ada.rearrange("(k p) n -> p k n", p=128)
            def a_load_w(nb):
                c0 = nb * 512
                ws = wst[nb % 2]
                S.dma("sp", lambda e: e.dma_start(out=ws.ap, in_=w_ada_v[:, :, c0:c0 + 512]), writes=[ws])

            def a_load_b(nb):
                c0 = nb * 512
                bb = bblk[nb % 2]
                S.dma("sp", lambda e: e.dma_start(out=bb.ap, in_=b_ada[0:1, c0:c0 + 512].to_broadcast([128, 512])), writes=[bb])

            for nb0 in range(2):
                a_load_w(nb0)
                a_load_b(nb0)
            out_dmas = []
            for nb in range(12):
                wb = wblk[nb % 2]
                bb = bblk[nb % 2]
                ws = wst[nb % 2]
                if nb % 2 == 0:
                    S.op("act", lambda e, ws=ws, wb=wb: e.copy(out=wb.ap, in_=ws.ap), reads=[ws], writes=[wb])
                else:
                    S.op("dve", lambda e, ws=ws, wb=wb: e.tensor_copy(out=wb.ap, in_=ws.ap), reads=[ws], writes=[wb])
                if nb + 2 < 12:
                    a_load_w(nb + 2)
                pp = pA[(nb % 2) * 2]
                ps_ = pA[(nb % 2) * 2 + 1]
                for kc in range(8):
                    S.op("pe", lambda e, pp=pp, wb=wb, kc=kc: e.matmul(pp.ap, lhsT=lT_p.ap[:, kc, :], rhs=wb.ap[:, kc, :],
                                                                     start=(kc == 0), stop=(kc == 7)),
                         reads=[lT_p, wb], writes=[pp], inc=(kc == 7))
                for kc in range(8):
                    S.op("pe", lambda e, ps_=ps_, wb=wb, kc=kc: e.matmul(ps_.ap[0:16, :], lhsT=lT.ap[:, kc, 1:17], rhs=wb.ap[:, kc, :],
                                                                       start=(kc == 0), stop=(kc == 7)),
                         reads=[lT, wb], writes=[ps_], inc=(kc == 7))
                mi = nb // 2
                h0 = (nb % 2) * 512
                for (pz, stg, np_, scr) in ((pp, tp[nb % 2], 128, mod_p), (ps_, tsm[nb % 2], 16, mod_s)):
                    if mi in (1, 4):
                        gb = gm_bc if mi == 1 else gf_bc
                        S.op("dve", lambda e, pz=pz, stg=stg, bb=bb, np_=np_: e.scalar_tensor_tensor(
                            out=stg.ap[0:np_, :], in0=pz.ap[0:np_, :], scalar=1.0, in1=bb.ap[0:np_, :],
                            op0=ALU.add, op1=ALU.add), reads=[pz, bb], writes=[stg])
                        S.op("pool", lambda e, stg=stg, gb=gb, np_=np_, h0=h0: e.tensor_tensor(
                            out=stg.ap[0:np_, :], in0=stg.ap[0:np_, :], in1=gb.ap[0:np_, h0:h0 + 512], op=ALU.mult),
                            reads=[stg, gb], writes=[stg])
                    else:
                        S.op("dve", lambda e, pz=pz, stg=stg, bb=bb, np_=np_: e.tensor_tensor(
                            out=stg.ap[0:np_, :], in0=pz.ap[0:np_, :], in1=bb.ap[0:np_, :], op=ALU.add),
                            reads=[pz, bb], writes=[stg])
                    S.dma("sp", lambda e, stg=stg, np_=np_, scr=scr, mi=mi, h0=h0: e.dma_start(
                        out=scr.ap[mi, 0:np_, h0:h0 + 512], in_=stg.ap[0:np_, :]), reads=[stg])
                if nb + 2 < 12:
                    a_load_b(nb + 2)
            wstf = [T(w_.ap.rearrange("p a b -> p (a b)"), f"wstf{i_}") for i_, w_ in enumerate(wst)]
            for j in range(5):
                wstf[j % 2].b = wst[j % 2].b
                load_cast(win, win.ap[:, :, j * 512:(j + 1) * 512], w_in_v[:, :, j * 512:(j + 1) * 512], wstf[j % 2], (128, 8, 512))
            S.barrier()
        M.release(base_mark)

        own = {}
        halo = {}
        for m in range(4):
            halo[11 + 16 * m] = m
            for t in range(4):
                own[12 + 16 * m + t] = (m, t)

        with psum_scope() as pst:
            pT = [pbank(pst, "pT0", BF16)]
            pX2 = pbank(pst, "pX2", BF16)
            pK = pbank(pst, "pK")
            pV = pbank(pst, "pV")
            pQ = pbank(pst, "pQ")
            pCa = pbank(pst, "pCa")
            pCg = pbank(pst, "pCg")
            pX = pbank(pst, "pX", BF16)

            Am_p = M.alloc("Am_p", [128, D], F32)
            shm_p = M.alloc("shm_p", [128, D], F32)
            diag = M.alloc("diag", [128, 4, 31, 128], BF16)
            xb = [M.alloc(f"xb{i}", [128, D], F32) for i in range(2)]
            rope_all = M.alloc("rope_all", [128, NS, 16], F32)
            rp_s = M.alloc("rp_s", [128, 16], F32)
            ssq = [M.alloc(f"ssq{i}", [128, 1], F32) for i in range(2)]
            rstd = [M.alloc(f"rstd{i}", [128, 1], F32) for i in range(2)]
            hpre = M.alloc("hpre", [128, D], F32)
            hb = [M.alloc(f"hb{i}", [128, D], BF16) for i in range(2)]
            hT = [M.alloc(f"hT{i}", [128, 8, 128], BF16) for i in range(2)]
            k32 = [M.alloc(f"k32{i}", [128, 512], F32) for i in range(2)]
            v32 = [M.alloc(f"v32{i}", [128, 512], F32) for i in range(2)]
            q32 = M.alloc("q32", [128, 512], F32)
            sg32 = M.alloc("sg32", [128, 512], F32)
            u32 = [M.alloc(f"u32{i}", [128, 512], F32) for i in range(2)]
            rtmp = M.alloc("rtmp", [128, 4, 8, 8], F32)
            kb2 = [M.alloc(f"kb{i}", [128, 512], BF16) for i in range(2)]
            qb2 = [M.alloc(f"qb{i}", [128, 512], BF16) for i in range(2)]
            ub2 = [M.alloc(f"ub{i}", [128, 512], BF16) for i in range(2)]
            ub = ub2[0]
            bstat = [M.alloc(f"bstat{i}", [128, 12], F32) for i in range(2)]
            mv = [M.alloc(f"mv{i}", [128, 2], F32) for i in range(2)]
            kst = [M.alloc(f"kst{i}", [128, 4, 512], BF16) for i in range(2)]
            vst = [M.alloc(f"vst{i}", [128, 4, 4 * 130], BF16) for i in range(2)]
            uT = M.alloc("uT", [128, 4, 640], BF16)
            conv_mark = M.mark()
            ycv = M.alloc("ycv", [128, 4, 512], F32)
            ycb = M.alloc("ycb", [128, 4, 512], BF16)
            y2b = M.alloc("y2b", [128, 4, 512], BF16)
            mean = M.alloc("mean", [128, 512], F32)
            var = M.alloc("var", [128, 512], F32)
            ztmp = M.alloc("ztmp", [128, 512], F32)
            ztmp2 = M.alloc("ztmp2", [128, 512], F32)

            S.dma("sp", lambda e: e.dma_start(out=rope_all.ap, in_=rope.rearrange("p (s f) -> p s f", f=16)), writes=[rope_all])
            S.dma("sp", lambda e: e.dma_start(out=Am_p.ap, in_=mod_p.ap[1]), writes=[Am_p])
            S.dma("sp", lambda e: e.dma_start(out=shm_p.ap, in_=mod_p.ap[0]), writes=[shm_p])
            diag_jobs = [(ct, w) for ct in range(4) for w in range(31)]

            def build_diag(n):
                for _ in range(n):
                    if not diag_jobs:
                        return
                    ct, w = diag_jobs.pop(0)
                    S.op("dve", lambda e, ct=ct, w=w: e.tensor_scalar(out=diag.ap[:, ct, w, :], in0=ident_f.ap,
                                                                     scalar1=cvec.ap[:, ct, w:w + 1], scalar2=None,
                                                                     op0=ALU.mult),
                         reads=[ident_f, cvec], writes=[diag])

            def rmsnorm_h(np_, x_t, A_t, sh_t, par):
                sq, rs, hbt = ssq[par], rstd[par], hb[par]
                S.op("act", lambda e: e.activation(out=hpre.ap[0:np_, :], in_=x_t.ap[0:np_, :], func=AF.Square,
                                                   accum_out=sq.ap[0:np_, :]), reads=[x_t], writes=[hpre, sq])
                S.op("dve", lambda e: e.tensor_scalar(out=sq.ap[0:np_, :], in0=sq.ap[0:np_, :], scalar1=1.0 / D, scalar2=EPS,
                                                      op0=ALU.mult, op1=ALU.add), reads=[sq], writes=[sq])
                S.op("act", lambda e: e.activation(out=sq.ap[0:np_, :], in_=sq.ap[0:np_, :], func=AF.Sqrt), reads=[sq], writes=[sq])
                S.op("dve", lambda e: e.reciprocal(out=rs.ap[0:np_, :], in_=sq.ap[0:np_, :]), reads=[sq], writes=[rs])
                S.op("dve", lambda e: e.scalar_tensor_tensor(out=hpre.ap[0:np_, :], in0=x_t.ap[0:np_, :], scalar=rs.ap[0:np_, 0:1],
                                                            in1=A_t.ap[0:np_, :], op0=ALU.mult, op1=ALU.mult),
                     reads=[x_t, rs, A_t], writes=[hpre])
                S.op("pool", lambda e: e.tensor_tensor(out=hbt.ap[0:np_, :], in0=hpre.ap[0:np_, :], in1=sh_t.ap[0:np_, :],
                                                       op=ALU.add), reads=[hpre, sh_t], writes=[hbt])
                return hbt

            def transpose8(np_, src_bf, dst_t, dst_ap, pbankT, nchunk=8, copy_eng="act"):
                for kc in range(nchunk):
                    S.op("pe", lambda e, kc=kc: e.transpose(out=pbankT.ap[:, kc * 128:kc * 128 + np_],
                                                            in_=src_bf.ap[0:np_, kc * 128:(kc + 1) * 128],
                                                            identity=ident.ap[0:np_, 0:np_]),
                         reads=[src_bf, ident], writes=[pbankT], inc=(kc == nchunk - 1))
                src = pbankT.ap[:, 0:nchunk * 128].rearrange("p (c t) -> p c t", t=128)[:, :, 0:np_]
                if copy_eng == "act":
                    S.op("act", lambda e: e.copy(out=dst_ap, in_=src), reads=[pbankT], writes=[dst_t])
                else:
                    S.op("dve", lambda e: e.tensor_copy(out=dst_ap, in_=src), reads=[pbankT], writes=[dst_t])

            def proj(np_, hTt, j, pb):
                for kc in range(8):
                    S.op("pe", lambda e, kc=kc: e.matmul(pb.ap[0:np_, :], lhsT=hTt.ap[:, kc, 0:np_],
                                                         rhs=win.ap[:, kc, j * 512:(j + 1) * 512],
                                                         start=(kc == 0), stop=(kc == 7)),
                         reads=[hTt, win], writes=[pb], inc=(kc == 7))

            def rope_inplace(np_, t32, rp_t, rp_ap):
                v = t32.ap[0:np_, :].rearrange("p (g d) -> p g d", d=64)
                x1 = v[:, :, 0:8]
                x2 = v[:, :, 8:16]
                cos = rp_ap[0:np_, 0:8].unsqueeze(1).to_broadcast([np_, 8, 8])
                sin = rp_ap[0:np_, 8:16].unsqueeze(1).to_broadcast([np_, 8, 8])
                tm = rtmp.ap[0:np_]
                reng = "dve"
                S.op(reng, lambda e: e.tensor_tensor(out=tm[:, 0], in0=x1, in1=cos, op=ALU.mult), reads=[t32, rp_t], writes=[rtmp])
                S.op(reng, lambda e: e.tensor_tensor(out=tm[:, 1], in0=x2, in1=sin, op=ALU.mult), reads=[t32, rp_t], writes=[rtmp])
                S.op(reng, lambda e: e.tensor_tensor(out=tm[:, 2], in0=x2, in1=cos, op=ALU.mult), reads=[t32, rp_t], writes=[rtmp])
                S.op(reng, lambda e: e.tensor_tensor(out=tm[:, 3], in0=x1, in1=sin, op=ALU.mult), reads=[t32, rp_t], writes=[rtmp])
                S.op(reng, lambda e: e.tensor_tensor(out=x1, in0=tm[:, 0], in1=tm[:, 1], op=ALU.subtract), reads=[rtmp], writes=[t32])
                S.op(reng, lambda e: e.tensor_tensor(out=x2, in0=tm[:, 2], in1=tm[:, 3], op=ALU.add), reads=[rtmp], writes=[t32])

            def st0(s):
                xt = xb[s % 2]
                S.dma("sp", lambda e: e.dma_start(out=xt.ap, in_=xs[s * 128:(s + 1) * 128, :]), writes=[xt])

            def st1(s):
                xt, bs, mvt, sq = xb[s % 2], bstat[s % 2], mv[s % 2], ssq[s % 2]
                S.op("dve", lambda e: e.bn_stats(out=bs.ap[:, 0:6], in_=xt.ap[:, 0:512]), reads=[xt], writes=[bs])
                S.op("dve", lambda e: e.bn_stats(out=bs.ap[:, 6:12], in_=xt.ap[:, 512:1024]), reads=[xt], writes=[bs])
                S.op("dve", lambda e: e.bn_aggr(out=mvt.ap, in_=bs.ap), reads=[bs], writes=[mvt])
                S.op("dve", lambda e: e.scalar_tensor_tensor(out=sq.ap, in0=mvt.ap[:, 0:1], scalar=mvt.ap[:, 0:1], in1=mvt.ap[:, 1:2],
                                                            op0=ALU.mult, op1=ALU.add), reads=[mvt], writes=[sq])
                S.op("dve", lambda e: e.tensor_scalar(out=sq.ap, in0=sq.ap, scalar1=EPS, scalar2=None, op0=ALU.add),
                     reads=[sq], writes=[sq])

            def st2(s):
                xt, sq, rs, hbt = xb[s % 2], ssq[s % 2], rstd[s % 2], hb[s % 2]
                S.op("act", lambda e: e.activation(out=sq.ap, in_=sq.ap, func=AF.Sqrt), reads=[sq], writes=[sq])
                S.op("dve", lambda e: e.reciprocal(out=rs.ap, in_=sq.ap), reads=[sq], writes=[rs])
                S.op("dve", lambda e: e.scalar_tensor_tensor(out=hpre.ap, in0=xt.ap, scalar=rs.ap[:, 0:1], in1=Am_p.ap,
                                                            op0=ALU.mult, op1=ALU.mult), reads=[xt, rs, Am_p], writes=[hpre])
                S.op("pool", lambda e: e.tensor_tensor(out=hbt.ap, in0=hpre.ap, in1=shm_p.ap, op=ALU.add),
                     reads=[hpre, shm_p], writes=[hbt])

            def st3(s):
                transpose8(128, hb[s % 2], hT[s % 2], hT[s % 2].ap, pT[0])

            def st4(s):
                hTt, k3, v3 = hT[s % 2], k32[s % 2], v32[s % 2]
                proj(128, hTt, 3, pK)
                S.op("act", lambda e: e.copy(out=k3.ap, in_=pK.ap), reads=[pK], writes=[k3])
                proj(128, hTt, 4, pV)
                g4, grp = s % 4, s // 4
                vstg = vst[grp % 2]
                vdst = vstg.ap[:, :, g4 * 130:g4 * 130 + 129]
                S.op("act", lambda e: e.activation(out=vdst[:, :, 0:128], in_=pV.ap.rearrange("p (h d) -> p h d", d=128),
                                                   func=AF.Identity, scale=valid.ap[:, s:s + 1]), reads=[pV, valid], writes=[vstg])
                S.op("pool", lambda e: e.tensor_copy(
                    out=vdst[:, :, 128:129], in_=valid.ap[:, s:s + 1].unsqueeze(1).to_broadcast([128, 4, 1])),
                    reads=[valid], writes=[vstg])
                if s in own:
                    S.op("act", lambda e: e.copy(out=v3.ap, in_=pV.ap), reads=[pV], writes=[v3])
                if s in own:
                    proj(128, hTt, 2, pQ)
                    S.op("act", lambda e: e.copy(out=q32.ap, in_=pQ.ap), reads=[pQ], writes=[q32])
                if s in own or s in halo:
                    proj(128, hTt, 0, pCa)
                    proj(128, hTt, 1, pCg)
                    S.op("act", lambda e: e.activation(out=sg32.ap, in_=pCg.ap, func=AF.Sigmoid), reads=[pCg], writes=[sg32])

            def st5(s):
                k3, v3, kbt = k32[s % 2], v32[s % 2], kb2[s % 2]
                rpa = rope_all.ap[:, s, :]
                g4, grp = s % 4, s // 4
                vstg = vst[grp % 2]
                rope_inplace(128, k3, rope_all, rpa)
                S.op("dve", lambda e: e.tensor_copy(out=kbt.ap, in_=k3.ap), reads=[k3], writes=[kbt])
                if s in own:
                    i_own = own[s][0] * 4 + own[s][1]
                    S.dma("sp", lambda e: e.dma_start(out=k_o[i_own * 128:(i_own + 1) * 128, :], in_=k3.ap),
                          reads=[k3], semkey="D_ko", is_out=True)
                    S.dma("sp", lambda e: e.dma_start(out=v_o[i_own * 128:(i_own + 1) * 128, :], in_=v3.ap),
                          reads=[v3], semkey="D_vo", is_out=True)
                    qbt = qb2[s % 2]
                    rope_inplace(128, q32, rope_all, rpa)
                    S.op("dve", lambda e: e.tensor_copy(out=qbt.ap, in_=q32.ap), reads=[q32], writes=[qbt])
                if s in own or s in halo:
                    ut, ubt = u32[s % 2], ub2[s % 2]
                    S.op("dve", lambda e: e.scalar_tensor_tensor(out=ut.ap, in0=pCa.ap, scalar=valid.ap[:, s:s + 1], in1=sg32.ap,
                                                                op0=ALU.mult, op1=ALU.mult), reads=[pCa, sg32, valid], writes=[ut])
                    S.op("dve", lambda e: e.tensor_copy(out=ubt.ap, in_=ut.ap), reads=[ut], writes=[ubt])
                    if s in own and own[s] == (3, 3):
                        S.dma("sp", lambda e: e.dma_start(out=cp_o, in_=ut.ap[98:128, :]), reads=[ut], semkey="D_cpo", is_out=True)

            def st6(s):
                kbt = kb2[s % 2]
                g4, grp = s % 4, s // 4
                kstg = kst[grp % 2]
                for h in range(4):
                    S.op("pe", lambda e, h=h: e.transpose(out=pX.ap[:, h * 128:(h + 1) * 128], in_=kbt.ap[:, h * 128:(h + 1) * 128],
                                                          identity=ident.ap), reads=[kbt, ident], writes=[pX], inc=(h == 3))
                S.op("act", lambda e: e.copy(out=kstg.ap[:, :, g4 * 128:(g4 + 1) * 128],
                                             in_=pX.ap[:, 0:512].rearrange("p (h t) -> p h t", t=128)), reads=[pX], writes=[kstg])
                if s in own:
                    m, t = own[s]
                    qbt = qb2[s % 2]
                    for h in range(4):
                        S.op("pe", lambda e, h=h: e.transpose(out=pX2.ap[:, h * 128:(h + 1) * 128], in_=qbt.ap[:, h * 128:(h + 1) * 128],
                                                              identity=ident.ap), reads=[qbt, ident], writes=[pX2], inc=(h == 3))
                    tok0 = (m * 4 + t) * 128
                    S.op("act", lambda e: e.copy(out=QT.ap[:, :, tok0:tok0 + 128],
                                                 in_=pX2.ap[:, 0:512].rearrange("p (h t) -> p h t", t=128)), reads=[pX2], writes=[QT])

            def st7(s):
                g4, grp = s % 4, s // 4
                kstg, vstg = kst[grp % 2], vst[grp % 2]
                if g4 == 3:
                    S.dma("sp", lambda e: e.dma_start(
                        out=kT_scr.ap[:, :, grp * 512:(grp + 1) * 512].rearrange("h p t -> p h t"), in_=kstg.ap),
                        reads=[kstg], semkey=f"D_kst{grp % 2}")
                    S.dma("sp", lambda e: e.dma_start(
                        out=v_scr.ap[:, :, grp * 520:(grp + 1) * 520].rearrange("h p t -> p h t"), in_=vstg.ap),
                        reads=[vstg], semkey=f"D_vst{grp % 2}")
                if s in own or s in halo:
                    ubt = ub2[s % 2]
                    for ct in range(4):
                        S.op("pe", lambda e, ct=ct: e.transpose(out=pX.ap[:, ct * 128:(ct + 1) * 128],
                                                                in_=ubt.ap[:, ct * 128:(ct + 1) * 128], identity=ident.ap),
                             reads=[ubt, ident], writes=[pX], inc=(ct == 3))
                    pos = 0 if s in halo else (1 + own[s][1])
                    S.op("act", lambda e: e.copy(out=uT.ap[:, :, pos * 128:(pos + 1) * 128],
                                                 in_=pX.ap[:, 0:512].rearrange("p (c t) -> p c t", t=128)), reads=[pX], writes=[uT])

            def st8(s):
                if not (s in own and own[s][1] == 3):
                    return
                m = own[s][0]
                cbanks = [pK, pV, pQ, pCa]
                for ct in range(4):
                    pb = cbanks[ct]
                    for w in range(31):
                        S.op("pe", lambda e, ct=ct, w=w, pb=pb: e.matmul(pb.ap, lhsT=diag.ap[:, ct, w, :],
                                                                       rhs=uT.ap[:, ct, 98 + w:98 + w + 512],
                                                                       start=(w == 0), stop=(w == 30)),
                             reads=[diag, uT], writes=[pb], inc=(w == 30))
                    S.op("act", lambda e, ct=ct, pb=pb: e.activation(out=ycv.ap[:, ct, :], in_=pb.ap, func=AF.Identity,
                                                                   bias=cvec.ap[:, ct, 31:32]), reads=[pb, cvec], writes=[ycv])
                    S.op("dve", lambda e, ct=ct: e.tensor_copy(out=ycb.ap[:, ct, :], in_=ycv.ap[:, ct, :]), reads=[ycv], writes=[ycb])
                    S.op("pool", lambda e, ct=ct: e.tensor_tensor(out=y2b.ap[:, ct, :], in0=ycv.ap[:, ct, :], in1=ycv.ap[:, ct, :],
                                                                  op=ALU.mult), reads=[ycv], writes=[y2b])
                for ct in range(4):
                    S.op("pe", lambda e, ct=ct: e.matmul(pCg.ap, lhsT=ones_bf.ap, rhs=ycb.ap[:, ct, :],
                                                         start=(ct == 0), stop=(ct == 3)), reads=[ones_bf, ycb], writes=[pCg],
                         inc=(ct == 3))
                for ct in range(4):
                    S.op("pe", lambda e, ct=ct: e.matmul(pK.ap, lhsT=ones_bf.ap, rhs=y2b.ap[:, ct, :],
                                                         start=(ct == 0), stop=(ct == 3)), reads=[ones_bf, y2b], writes=[pK],
                         inc=(ct == 3))
                S.op("dve", lambda e: e.tensor_scalar(out=mean.ap, in0=pCg.ap, scalar1=1.0 / 512, scalar2=None, op0=ALU.mult),
                     reads=[pCg], writes=[mean])
                S.op("dve", lambda e: e.tensor_tensor(out=ztmp.ap, in0=mean.ap, in1=mean.ap, op=ALU.mult), reads=[mean], writes=[ztmp])
                S.op("dve", lambda e: e.scalar_tensor_tensor(out=var.ap, in0=pK.ap, scalar=1.0 / 512, in1=ztmp.ap,
                                                            op0=ALU.mult, op1=ALU.subtract), reads=[pK, ztmp], writes=[var])
                S.op("dve", lambda e: e.tensor_scalar(out=var.ap, in0=var.ap, scalar1=EPS, scalar2=None, op0=ALU.add),
                     reads=[var], writes=[var])
                S.op("act", lambda e: e.activation(out=var.ap, in_=var.ap, func=AF.Sqrt), reads=[var], writes=[var])
                S.op("dve", lambda e: e.reciprocal(out=var.ap, in_=var.ap), reads=[var], writes=[var])
                for ct in range(4):
                    zt = ztmp if ct % 2 == 0 else ztmp2
                    S.op("dve", lambda e, ct=ct, zt=zt: e.tensor_tensor(out=zt.ap, in0=ycv.ap[:, ct, :], in1=mean.ap, op=ALU.subtract),
                         reads=[ycv, mean], writes=[zt])
                    S.op("pool", lambda e, zt=zt: e.tensor_tensor(out=zt.ap, in0=zt.ap, in1=var.ap, op=ALU.mult),
                         reads=[zt, var], writes=[zt])
                    S.op("act", lambda e, ct=ct, m=m, zt=zt: e.activation(out=convT.ap[:, ct, m * 512:(m + 1) * 512], in_=zt.ap,
                                                                        func=AF.Silu, scale=cvec.ap[:, ct, 32:33],
                                                                        bias=cvec.ap[:, ct, 33:34]),
                         reads=[zt, cvec], writes=[convT])

            stages = [st0, st1, st2, st3, st4, st5, st6, st7, st8]
            for it in range(NS + len(stages) - 1):
                build_diag(12)
                for k in (2, 1, 0, 8, 7, 6, 5, 4, 3):
                    sl = it - k
                    if 0 <= sl < NS:
                        stages[k](sl)

            SP = 16
            xsb = xb[0]
            rps = rp_s
            S.dma("sp", lambda e: e.dma_start(out=xsb.ap[0:SP, :], in_=xsm), writes=[xsb])
            S.dma("sp", lambda e: e.dma_start(out=rps.ap[0:SP, :], in_=rope_s), writes=[rps])
            S.dma("sp", lambda e: e.dma_start(out=Am_p.ap[0:16, :], in_=mod_s.ap[1]), writes=[Am_p])
            S.dma("sp", lambda e: e.dma_start(out=shm_p.ap[0:16, :], in_=mod_s.ap[0]), writes=[shm_p])
            hbs = rmsnorm_h(SP, xsb, Am_p, shm_p, 0)
            hTs = hT[0]
            transpose8(SP, hbs, hTs, hTs.ap[:, :, 0:SP], pT[0])
            ks3, vs3 = k32[0], v32[0]
            proj(SP, hTs, 3, pK)
            S.op("act", lambda e: e.copy(out=ks3.ap[0:SP, :], in_=pK.ap[0:SP, :]), reads=[pK], writes=[ks3])
            rope_inplace(SP, ks3, rps, rps.ap)
            S.dma("sp", lambda e: e.dma_start(out=ks_o, in_=ks3.ap[0:SP, :]), reads=[ks3], semkey="D_kso", is_out=True)
            proj(SP, hTs, 4, pV)
            S.op("act", lambda e: e.copy(out=vs3.ap[0:SP, :], in_=pV.ap[0:SP, :]), reads=[pV], writes=[vs3])
            S.dma("sp", lambda e: e.dma_start(out=vs_o, in_=vs3.ap[0:SP, :]), reads=[vs3], semkey="D_vso", is_out=True)
            proj(SP, hTs, 2, pQ)
            S.op("act", lambda e: e.copy(out=q32.ap[0:SP, :], in_=pQ.ap[0:SP, :]), reads=[pQ], writes=[q32])
            rope_inplace(SP, q32, rps, rps.ap)
            S.dma("sp", lambda e: e.dma_start(out=qs_scr.ap, in_=q32.ap[0:SP, :]), reads=[q32])
            us3 = u32[0]
            proj(SP, hTs, 0, pCa)
            proj(SP, hTs, 1, pCg)
            S.op("act", lambda e: e.activation(out=sg32.ap[0:SP, :], in_=pCg.ap[0:SP, :], func=AF.Sigmoid), reads=[pCg], writes=[sg32])
            S.op("dve", lambda e: e.tensor_tensor(out=us3.ap[0:SP, :], in0=pCa.ap[0:SP, :], in1=sg32.ap[0:SP, :], op=ALU.mult),
                 reads=[pCa, sg32], writes=[us3])
            S.barrier()
            M.release(conv_mark)
            S.dma("sp", lambda e: e.dma_start(out=cs_o[:, 29, :], in_=us3.ap[0:SP, :]), reads=[us3], semkey="D_cso2", is_out=True)
            stc = [M.alloc("stc0", [128, 4, 512], F32)] * 2
            dwt = M.alloc("dwt", [128, 512], F32)
            selt = M.alloc("selt", [128, 16, 16], F32)
            dw30 = M.alloc("dw30", [128, 512], F32)
            dwb_bc = M.alloc("dwb_bc", [128, 512], F32)
            clg_bc = M.alloc("clg_bc", [128, 512], F32)
            clb_bc = M.alloc("clb_bc", [128, 512], F32)
            S.dma("sp", lambda e: e.dma_start(out=dwt.ap[0:30, :], in_=dw_w[0:30, :]), writes=[dwt], group="sconv")
            S.dma("sp", lambda e: e.dma_start(out=selt.ap[0:30], in_=sel_d.rearrange("w (a b) -> w a b", b=16)), writes=[selt], group="sconv")
            S.dma("sp", lambda e: e.dma_start(out=dw30.ap[0:SP, :], in_=dw_w[30:31, :].to_broadcast([SP, 512])), writes=[dw30], group="sconv")
            S.dma("sp", lambda e: e.dma_start(out=dwb_bc.ap[0:SP, :], in_=dw_b.to_broadcast([SP, 512])), writes=[dwb_bc], group="sconv")
            S.dma("sp", lambda e: e.dma_start(out=clg_bc.ap[0:SP, :], in_=cln_g.to_broadcast([SP, 512])), writes=[clg_bc], group="sconv")
            S.dma("sp", lambda e: e.dma_start(out=clb_bc.ap[0:SP, :], in_=cln_b.to_broadcast([SP, 512])), writes=[clb_bc], group="sconv")
            stv = stconv.rearrange("s w c -> w s c")
            for q4 in range(4):
                stq = stc[q4 % 2]
                S.dma("sp", lambda e, stq=stq, q4=q4: e.dma_start(out=stq.ap[0:30], in_=stv[:, q4 * 4:(q4 + 1) * 4, :]), writes=[stq])
                S.dma("sp", lambda e, stq=stq, q4=q4: e.dma_start(out=cs_o[q4 * 4:(q4 + 1) * 4, 0:29, :].rearrange("s w c -> w s c"),
                                                                  in_=stq.ap[1:30]), reads=[stq], semkey="D_cso1", is_out=True)
                S.op("dve", lambda e, stq=stq: e.tensor_tensor(out=stq.ap[0:30], in0=stq.ap[0:30],
                                                               in1=dwt.ap[0:30, :].unsqueeze(1).to_broadcast([30, 4, 512]), op=ALU.mult),
                     reads=[stq, dwt], writes=[stq])
                for s4 in range(4):
                    sm = q4 * 4 + s4
                    S.op("pe", lambda e, sm=sm, s4=s4, stq=stq: e.matmul(pK.ap[0:SP, :], lhsT=selt.ap[0:30, sm, :], rhs=stq.ap[0:30, s4, :],
                                                                       start=(sm == 0), stop=(sm == 15)),
                         reads=[selt, stq], writes=[pK], inc=(s4 == 3))
            cvs = M.alloc("cvs", [128, 512], F32)
            S.op("dve", lambda e: e.tensor_tensor(out=cvs.ap[0:SP, :], in0=us3.ap[0:SP, :], in1=dw30.ap[0:SP, :], op=ALU.mult),
                 reads=[us3, dw30], writes=[cvs])
            S.op("dve", lambda e: e.tensor_tensor(out=cvs.ap[0:SP, :], in0=cvs.ap[0:SP, :], in1=pK.ap[0:SP, :], op=ALU.add),
                 reads=[cvs, pK], writes=[cvs])
            S.op("dve", lambda e: e.tensor_tensor(out=cvs.ap[0:SP, :], in0=cvs.ap[0:SP, :], in1=dwb_bc.ap[0:SP, :], op=ALU.add),
                 reads=[cvs, dwb_bc], writes=[cvs])
            bst = M.alloc("bst", [128, 6], F32)
            bag = M.alloc("bag", [128, 2], F32)
            S.op("dve", lambda e: e.bn_stats(out=bst.ap[0:SP, :], in_=cvs.ap[0:SP, :]), reads=[cvs], writes=[bst])
            S.op("dve", lambda e: e.bn_aggr(out=bag.ap[0:SP, :], in_=bst.ap[0:SP, :]), reads=[bst], writes=[bag])
            S.op("dve", lambda e: e.tensor_scalar(out=bag.ap[0:SP, 1:2], in0=bag.ap[0:SP, 1:2], scalar1=EPS, scalar2=None, op0=ALU.add),
                 reads=[bag], writes=[bag])
            S.op("act", lambda e: e.activation(out=bag.ap[0:SP, 1:2], in_=bag.ap[0:SP, 1:2], func=AF.Sqrt), reads=[bag], writes=[bag])
            S.op("dve", lambda e: e.reciprocal(out=bag.ap[0:SP, 1:2], in_=bag.ap[0:SP, 1:2]), reads=[bag], writes=[bag])
            S.op("dve", lambda e: e.tensor_scalar(out=cvs.ap[0:SP, :], in0=cvs.ap[0:SP, :], scalar1=bag.ap[0:SP, 0:1],
                                                  scalar2=bag.ap[0:SP, 1:2], op0=ALU.subtract, op1=ALU.mult), reads=[cvs, bag], writes=[cvs])
            S.op("dve", lambda e: e.tensor_tensor(out=cvs.ap[0:SP, :], in0=cvs.ap[0:SP, :], in1=clg_bc.ap[0:SP, :], op=ALU.mult),
                 reads=[cvs, clg_bc], writes=[cvs])
            S.op("dve", lambda e: e.tensor_tensor(out=cvs.ap[0:SP, :], in0=cvs.ap[0:SP, :], in1=clb_bc.ap[0:SP, :], op=ALU.add),
                 reads=[cvs, clb_bc], writes=[cvs])
            S.op("act", lambda e: e.activation(out=ub.ap[0:SP, :], in_=cvs.ap[0:SP, :], func=AF.Silu), reads=[cvs], writes=[ub])
            transpose8(SP, ub, catS, catS.ap[:, 0:4, :], pX, nchunk=4)
            S.barrier()
        M.release(pre_win_mark)
        oT = M.alloc("oT", [128, 4, 2048], BF16)
        base_mark = M.mark()

        with psum_scope() as pst:
            pS = [pbank(pst, f"pS{i}") for i in range(3)]
            pOA = pbank(pst, "pOA")
            pOB = pbank(pst, "pOB")
            pXc = pbank(pst, "pXc", BF16)
            pSA = pbank(pst, "pSA")
            pSB = pbank(pst, "pSB")

            KTb = [M.alloc(f"KTb{i}", [128, NS * 128], BF16) for i in range(2)]
            Vb = [M.alloc(f"Vb{i}", [128, NS, 130], BF16) for i in range(2)]
            pb_ = [M.alloc(f"pb{i}", [128, 512], BF16) for i in range(3)]
            oacc = M.alloc("oacc", [128, 4, 129], F32)
            o1n = M.alloc("o1n", [128, 4, 128], F32)
            rden = M.alloc("rden", [128, 4], F32)
            ssn = M.alloc("ssn", [128, 4], F32)
            ojunk = M.alloc("ojunk", [128, 128], F32)
            obf = M.alloc("obf", [128, 4, 128], BF16)

            SP = 16
            qs32 = M.alloc("qs32", [128, 512], F32)
            ksm32 = M.alloc("ksm32", [128, 512], F32)
            vsm32 = M.alloc("vsm32", [128, 512], F32)
            vsa = M.alloc("vsa", [128, 4, 129], BF16)
            pself = M.alloc("pself", [128, 8], F32)
            lself = M.alloc("lself", [128, 4, 64], BF16)
            ptf = M.alloc("ptf", [128, 256], F32)
            pti = M.alloc("pti", [128, 256], I32)
            idx = M.alloc("idx", [128, 256], I32)
            qbc = [M.alloc(f"qbc{i}", [128, 512], F32) for i in range(2)]
            NG = 4
            kpg = [M.alloc(f"kpg{i}", [128, NG, 512], BF16) for i in range(4)]
            vpg = [M.alloc(f"vpg{i}", [128, NG, 512], BF16) for i in range(4)]
            prod = M.alloc("prod", [128, NG, 512], F32)
            scs = [M.alloc(f"scs{i}", [128, 16, 8], F32) for i in range(2)]
            Pz = [M.alloc(f"Pz{i}", [128, 16, 4, 64], BF16) for i in range(2)]
            osm = M.alloc("osm", [128, 4, 129], F32)
            osm2 = M.alloc("osm2", [128, 4, 129], F32)
            osn = M.alloc("osn", [128, 4, 128], F32)

            S.dma("sp", lambda e: e.dma_start(out=qs32.ap[0:SP, :], in_=qs_scr.ap), writes=[qs32], group="sattn")
            S.dma("sp", lambda e: e.dma_start(out=ksm32.ap[0:SP, :], in_=ks_o), writes=[ksm32], group="sattn")
            S.dma("sp", lambda e: e.dma_start(out=vsm32.ap[0:SP, :], in_=vs_o), writes=[vsm32], group="sattn")
            S.dma("sp", lambda e: e.dma_start(out=pti.ap, in_=ptab.to_broadcast([128, 256])), writes=[pti], group="sattn")
            S.op("dve", lambda e: e.tensor_copy(out=ptf.ap, in_=pti.ap), reads=[pti], writes=[ptf])
            S.op("dve", lambda e: e.tensor_scalar(out=ptf.ap, in0=ptf.ap, scalar1=128.0, scalar2=lane.ap[:, 0:1],
                                                  op0=ALU.mult, op1=ALU.add), reads=[ptf, lane], writes=[ptf])
            S.op("dve", lambda e: e.tensor_copy(out=idx.ap, in_=ptf.ap), reads=[ptf], writes=[idx])
            for i in range(2):
                S.op("pool", lambda e, i=i: e.memset(Pz[i].ap, 0.0), writes=[Pz[i]])
            S.op("dve", lambda e: e.memset(pSA.ap, 0.0), writes=[pSA])
            S.op("dve", lambda e: e.memset(pSB.ap, 0.0), writes=[pSB])
            S.op("dve", lambda e: e.tensor_tensor(out=prod.ap[0:SP, 0, :], in0=qs32.ap[0:SP, :], in1=ksm32.ap[0:SP, :], op=ALU.mult),
                 reads=[qs32, ksm32], writes=[prod])
            S.op("dve", lambda e: e.tensor_reduce(out=pself.ap[0:SP, :], in_=prod.ap[0:SP, 0, :].rearrange("p (g d) -> p g d", d=64),
                                                  axis=AX.X, op=ALU.add), reads=[prod], writes=[pself])
            S.op("act", lambda e: e.activation(out=pself.ap[0:SP, :], in_=pself.ap[0:SP, :], func=AF.Exp, scale=SCALE),
                 reads=[pself], writes=[pself])
            vsa2 = vsa.ap.rearrange("p h d -> p (h d)")[:, 0:512]
            S.op("dve", lambda e: e.tensor_copy(out=vsa2[0:SP, :], in_=vsm32.ap[0:SP, :]), reads=[vsm32], writes=[vsa])
            S.op("pool", lambda e: e.memset(lself.ap[0:SP], 0.0), writes=[lself])
            for h in range(4):
                for c in range(2):
                    S.op("dve", lambda e, h=h, c=c: e.tensor_scalar(out=lself.ap[0:SP, h, c * 32:c * 32 + 16], in0=ident_f.ap[0:SP, 0:SP],
                                                                    scalar1=pself.ap[0:SP, 2 * h + c:2 * h + c + 1], scalar2=None,
                                                                    op0=ALU.mult), reads=[ident_f, pself], writes=[lself])
            for hp in range(2):
                lt = lself.ap[0:SP, 2 * hp:2 * hp + 2, :].rearrange("p h x -> p (h x)")
                S.op("pe", lambda e, hp=hp, lt=lt: e.matmul(pSA.ap[:, hp * 256:(hp + 1) * 256], lhsT=lt,
                                                            rhs=vsa2[0:SP, hp * 256:(hp + 1) * 256], start=False, stop=False,
                                                            skip_group_check=True), reads=[lself, vsa], writes=[pSA], inc=False)
                S.op("pe", lambda e, hp=hp, lt=lt: e.matmul(pSB.ap[:, hp:hp + 1], lhsT=lt, rhs=ones_bf.ap[0:SP, 0:1], start=False, stop=False,
                                                            skip_group_check=True), reads=[lself, ones_bf], writes=[pSB], inc=(hp == 1))

            jobs = [(sm, g) for sm in range(16) for g in range(16 // NG)]

            def s_stage0(ji):
                sm, g = jobs[ji]
                if g == 0:
                    qb_t = qbc[sm % 2]
                    S.dma("sp", lambda e: e.dma_start(out=qb_t.ap, in_=qs_scr.ap[sm:sm + 1, :].to_broadcast([128, 512])),
                          writes=[qb_t])
                kg = kpg[ji % 4]
                vg = vpg[ji % 4]
                for pg in range(NG):
                    col = sm * 16 + g * NG + pg
                    S.dma("pool", lambda e, kg=kg, pg=pg, col=col: e.indirect_dma_start(
                        out=kg.ap[:, pg, :], out_offset=None, in_=cache_k,
                        in_offset=bass.IndirectOffsetOnAxis(ap=idx.ap[:, col:col + 1], axis=0)),
                        reads=[idx], writes=[kg], semkey=f"D_kpg{ji % 4}")
                    S.dma("pool", lambda e, vg=vg, pg=pg, col=col: e.indirect_dma_start(
                        out=vg.ap[:, pg, :], out_offset=None, in_=cache_v,
                        in_offset=bass.IndirectOffsetOnAxis(ap=idx.ap[:, col:col + 1], axis=0)),
                        reads=[idx], writes=[vg], semkey=f"D_vpg{ji % 4}")

            def s_stage1(ji):
                sm, g = jobs[ji]
                kg = kpg[ji % 4]
                qb_t = qbc[sm % 2]
                sc = scs[sm % 2]
                pz = Pz[sm % 2]
                S.op("dve", lambda e: e.tensor_tensor(
                    out=prod.ap, in0=kg.ap, in1=qb_t.ap.unsqueeze(1).to_broadcast([128, NG, 512]), op=ALU.mult),
                    reads=[kg, qb_t], writes=[prod])
                S.op("dve", lambda e: e.tensor_reduce(
                    out=sc.ap[:, g * NG:(g + 1) * NG, :], in_=prod.ap.rearrange("p n (g d) -> p n g d", d=64),
                    axis=AX.X, op=ALU.add), reads=[prod], writes=[sc])
                S.op("act", lambda e: e.activation(
                    out=pz.ap[:, g * NG:(g + 1) * NG, :, :].rearrange("p n h (c s) -> p n h c s", s=32)[:, :, :, :, sm],
                    in_=sc.ap[:, g * NG:(g + 1) * NG, :].rearrange("p n (h c) -> p n h c", c=2),
                    func=AF.Exp, scale=SCALE), reads=[sc], writes=[pz])

            def s_stage2(ji):
                sm, g = jobs[ji]
                vg = vpg[ji % 4]
                pz = Pz[sm % 2]
                for pg in range(NG):
                    for hp in range(2):
                        lt = pz.ap[:, g * NG + pg, 2 * hp:2 * hp + 2, :].rearrange("p h x -> p (h x)")
                        S.op("pe", lambda e, pg=pg, hp=hp, lt=lt: e.matmul(
                            pSA.ap[:, hp * 256:(hp + 1) * 256], lhsT=lt, rhs=vg.ap[:, pg, hp * 256:(hp + 1) * 256],
                            start=False, stop=False, skip_group_check=True), reads=[pz, vg], writes=[pSA], inc=False)
                        S.op("pe", lambda e, hp=hp, lt=lt: e.matmul(
                            pSB.ap[:, hp:hp + 1], lhsT=lt, rhs=ones_bf.ap[:, 0:1],
                            start=False, stop=False, skip_group_check=True), reads=[pz, ones_bf], writes=[pSB], inc=(hp == 1))
                if g == 16 // NG - 1:
                    S.op("pool", lambda e: e.memset(
                        pz.ap.rearrange("p n h (c s) -> p n h c s", s=32)[:, :, :, :, sm], 0.0), writes=[pz])

            stick = [0]

            def sample_tick():
                t = stick[0]
                stick[0] += 1
                if t < len(jobs):
                    s_stage0(t)
                if 0 <= t - 1 < len(jobs):
                    s_stage1(t - 1)
                if 0 <= t - 2 < len(jobs):
                    s_stage2(t - 2)

            NTICK = len(jobs) + 2

            tiles = []
            for h in range(4):
                for m in range(4):
                    for c in range(2):
                        nfull = 12 + 16 * m
                        for si in range(nfull + 4):
                            tiles.append(dict(h=h, m=m, c=c, si=si, t=(si - nfull if si >= nfull else 0), diag=(si >= nfull),
                                              first=(si == 0), last=(si == nfull + 3)))
            NT = len(tiles)
            LA = 2
            tick_every = max(1, NT // NTICK)
            deferred = []

            def later(step, delay, fn):
                deferred.append((step + delay, fn))

            def run_due(step):
                k = 0
                while k < len(deferred):
                    if deferred[k][0] <= step:
                        deferred.pop(k)[1]()
                    else:
                        k += 1

            def oslot(qb_i):
                return (pOA, pOA.ap[:, qb_i * 129:(qb_i + 1) * 129]) if qb_i < 3 else (pOB, pOB.ap[:, 0:129])

            loaded_h = [-1]
            qz_raw = ksm32.ap[:, 0:512].bitcast(BF16)
            qz_raw2 = vsm32.ap[:, 0:512].bitcast(BF16)
            qz = [[T(qz_raw[:, 0:512], "qz00"), T(qz_raw[:, 512:1024], "qz01")],
                  [T(qz_raw2[:, 0:512], "qz10"), T(qz_raw2[:, 512:1024], "qz11")]]
            for c_ in range(2):
                for p_ in range(2):
                    S.op("pool", lambda e, c_=c_, p_=p_: e.memset(qz[c_][p_].ap, 0.0),
                         writes=[qz[c_][p_], ksm32, vsm32, vsa, pself, prod])

            def emit_qk(i):
                tl = tiles[i]
                h, m, c, si, q0 = tl["h"], tl["m"], tl["c"], tl["si"], tl["t"] * 128
                if loaded_h[0] < h:
                    loaded_h[0] = h
                    KT_, Vh_ = KTb[h % 2], Vb[h % 2]
                    S.dma("sp", lambda e: e.dma_start(out=KT_.ap, in_=kT_scr.ap[h]), writes=[KT_])
                    S.dma("sp", lambda e: e.dma_start(out=Vh_.ap, in_=v_scr.ap[h].rearrange("p (s d) -> p s d", d=130)), writes=[Vh_])
                KT = KTb[h % 2]
                psb = pS[i % 3]
                gpar = (h * 4 + m) % 2
                if tl["first"] and c == 0:
                    for c_ in range(2):
                        S.op("pool", lambda e, c_=c_: e.tensor_copy(out=qz[c_][gpar].ap[64 * c_:64 * c_ + 64, :],
                                                                    in_=QT.ap[64 * c_:64 * c_ + 64, h, m * 512:(m + 1) * 512]),
                             reads=[QT], writes=[qz[c_][gpar]])
                qzt = qz[c][gpar]
                S.op("pe", lambda e: e.matmul(
                    psb.ap[:, q0:512], lhsT=KT.ap[:, si * 128:(si + 1) * 128],
                    rhs=qzt.ap[:, q0:512], start=True, stop=True),
                    reads=[KT, qzt], writes=[psb])

            def finish_group(step, h, m, c):
                def f0():
                    S.op("act", lambda e: e.copy(out=oacc.ap[:, 0:3, :], in_=pOA.ap[:, 0:387].rearrange("p (q d) -> p q d", d=129)),
                         reads=[pOA], writes=[oacc])
                    S.op("act", lambda e: e.copy(out=oacc.ap[:, 3, :], in_=pOB.ap[:, 0:129]), reads=[pOB], writes=[oacc])
                f0()

                def f1():
                    S.op("dve", lambda e: e.reciprocal(out=rden.ap, in_=oacc.ap[:, :, 128]), reads=[oacc], writes=[rden])
                    if c == 0:
                        S.op("dve", lambda e: e.tensor_tensor(out=o1n.ap, in0=oacc.ap[:, :, 0:128],
                                                              in1=rden.ap.unsqueeze(2).to_broadcast([128, 4, 128]), op=ALU.mult),
                             reads=[oacc, rden], writes=[o1n])
                    else:
                        S.op("dve", lambda e: e.tensor_tensor(out=oacc.ap[:, :, 0:128], in0=oacc.ap[:, :, 0:128],
                                                              in1=rden.ap.unsqueeze(2).to_broadcast([128, 4, 128]), op=ALU.mult),
                             reads=[oacc, rden], writes=[oacc])
                        S.op("dve", lambda e: e.scalar_tensor_tensor(out=o1n.ap, in0=oacc.ap[:, :, 0:128], scalar=neglam.ap[:, 0:1],
                                                                    in1=o1n.ap, op0=ALU.mult, op1=ALU.add),
                             reads=[oacc, neglam, o1n], writes=[o1n])
                later(step, 2, f1)
                if c == 0:
                    return

                def f2():
                    for qb_i in range(4):
                        S.op("act", lambda e, qb_i=qb_i: e.activation(out=ojunk.ap, in_=o1n.ap[:, qb_i, :], func=AF.Square,
                                                                      accum_out=ssn.ap[:, qb_i:qb_i + 1]),
                             reads=[o1n], writes=[ojunk, ssn])
                later(step, 4, f2)

                def f3():
                    S.op("dve", lambda e: e.tensor_scalar(out=ssn.ap, in0=ssn.ap, scalar1=1.0 / 128, scalar2=EPS,
                                                          op0=ALU.mult, op1=ALU.add), reads=[ssn], writes=[ssn])
                later(step, 6, f3)

                def f4():
                    S.op("act", lambda e: e.activation(out=ssn.ap, in_=ssn.ap, func=AF.Sqrt), reads=[ssn], writes=[ssn])
                later(step, 8, f4)

                def f5():
                    S.op("dve", lambda e: e.reciprocal(out=ssn.ap, in_=ssn.ap), reads=[ssn], writes=[ssn])
                    for qb_i in range(4):
                        S.op("dve", lambda e, qb_i=qb_i: e.scalar_tensor_tensor(
                            out=obf.ap[:, qb_i, :], in0=o1n.ap[:, qb_i, :], scalar=ssn.ap[:, qb_i:qb_i + 1], in1=gsub.ap,
                            op0=ALU.mult, op1=ALU.mult), reads=[o1n, ssn, gsub], writes=[obf])
                later(step, 10, f5)

                def f6():
                    for qb_i in range(4):
                        S.op("pe", lambda e, qb_i=qb_i: e.transpose(out=pXc.ap[:, qb_i * 128:(qb_i + 1) * 128], in_=obf.ap[:, qb_i, :],
                                                                    identity=ident.ap), reads=[obf, ident], writes=[pXc],
                             inc=(qb_i == 3))
                later(step, 13, f6)

                def f7():
                    S.op("dve", lambda e: e.tensor_copy(out=oT.ap[:, h, m * 512:(m + 1) * 512], in_=pXc.ap[:, 0:512]),
                         reads=[pXc], writes=[oT])
                later(step, 15, f7)

            def emit_tile(j):
                tl = tiles[j]
                h, m, c, si, t = tl["h"], tl["m"], tl["c"], tl["si"], tl["t"]
                q0 = t * 128
                Vh = Vb[h % 2]
                psb = pS[j % 3]
                pbt = pb_[j % 3]
                if tl["first"]:
                    S.op("dve", lambda e: e.memset(pOA.ap, 0.0), writes=[pOA])
                    S.op("dve", lambda e: e.memset(pOB.ap[:, 0:129], 0.0), writes=[pOB])
                S.op("act", lambda e: e.activation(out=pbt.ap[:, q0:512], in_=psb.ap[:, q0:512], func=AF.Exp, scale=SCALE),
                     reads=[psb], writes=[pbt])
                if tl["diag"]:
                    S.op("dve", lambda e: e.tensor_tensor(out=pbt.ap[:, q0:q0 + 128], in0=pbt.ap[:, q0:q0 + 128],
                                                          in1=tri.ap, op=ALU.mult), reads=[pbt, tri], writes=[pbt])
                for qb_i in range(t, 4):
                    bk, ap_ = oslot(qb_i)
                    S.op("pe", lambda e, ap_=ap_, qb_i=qb_i: e.matmul(
                        ap_, lhsT=pbt.ap[:, qb_i * 128:(qb_i + 1) * 128], rhs=Vh.ap[:, si, 0:129], start=False, stop=False,
                        skip_group_check=True), reads=[pbt, Vh], writes=[bk], inc=(qb_i == 3))
                if tl["last"]:
                    finish_group(j, h, m, c)

            for i in range(NT + LA):
                if i < NT:
                    emit_qk(i)
                j = i - LA
                if j >= 0:
                    emit_tile(j)
                    run_due(j)
                    if j % tick_every == 0 and stick[0] < NTICK:
                        sample_tick()
            while deferred:
                run_due(10 ** 9)
            while stick[0] < NTICK:
                sample_tick()
            for h in range(4):
                hp, hl = h // 2, h % 2
                for c in range(2):
                    r0 = hl * 64 + c * 32
                    dst = osm if c == 0 else osm2
                    eng = "act" if c == 0 else "dve"
                    src_n = pSA.ap[r0:r0 + SP, hp * 256 + hl * 128:hp * 256 + hl * 128 + 128]
                    src_d = pSB.ap[r0:r0 + SP, hp:hp + 1]
                    if eng == "act":
                        S.op("act", lambda e, dst=dst, h=h, src_n=src_n: e.copy(out=dst.ap[0:SP, h, 0:128], in_=src_n), reads=[pSA], writes=[dst])
                        S.op("act", lambda e, dst=dst, h=h, src_d=src_d: e.copy(out=dst.ap[0:SP, h, 128:129], in_=src_d), reads=[pSB], writes=[dst])
                    else:
                        S.op("dve", lambda e, dst=dst, h=h, src_n=src_n: e.tensor_copy(out=dst.ap[0:SP, h, 0:128], in_=src_n), reads=[pSA], writes=[dst])
                        S.op("dve", lambda e, dst=dst, h=h, src_d=src_d: e.tensor_copy(out=dst.ap[0:SP, h, 128:129], in_=src_d), reads=[pSB], writes=[dst])
            S.op("dve", lambda e: e.reciprocal(out=rden.ap[0:SP, :], in_=osm.ap[0:SP, :, 128]), reads=[osm], writes=[rden])
            S.op("dve", lambda e: e.tensor_tensor(out=osn.ap[0:SP], in0=osm.ap[0:SP, :, 0:128],
                                                  in1=rden.ap[0:SP].unsqueeze(2).to_broadcast([SP, 4, 128]), op=ALU.mult),
                 reads=[osm, rden], writes=[osn])
            S.op("dve", lambda e: e.reciprocal(out=rden.ap[0:SP, :], in_=osm2.ap[0:SP, :, 128]), reads=[osm2], writes=[rden])
            S.op("dve", lambda e: e.tensor_tensor(out=osm2.ap[0:SP, :, 0:128], in0=osm2.ap[0:SP, :, 0:128],
                                                  in1=rden.ap[0:SP].unsqueeze(2).to_broadcast([SP, 4, 128]), op=ALU.mult),
                 reads=[osm2, rden], writes=[osm2])
            S.op("dve", lambda e: e.scalar_tensor_tensor(out=osn.ap[0:SP], in0=osm2.ap[0:SP, :, 0:128], scalar=neglam.ap[0:SP, 0:1],
                                                        in1=osn.ap[0:SP], op0=ALU.mult, op1=ALU.add),
                 reads=[osm2, neglam, osn], writes=[osn])
            for h in range(4):
                S.op("act", lambda e, h=h: e.activation(out=ojunk.ap[0:SP, :], in_=osn.ap[0:SP, h, :], func=AF.Square,
                                                        accum_out=ssn.ap[0:SP, h:h + 1]), reads=[osn], writes=[ojunk, ssn])
            S.op("dve", lambda e: e.tensor_scalar(out=ssn.ap[0:SP, :], in0=ssn.ap[0:SP, :], scalar1=1.0 / 128, scalar2=EPS,
                                                  op0=ALU.mult, op1=ALU.add), reads=[ssn], writes=[ssn])
            S.op("act", lambda e: e.activation(out=ssn.ap[0:SP, :], in_=ssn.ap[0:SP, :], func=AF.Sqrt), reads=[ssn], writes=[ssn])
            S.op("dve", lambda e: e.reciprocal(out=ssn.ap[0:SP, :], in_=ssn.ap[0:SP, :]), reads=[ssn], writes=[ssn])
            for h in range(4):
                S.op("dve", lambda e, h=h: e.scalar_tensor_tensor(out=obf.ap[0:SP, h, :], in0=osn.ap[0:SP, h, :],
                                                                  scalar=ssn.ap[0:SP, h:h + 1], in1=gsub.ap[0:SP, :],
                                                                  op0=ALU.mult, op1=ALU.mult), reads=[osn, ssn, gsub], writes=[obf])
            for h in range(4):
                S.op("pe", lambda e, h=h: e.transpose(out=pXc.ap[:, h * 128:h * 128 + SP], in_=obf.ap[0:SP, h, :],
                                                      identity=ident.ap[0:SP, 0:SP]), reads=[obf, ident], writes=[pXc], inc=(h == 3))
            S.op("dve", lambda e: e.tensor_copy(out=catS.ap[:, 4:8, :],
                                                in_=pXc.ap[:, 0:512].rearrange("p (h t) -> p h t", t=128)[:, :, 0:SP]),
                 reads=[pXc], writes=[catS])
            S.barrier()
        M.release(base_mark)

        with psum_scope() as pst:
            pM = [pbank(pst, f"pM{i}") for i in range(4)]
            wo = M.alloc("wo", [128, 8, D], BF16)
            gtm_p = M.alloc("gtm_p", [128, D], F32)
            gtm_s = M.alloc("gtm_s", [128, D], F32)
            xd = [M.alloc(f"xd{i}", [128, D], F32) for i in range(2)]
            x1t = [M.alloc(f"x1t{i}", [128, D], F32) for i in range(2)]
            w_out_v = w_out.rearrange("(k p) n -> p k n", p=128)
            wos = [M.alloc(f"wos{i_}", [128, 4096], F32) for i_ in range(2)]
            for j in range(2):
                load_cast(wo, wo.ap[:, :, j * 512:(j + 1) * 512], w_out_v[:, :, j * 512:(j + 1) * 512], wos[j], (128, 8, 512))
            S.dma("sp", lambda e: e.dma_start(out=gtm_p.ap, in_=mod_p.ap[2]), writes=[gtm_p], group="d1mods")
            S.dma("sp", lambda e: e.dma_start(out=gtm_s.ap[0:16, :], in_=mod_s.ap[2]), writes=[gtm_s], group="d1mods")
            for i in range(17):
                np_ = 128 if i < 16 else 16
                par = i % 2
                xt = xd[par]
                x1 = x1t[par]
                if i < 16:
                    m, t = i // 4, i % 4
                    sl = 12 + 16 * m + t
                    S.dma("sp", lambda e, xt=xt, sl=sl: e.dma_start(out=xt.ap, in_=xs[sl * 128:(sl + 1) * 128, :]), writes=[xt])
                    gt = gtm_p
                else:
                    S.dma("sp", lambda e, xt=xt: e.dma_start(out=xt.ap[0:16, :], in_=xsm), writes=[xt])
                    gt = gtm_s
                for nb in range(2):
                    pm = pM[par * 2 + nb]
                    for fc in range(8):
                        if i < 16:
                            lt = (convT.ap[:, fc, i * 128:(i + 1) * 128] if fc < 4 else oT.ap[:, fc - 4, i * 128:(i + 1) * 128])
                            rd = convT if fc < 4 else oT
                        else:
                            lt = catS.ap[:, fc, :]
                            rd = catS
                        S.op("pe", lambda e, pm=pm, lt=lt, fc=fc, nb=nb, np_=np_: e.matmul(
                            pm.ap[0:np_, :], lhsT=lt, rhs=wo.ap[:, fc, nb * 512:(nb + 1) * 512], start=(fc == 0), stop=(fc == 7)),
                            reads=[rd, wo], writes=[pm], inc=(fc == 7))
                    S.op("dve", lambda e, pm=pm, x1=x1, gt=gt, nb=nb, np_=np_: e.tensor_tensor(
                        out=x1.ap[0:np_, nb * 512:(nb + 1) * 512], in0=pm.ap[0:np_, :], in1=gt.ap[0:np_, nb * 512:(nb + 1) * 512],
                        op=ALU.mult), reads=[pm, gt], writes=[x1])
                S.op("pool", lambda e, x1=x1, xt=xt, np_=np_: e.tensor_tensor(out=x1.ap[0:np_, :], in0=x1.ap[0:np_, :], in1=xt.ap[0:np_, :],
                                                                            op=ALU.add), reads=[x1, xt], writes=[x1])
                S.dma("sp", lambda e, x1=x1, i=i, np_=np_: e.dma_start(out=x1_scr.ap[i * 128:i * 128 + np_, :], in_=x1.ap[0:np_, :]),
                      reads=[x1], semkey=f"D_x1t{par}")
            S.barrier()
        M.release(0)

        with psum_scope() as pst:
            pT2 = pbank(pst, "pT2", BF16)
            pG = [pbank(pst, f"pG{i}") for i in range(2)]
            pU = [pbank(pst, f"pU{i}") for i in range(2)]
            pAT = pbank(pst, "pAT", BF16)
            pO = [pbank(pst, f"pO{i}") for i in range(2)]

            identb = M.alloc("identb", [128, 128], BF16)
            identf2 = M.alloc("identf2", [128, 128], F32)
            mh2 = M.alloc("mh2", [128, 1], F32)
            S.op("pool", lambda e: e.memset(identf2.ap, 0.0), writes=[identf2])
            S.op("pool", lambda e: e.affine_select(out=identf2.ap, in_=identf2.ap, pattern=[[-1, 128]],
                                                   compare_op=ALU.not_equal, fill=1.0, base=0, channel_multiplier=1),
                 reads=[identf2], writes=[identf2])
            S.op("dve", lambda e: e.tensor_copy(out=identb.ap, in_=identf2.ap), reads=[identf2], writes=[identb])
            S.op("pool", lambda e: e.memset(mh2.ap, -0.5), writes=[mh2])
            wf1 = M.alloc("wf1", [128, 8, 2 * DFF], BF16)
            wf2 = M.alloc("wf2", [128, 22, D], BF16)
            w_f1_v = w_f1.rearrange("(k p) n -> p k n", p=128)
            w_f2_v = w_f2.rearrange("(k p) n -> p k n", p=128)
            stg_mark = M.mark()
            wfs = [M.alloc(f"wfs{i_}", [128, 4096], F32) for i_ in range(3)]
            for j in range(11):
                load_cast(wf1, wf1.ap[:, :, j * 512:(j + 1) * 512], w_f1_v[:, :, j * 512:(j + 1) * 512], wfs[j % 3], (128, 8, 512))
            for j in range(6):
                k0, k1 = j * 4, min(22, j * 4 + 4)
                load_cast(wf2, wf2.ap[:, k0:k1, :], w_f2_v[:, k0:k1, :], wfs[(11 + j) % 3], (128, k1 - k0, 1024))
            S.barrier()
            M.release(stg_mark)
            Af_t = M.alloc("Af_t", [128, D], F32)
            shf_t = M.alloc("shf_t", [128, D], F32)
            gtf_t = M.alloc("gtf_t", [128, D], F32)
            gfin = M.alloc("gfin", [128, D], F32)
            x1b = [M.alloc(f"x1b{i}", [128, D], F32) for i in range(2)]
            x1e = M.alloc("x1e", [128, D], F32)
            sq2 = [M.alloc(f"sq2{i}", [128, 1], F32) for i in range(2)]
            rs2 = [M.alloc(f"rs2{i}", [128, 1], F32) for i in range(2)]
            sq3 = [M.alloc(f"sq3{i}", [128, 1], F32) for i in range(2)]
            rs3 = [M.alloc(f"rs3{i}", [128, 1], F32) for i in range(2)]
            hp2 = M.alloc("hp2", [128, D], F32)
            h2b = [M.alloc(f"h2b{i}", [128, D], BF16) for i in range(2)]
            h2T = [M.alloc(f"h2T{i}", [128, 8, 128], BF16) for i in range(2)]
            sgt = [M.alloc(f"sgt{i}", [128, 352], F32) for i in range(2)]
            actb = [M.alloc(f"actb{i}", [128, DFF], BF16) for i in range(2)]
            actT = [M.alloc(f"actT{i}", [128, 22, 128], BF16) for i in range(2)]
            x2t = [M.alloc(f"x2t{i}", [128, D], F32) for i in range(2)]

            S.dma("sp", lambda e: e.dma_start(out=Af_t.ap, in_=mod_p.ap[4]), writes=[Af_t], group="d2mods")
            S.dma("sp", lambda e: e.dma_start(out=shf_t.ap, in_=mod_p.ap[3]), writes=[shf_t], group="d2mods")
            S.dma("sp", lambda e: e.dma_start(out=gtf_t.ap, in_=mod_p.ap[5]), writes=[gtf_t], group="d2mods")
            S.dma("sp", lambda e: e.dma_start(out=gfin.ap, in_=g_fin.to_broadcast([128, D])), writes=[gfin], group="d2mods")

            NTL = 17

            def npof(i):
                return 128 if i < 16 else 16

            def e0(i):
                np_, x1 = npof(i), x1b[i % 2]
                S.dma("sp", lambda e: e.dma_start(out=x1.ap[0:np_, :], in_=x1_scr.ap[i * 128:i * 128 + np_, :]), writes=[x1])

            def e1(i):
                np_, x1, sq, jk = npof(i), x1b[i % 2], sq2[i % 2], h2b[i % 2]
                S.op("act", lambda e: e.activation(out=jk.ap[0:np_, :], in_=x1.ap[0:np_, :], func=AF.Square,
                                                   accum_out=sq.ap[0:np_, :]), reads=[x1], writes=[jk, sq])
                S.op("dve", lambda e: e.tensor_scalar(out=sq.ap[0:np_, :], in0=sq.ap[0:np_, :], scalar1=1.0 / D, scalar2=EPS,
                                                      op0=ALU.mult, op1=ALU.add), reads=[sq], writes=[sq])

            def e2(i):
                np_, x1, sq, rs, hbt = npof(i), x1b[i % 2], sq2[i % 2], rs2[i % 2], h2b[i % 2]
                if i == 16:
                    S.dma("sp", lambda e: e.dma_start(out=Af_t.ap[0:16, :], in_=mod_s.ap[4]), writes=[Af_t], semkey="D_afs")
                    S.dma("sp", lambda e: e.dma_start(out=shf_t.ap[0:16, :], in_=mod_s.ap[3]), writes=[shf_t], semkey="D_shs")
                S.op("act", lambda e: e.activation(out=sq.ap[0:np_, :], in_=sq.ap[0:np_, :], func=AF.Sqrt), reads=[sq], writes=[sq])
                S.op("dve", lambda e: e.reciprocal(out=rs.ap[0:np_, :], in_=sq.ap[0:np_, :]), reads=[sq], writes=[rs])
                S.op("dve", lambda e: e.scalar_tensor_tensor(out=hp2.ap[0:np_, :], in0=x1.ap[0:np_, :], scalar=rs.ap[0:np_, 0:1],
                                                            in1=Af_t.ap[0:np_, :], op0=ALU.mult, op1=ALU.mult),
                     reads=[x1, rs, Af_t], writes=[hp2])
                S.op("pool", lambda e: e.tensor_tensor(out=hbt.ap[0:np_, :], in0=hp2.ap[0:np_, :], in1=shf_t.ap[0:np_, :], op=ALU.add),
                     reads=[hp2, shf_t], writes=[hbt])

            def e3(i):
                np_, hbt, hTt = npof(i), h2b[i % 2], h2T[i % 2]
                for kc in range(8):
                    S.op("pe", lambda e, kc=kc: e.transpose(out=pT2.ap[:, kc * 128:kc * 128 + np_],
                                                            in_=hbt.ap[0:np_, kc * 128:(kc + 1) * 128],
                                                            identity=identb.ap[0:np_, 0:np_]),
                         reads=[hbt, identb], writes=[pT2], inc=(kc == 7))
                S.op("act", lambda e: e.copy(out=hTt.ap[:, :, 0:np_],
                                             in_=pT2.ap[:, 0:1024].rearrange("p (c t) -> p c t", t=128)[:, :, 0:np_]),
                     reads=[pT2], writes=[hTt])

            def e4(i):
                np_, hTt, ab = npof(i), h2T[i % 2], actb[i % 2]
                for blk in range(8):
                    pg_, pu_ = pG[blk % 2], pU[blk % 2]
                    c0 = blk * 352
                    for kc in range(8):
                        S.op("pe", lambda e, kc=kc, pg_=pg_, c0=c0: e.matmul(
                            pg_.ap[0:np_, 0:352], lhsT=hTt.ap[:, kc, 0:np_], rhs=wf1.ap[:, kc, c0:c0 + 352],
                            start=(kc == 0), stop=(kc == 7)), reads=[hTt, wf1], writes=[pg_], inc=(kc == 7))
                    for kc in range(8):
                        S.op("pe", lambda e, kc=kc, pu_=pu_, c0=c0: e.matmul(
                            pu_.ap[0:np_, 0:352], lhsT=hTt.ap[:, kc, 0:np_], rhs=wf1.ap[:, kc, DFF + c0:DFF + c0 + 352],
                            start=(kc == 0), stop=(kc == 7)), reads=[hTt, wf1], writes=[pu_], inc=(kc == 7))
                    sg = sgt[blk % 2]
                    S.op("act", lambda e, sg=sg, pg_=pg_: e.activation(out=sg.ap[0:np_, :], in_=pg_.ap[0:np_, 0:352], func=AF.Silu),
                         reads=[pg_], writes=[sg])
                    S.op("dve", lambda e, sg=sg, pu_=pu_, c0=c0: e.tensor_tensor(
                        out=ab.ap[0:np_, c0:c0 + 352], in0=pu_.ap[0:np_, 0:352], in1=sg.ap[0:np_, :], op=ALU.mult),
                        reads=[pu_, sg], writes=[ab])

            def e5(i):
                np_, ab, aT = npof(i), actb[i % 2], actT[i % 2]
                S.dma("sp", lambda e: e.dma_start(out=x1e.ap[0:np_, :], in_=x1_scr.ap[i * 128:i * 128 + np_, :]), writes=[x1e])
                if i == 16:
                    S.dma("sp", lambda e: e.dma_start(out=gtf_t.ap[0:16, :], in_=mod_s.ap[5]), writes=[gtf_t], semkey="D_gts")
                for r0 in range(0, 22, 8):
                    nr = min(8, 22 - r0)
                    for k2 in range(nr):
                        fc = r0 + k2
                        S.op("pe", lambda e, fc=fc, k2=k2: e.transpose(
                            out=pAT.ap[:, k2 * 128:k2 * 128 + np_], in_=ab.ap[0:np_, fc * 128:(fc + 1) * 128],
                            identity=identb.ap[0:np_, 0:np_]), reads=[ab, identb], writes=[pAT], inc=(k2 == nr - 1))
                    src = pAT.ap[:, 0:nr * 128].rearrange("p (c t) -> p c t", t=128)[:, :, 0:np_]
                    dst = aT.ap[:, r0:r0 + nr, 0:np_]
                    if (r0 // 8) % 2 == 0:
                        S.op("act", lambda e, src=src, dst=dst: e.copy(out=dst, in_=src), reads=[pAT], writes=[aT])
                    else:
                        S.op("dve", lambda e, src=src, dst=dst: e.tensor_copy(out=dst, in_=src), reads=[pAT], writes=[aT])

            def e6(i):
                np_, aT, x2 = npof(i), actT[i % 2], x2t[i % 2]
                for nb in range(2):
                    po = pO[nb]
                    for fc in range(22):
                        S.op("pe", lambda e, fc=fc, po=po, nb=nb: e.matmul(
                            po.ap[0:np_, :], lhsT=aT.ap[:, fc, 0:np_], rhs=wf2.ap[:, fc, nb * 512:(nb + 1) * 512],
                            start=(fc == 0), stop=(fc == 21)), reads=[aT, wf2], writes=[po], inc=(fc == 21))
                    S.op("dve", lambda e, po=po, nb=nb: e.tensor_tensor(
                        out=x2.ap[0:np_, nb * 512:(nb + 1) * 512], in0=po.ap[0:np_, :], in1=gtf_t.ap[0:np_, nb * 512:(nb + 1) * 512],
                        op=ALU.mult), reads=[po, gtf_t], writes=[x2])
                S.op("pool", lambda e: e.tensor_tensor(out=x2.ap[0:np_, :], in0=x2.ap[0:np_, :], in1=x1e.ap[0:np_, :], op=ALU.add),
                     reads=[x2, x1e], writes=[x2])

            def e7(i):
                np_, x2, sq = npof(i), x2t[i % 2], sq3[i % 2]
                S.op("act", lambda e: e.activation(out=hp2.ap[0:np_, :], in_=x2.ap[0:np_, :], func=AF.Square,
                                                   accum_out=sq.ap[0:np_, :]), reads=[x2], writes=[hp2, sq])
                S.op("dve", lambda e: e.tensor_scalar(out=sq.ap[0:np_, :], in0=sq.ap[0:np_, :], scalar1=1.0 / D, scalar2=EPS,
                                                      op0=ALU.mult, op1=ALU.add), reads=[sq], writes=[sq])

            def e8(i):
                np_, x2, sq, rs = npof(i), x2t[i % 2], sq3[i % 2], rs3[i % 2]
                S.op("act", lambda e: e.activation(out=sq.ap[0:np_, :], in_=sq.ap[0:np_, :], func=AF.Sqrt), reads=[sq], writes=[sq])
                S.op("dve", lambda e: e.reciprocal(out=rs.ap[0:np_, :], in_=sq.ap[0:np_, :]), reads=[sq], writes=[rs])
                S.op("dve", lambda e: e.scalar_tensor_tensor(out=x2.ap[0:np_, :], in0=x2.ap[0:np_, :], scalar=rs.ap[0:np_, 0:1],
                                                            in1=gfin.ap[0:np_, :], op0=ALU.mult, op1=ALU.mult),
                     reads=[x2, rs, gfin], writes=[x2])
                if i < 16:
                    S.dma("sp", lambda e: e.dma_start(out=y_o[i * 128:(i + 1) * 128, :], in_=x2.ap), reads=[x2],
                          semkey="D_yo", is_out=True)
                else:
                    S.dma("sp", lambda e: e.dma_start(out=ys_o, in_=x2.ap[0:16, :]), reads=[x2], semkey="D_yso", is_out=True)

            estages = [e0, e1, e2, e3, e4, e5, e6, e7, e8]
            for it in range(NTL + len(estages) - 1):
                for k in range(len(estages) - 1, -1, -1):
                    sl = it - k
                    if 0 <= sl < NTL:
                        estages[k](sl)

        need = {}
        for k, v in S.out_evs:
            need[k] = max(need.get(k, 0), v)
        S.prog["sp"].append((list(need.items()), None, None, 0))
        S.replay()
    return nc


_CACHE = {}


def _rope_table(pos):
    inv = (500000.0 ** (-np.arange(0, 16, 2, dtype=np.float32) / 16)).astype(np.float32)
    ang = pos.astype(np.float32)[:, None] * inv[None, :]
    return np.concatenate([np.cos(ang), np.sin(ang)], axis=1).astype(np.float32)


def kernel(x_prompt, x_sample, cache_k, cache_v, state_conv, page_table, c_prompt, c_sample,
           norm_mix_g, norm_ffn_g, norm_final_g, w_ada, b_ada, w_in, conv_dw_w, conv_dw_b,
           conv_ln_g, conv_ln_b, lambda_q1, lambda_k1, lambda_q2, lambda_k2, subln_g,
           w_out, w_ffn_in, w_ffn_out):
    f = lambda a: np.ascontiguousarray(np.asarray(a), dtype=np.float32)
    x_prompt = f(x_prompt)
    npool = int(np.asarray(cache_k).shape[1])
    past = int(np.asarray(page_table).shape[1]) * 128
    if npool not in _CACHE:
        _CACHE[npool] = build_program(npool)
    nc = _CACHE[npool]
    ck = f(cache_k).reshape(npool * 128, 512)
    cv = f(cache_v).reshape(npool * 128, 512)
    sel = np.zeros((30, 16, 16), np.float32)
    for s in range(16):
        sel[:, s, s] = 1.0
    shared = {
        "cache_k": ck, "cache_v": cv,
        "w_ada": f(w_ada)[0], "b_ada": f(b_ada), "w_in": f(w_in)[0], "w_out": f(w_out)[0],
        "w_f1": f(w_ffn_in)[0], "w_f2": f(w_ffn_out)[0],
        "g_mix": f(norm_mix_g), "g_ffn": f(norm_ffn_g), "g_fin": f(norm_final_g).reshape(1, D),
        "dw_w": f(conv_dw_w)[0], "dw_b": f(conv_dw_b), "cln_g": f(conv_ln_g), "cln_b": f(conv_ln_b),
        "lam4": np.concatenate([f(lambda_q1), f(lambda_k1), f(lambda_q2), f(lambda_k2)], axis=1),
        "subg": f(subln_g), "sel": sel.reshape(30, 256),
        "lane": np.arange(128, dtype=np.float32).reshape(128, 1),
        "rope_s": np.repeat(_rope_table(np.array([past])), 16, axis=0),
    }
    pt = np.asarray(page_table).astype(np.int32)
    in_maps = []
    tile_of = []
    for c in range(8):
        b, j = c // 4, c % 4
        npad = 12 - 4 * j
        xs = np.zeros((NS, 128, D), np.float32)
        pos = np.zeros((NS, 128), np.float32)
        val = np.zeros((128, NS), np.float32)
        xt = x_prompt[b].reshape(64, 128, D)
        gl = []
        for s in range(NS):
            g = s - npad
            gl.append(g)
            if g >= 0:
                xs[s] = xt[g]
                pos[s] = g * 128 + np.arange(128)
                val[:, s] = 1.0
        tile_of.append(gl)
        m = dict(shared)
        m.update({
            "xs": xs.reshape(NS * 128, D), "rope": np.ascontiguousarray(_rope_table(pos.reshape(-1)).reshape(NS, 128, 16).transpose(1, 0, 2)).reshape(128, NS * 16), "valid": val,
            "xsm": f(x_sample)[16 * c:16 * c + 16, 0, :], "csm": f(c_sample)[16 * c:16 * c + 16],
            "cpr": f(c_prompt)[b:b + 1], "ptab": pt[16 * c:16 * c + 16].reshape(1, 256),
            "stconv": f(state_conv)[0, 16 * c:16 * c + 16],
        })
        in_maps.append(m)
    res = run_bass_kernel_spmd(nc, in_maps, core_ids=list(range(8))).results
    B, SEQ = x_prompt.shape[0], x_prompt.shape[1]
    y_prompt = np.zeros((B, SEQ, D), np.float32)
    k_prompt = np.zeros((1, B, SEQ, 4, 128), np.float32)
    v_prompt = np.zeros((1, B, SEQ, 4, 128), np.float32)
    conv_prompt = np.zeros((1, B, 30, 512), np.float32)
    y_sample = np.zeros((128, 1, D), np.float32)
    k_sample = np.zeros((1, 128, 1, 4, 128), np.float32)
    v_sample = np.zeros((1, 128, 1, 4, 128), np.float32)
    conv_sample = np.zeros((1, 128, 30, 512), np.float32)
    for c in range(8):
        b, j = c // 4, c % 4
        r = res[c]
        for m in range(4):
            for t in range(4):
                i = m * 4 + t
                g = tile_of[c][12 + 16 * m + t]
                y_prompt[b, g * 128:(g + 1) * 128] = r["y_o"][i * 128:(i + 1) * 128]
                k_prompt[0, b, g * 128:(g + 1) * 128] = r["k_o"][i * 128:(i + 1) * 128].reshape(128, 4, 128)
                v_prompt[0, b, g * 128:(g + 1) * 128] = r["v_o"][i * 128:(i + 1) * 128].reshape(128, 4, 128)
        if j == 3:
            conv_prompt[0, b] = r["cp_o"]
        y_sample[16 * c:16 * c + 16, 0] = r["ys_o"]
        k_sample[0, 16 * c:16 * c + 16, 0] = r["ks_o"].reshape(16, 4, 128)
        v_sample[0, 16 * c:16 * c + 16, 0] = r["vs_o"].reshape(16, 4, 128)
        conv_sample[0, 16 * c:16 * c + 16] = r["cs_o"]
    return (y_prompt, y_sample, k_prompt, v_prompt, conv_prompt, k_sample, v_sample, conv_sample)
```

```python
import numpy as np
import ml_dtypes
from contextlib import ExitStack
import concourse.bass as bass
import concourse.mybir as mybir
from concourse.bass_utils import run_bass_kernel_spmd

F32 = mybir.dt.float32
BF16 = mybir.dt.bfloat16
I32 = mybir.dt.int32
AF = mybir.ActivationFunctionType
ALU = mybir.AluOpType
AX = mybir.AxisListType

ENGS = ("pe", "act", "dve", "pool", "sp")
D = 1024
NS = 64
NOWN = 16
DFF = 2816
EPS = 1e-6
LAM_INIT = 0.2
SCALE = 0.125
SENT = 1 << 30


class Buf:
    __slots__ = ("name", "w", "r")

    def __init__(self, name):
        self.name = name
        self.w = None
        self.r = []


class T:
    __slots__ = ("ap", "b")

    def __init__(self, ap, name):
        self.ap = ap
        self.b = Buf(name)


class Sched:
    def __init__(self, nc, stack):
        self.nc = nc
        self.stack = stack
        self.prog = {e: [] for e in ENGS}
        self.sems = {}
        self.cnt = {}
        self.seen = {e: {} for e in ENGS}
        for e in ENGS:
            self._sem("E_" + e)
        self.dma_rr = 0
        self.out_evs = []

    def _sem(self, key):
        if key not in self.sems:
            self.sems[key] = self.stack.enter_context(self.nc.semaphore(key))
            self.cnt[key] = 0
        return self.sems[key]

    def _deps(self, eng, reads, writes, ignore_key=None):
        need = {}

        def add(ev):
            if ev is None:
                return
            k, v = ev
            if k == ignore_key:
                return
            if need.get(k, 0) < v:
                need[k] = v
        for b in reads:
            add(b.w)
        for b in writes:
            add(b.w)
            for ev in b.r:
                add(ev)
        waits = []
        seen = self.seen[eng]
        for k, v in need.items():
            if seen.get(k, 0) < v:
                seen[k] = v
                waits.append((k, v))
        return waits

    @staticmethod
    def _bufs(xs):
        return [x.b if isinstance(x, T) else x for x in xs]

    def op(self, eng, fn, reads=(), writes=(), inc=True):
        reads = self._bufs(reads)
        writes = self._bufs(writes)
        key = "E_" + eng
        waits = self._deps(eng, reads, writes)
        if eng == "pe":
            waits = [(k, v) for (k, v) in waits if k != key]
        if inc:
            self.cnt[key] += 1
            ev = (key, self.cnt[key])
        else:
            ev = (key, self.cnt[key] + 1)
        self.prog[eng].append((waits, fn, key if inc else None, 1))
        for b in reads:
            b.r.append(ev)
        for b in writes:
            b.w = ev
            b.r = []
        return ev

    def dma(self, eng, fn, reads=(), writes=(), semkey=None, is_out=False, group=None):
        reads = self._bufs(reads)
        writes = self._bufs(writes)
        if group is not None:
            semkey = "G_" + group
        if semkey is None:
            semkey = "D_" + (writes[0].name if writes else reads[0].name)
        self._sem(semkey)
        waits = self._deps(eng, reads, writes, ignore_key=semkey)
        self.cnt[semkey] += 16
        ev = (semkey, SENT if group is not None else self.cnt[semkey])
        self.prog[eng].append((waits, fn, semkey, 16))
        for b in reads:
            b.r.append(ev)
        for b in writes:
            b.w = ev
            b.r = []
        if is_out:
            self.out_evs.append(ev)
        return ev

    def barrier(self):
        evs = [(k, v) for k, v in self.cnt.items() if v > 0]
        for e in ENGS:
            waits = []
            seen = self.seen[e]
            for k, v in evs:
                if e == "pe" and k == "E_pe":
                    continue
                if seen.get(k, 0) < v:
                    seen[k] = v
                    waits.append((k, v))
            if waits:
                self.prog[e].append((waits, None, None, 0))

    def replay(self):
        sems = self.sems
        prog = self.prog

        def run(name):
            def body(e):
                for waits, fn, inck, incv in prog[name]:
                    for k, v in waits:
                        e.wait_ge(sems[k], self.cnt[k] if v == SENT else v)
                    if fn is None:
                        continue
                    ins = fn(e)
                    if inck is not None:
                        ins.then_inc(sems[inck], incv)
            return body

        with self.nc.Block() as block:
            block.tensor(run("pe"))
            block.scalar(run("act"))
            block.vector(run("dve"))
            block.gpsimd(run("pool"))
            block.sync(run("sp"))


class Mem:
    def __init__(self, big, nwords):
        self.big = big
        self.n = nwords
        self.top = 0
        self.uid = 0

    def mark(self):
        return self.top

    def release(self, m):
        self.top = m

    def alloc(self, name, shape, dt, parts=128):
        free = int(np.prod(shape[1:]))
        words = free if dt in (F32, I32) else (free + 1) // 2
        words = (words + 7) // 8 * 8
        assert self.top + words <= self.n, f"SBUF overflow allocating {name}: {self.top}+{words}>{self.n}"
        ap = self.big[0:shape[0], self.top:self.top + words]
        self.top += words
        if dt == BF16:
            ap = ap.bitcast(BF16)[:, 0:free]
        elif dt == I32:
            ap = ap.bitcast(I32)[:, 0:free]
        else:
            ap = ap[:, 0:free]
        if len(shape) == 3:
            ap = ap.rearrange("p (a b) -> p a b", b=shape[2])
        elif len(shape) == 4:
            ap = ap.rearrange("p (a b c) -> p a b c", b=shape[2], c=shape[3])
        self.uid += 1
        return T(ap, f"{name}_{self.uid}")


def build_program(npool):
    nc = bass.Bass("TRN2", target_bir_lowering=False)

    def din(name, shape, dt=F32):
        return nc.dram_tensor(name, shape, dt, kind="ExternalInput").ap()

    def dout(name, shape, dt=F32):
        return nc.dram_tensor(name, shape, dt, kind="ExternalOutput").ap()

    def dscr(name, shape, dt=F32):
        return T(nc.dram_tensor(name, shape, dt, kind="Internal").ap(), name)

    xs = din("xs", [NS * 128, D])
    rope = din("rope", [128, NS * 16])
    valid_d = din("valid", [128, NS])
    xsm = din("xsm", [16, D])
    rope_s = din("rope_s", [16, 16])
    csm = din("csm", [16, D])
    cpr = din("cpr", [1, D])
    cache_k = din("cache_k", [npool * 128, 512])
    cache_v = din("cache_v", [npool * 128, 512])
    ptab = din("ptab", [1, 256], I32)
    lane_d = din("lane", [128, 1])
    stconv = din("stconv", [16, 30, 512])
    w_ada = din("w_ada", [D, 6 * D])
    b_ada = din("b_ada", [1, 6 * D])
    w_in = din("w_in", [D, 2560])
    w_out = din("w_out", [D, D])
    w_f1 = din("w_f1", [D, 2 * DFF])
    w_f2 = din("w_f2", [DFF, D])
    g_mix = din("g_mix", [1, D])
    g_ffn = din("g_ffn", [1, D])
    g_fin = din("g_fin", [1, D])
    dw_w = din("dw_w", [31, 512])
    dw_b = din("dw_b", [1, 512])
    cln_g = din("cln_g", [1, 512])
    cln_b = din("cln_b", [1, 512])
    lam4 = din("lam4", [1, 256])
    subg = din("subg", [1, 128])
    sel_d = din("sel", [30, 256])

    y_o = dout("y_o", [NOWN * 128, D])
    k_o = dout("k_o", [NOWN * 128, 512])
    v_o = dout("v_o", [NOWN * 128, 512])
    cp_o = dout("cp_o", [30, 512])
    ys_o = dout("ys_o", [16, D])
    ks_o = dout("ks_o", [16, 512])
    vs_o = dout("vs_o", [16, 512])
    cs_o = dout("cs_o", [16, 30, 512])

    kT_scr = dscr("kT_scr", [4, 128, NS * 128], BF16)
    v_scr = dscr("v_scr", [4, 128, NS * 130], BF16)
    mod_p = dscr("mod_p", [6, 128, D])
    mod_s = dscr("mod_s", [6, 16, D])
    x1_scr = dscr("x1_scr", [17 * 128, D])
    qs_scr = dscr("qs_scr", [16, 512])

    NW = 52992
    with ExitStack() as st:
        S = Sched(nc, st)
        big = st.enter_context(nc.sbuf_tensor("big", [128, NW], F32))
        M = Mem(big, NW)

        def psum_scope():
            return ExitStack()

        def pbank(pst, name, dt=F32):
            n = 512 if dt == F32 else 1024
            return T(pst.enter_context(nc.psum_tensor(name, [128, n], dt))[:], name)

        cast_rr = [0]

        def load_cast(dst_t, dst_ap, src_ap, stg_t, shape3):
            view = stg_t.ap.rearrange("p (a b) -> p a b", b=shape3[2])[:, 0:shape3[1], :]
            S.dma("sp", lambda e: e.dma_start(out=view, in_=src_ap), writes=[stg_t])
            eng = ("act", "dve")[cast_rr[0] % 2]
            cast_rr[0] += 1
            if eng == "act":
                S.op("act", lambda e: e.copy(out=dst_ap, in_=view), reads=[stg_t], writes=[dst_t])
            elif eng == "dve":
                S.op("dve", lambda e: e.tensor_copy(out=dst_ap, in_=view), reads=[stg_t], writes=[dst_t])
            else:
                S.op("pool", lambda e: e.tensor_copy(out=dst_ap, in_=view), reads=[stg_t], writes=[dst_t])

        ident_f = M.alloc("ident_f", [128, 128], F32)
        ident = M.alloc("ident", [128, 128], BF16)
        tri_f = M.alloc("tri_f", [128, 128], F32)
        tri = M.alloc("tri", [128, 128], BF16)
        ones_bf = M.alloc("ones_bf", [128, 128], BF16)
        valid = M.alloc("valid", [128, NS], F32)
        lane = M.alloc("lane", [128, 1], F32)
        mhalf = M.alloc("mhalf", [128, 8], F32)
        neglam = M.alloc("neglam", [128, 1], F32)
        gsub = M.alloc("gsub", [128, 128], F32)
        lam2 = M.alloc("lam2", [128, 2], F32)
        cvec = M.alloc("cvec", [128, 4, 34], F32)

        Mt = Mem(big, NW)
        Mt.top = NW - 1024
        lamt = Mt.alloc("lamt", [128, 256], F32)
        cv34 = Mt.alloc("cv34", [128, 512], F32)
        lprod = Mt.alloc("lprod", [128, 128], F32)
        S.op("pool", lambda e: e.memset(ident_f.ap, 0.0), writes=[ident_f])
        S.op("pool", lambda e: e.affine_select(out=ident_f.ap, in_=ident_f.ap, pattern=[[-1, 128]],
                                               compare_op=ALU.not_equal, fill=1.0, base=0, channel_multiplier=1),
             reads=[ident_f], writes=[ident_f])
        S.op("dve", lambda e: e.tensor_copy(out=ident.ap, in_=ident_f.ap), reads=[ident_f], writes=[ident])
        S.op("pool", lambda e: e.memset(tri_f.ap, 1.0), writes=[tri_f])
        S.op("pool", lambda e: e.affine_select(out=tri_f.ap, in_=tri_f.ap, pattern=[[1, 128]],
                                               compare_op=ALU.is_ge, fill=0.0, base=0, channel_multiplier=-1),
             reads=[tri_f], writes=[tri_f])
        S.op("dve", lambda e: e.tensor_copy(out=tri.ap, in_=tri_f.ap), reads=[tri_f], writes=[tri])
        S.op("pool", lambda e: e.memset(ones_bf.ap, 1.0), writes=[ones_bf])
        S.op("pool", lambda e: e.memset(mhalf.ap, -0.5), writes=[mhalf])
        S.dma("sp", lambda e: e.dma_start(out=valid.ap, in_=valid_d), writes=[valid], group="const")
        S.dma("sp", lambda e: e.dma_start(out=lane.ap, in_=lane_d), writes=[lane], group="const")
        S.dma("sp", lambda e: e.dma_start(out=gsub.ap, in_=subg.to_broadcast([128, 128])), writes=[gsub], group="const")
        S.dma("sp", lambda e: e.dma_start(out=lamt.ap, in_=lam4.to_broadcast([128, 256])), writes=[lamt], group="const")
        S.dma("sp", lambda e: e.dma_start(out=cv34.ap[0:31, :], in_=dw_w), writes=[cv34], group="const")
        S.dma("sp", lambda e: e.dma_start(out=cv34.ap[31:32, :], in_=dw_b), writes=[cv34], group="const")
        S.dma("sp", lambda e: e.dma_start(out=cv34.ap[32:33, :], in_=cln_g), writes=[cv34], group="const")
        S.dma("sp", lambda e: e.dma_start(out=cv34.ap[33:34, :], in_=cln_b), writes=[cv34], group="const")
        with ExitStack() as pst0:
            pc0 = T(pst0.enter_context(nc.psum_tensor("pc0", [128, 512], F32))[:], "pc0")
            for ct in range(4):
                S.op("pe", lambda e, ct=ct: e.transpose(out=pc0.ap[:, ct * 34:(ct + 1) * 34], in_=cv34.ap[0:34, ct * 128:(ct + 1) * 128],
                                                        identity=ident_f.ap[0:34, 0:34]), reads=[cv34, ident_f], writes=[pc0], inc=(ct == 3))
            S.op("act", lambda e: e.copy(out=cvec.ap, in_=pc0.ap[:, 0:136].rearrange("p (c w) -> p c w", w=34)), reads=[pc0], writes=[cvec])
            S.barrier()
        S.op("dve", lambda e: e.tensor_scalar(out=gsub.ap, in0=gsub.ap, scalar1=1.0 - LAM_INIT, scalar2=None, op0=ALU.mult),
             reads=[gsub], writes=[gsub])
        S.op("dve", lambda e: e.tensor_tensor(out=lprod.ap.rearrange("p (a b) -> p a b", b=64),
                                              in0=lamt.ap.rearrange("p (a t b) -> p a t b", t=2, b=64)[:, :, 0, :],
                                              in1=lamt.ap.rearrange("p (a t b) -> p a t b", t=2, b=64)[:, :, 1, :],
                                              op=ALU.mult), reads=[lamt], writes=[lprod])
        S.op("dve", lambda e: e.tensor_reduce(out=lam2.ap, in_=lprod.ap.rearrange("p (a b) -> p a b", b=64),
                                              axis=AX.X, op=ALU.add), reads=[lprod], writes=[lam2])
        S.op("act", lambda e: e.activation(out=lam2.ap, in_=lam2.ap, func=AF.Exp), reads=[lam2], writes=[lam2])
        S.op("dve", lambda e: e.scalar_tensor_tensor(out=neglam.ap, in0=lam2.ap[:, 1:2], scalar=-LAM_INIT, in1=lam2.ap[:, 0:1],
                                                    op0=ALU.add, op1=ALU.subtract), reads=[lam2], writes=[neglam])

        S.barrier()
        QT = M.alloc("QT", [128, 4, 2048], BF16)
        convT = M.alloc("convT", [128, 4, 2048], BF16)
        catS = M.alloc("catS", [128, 8, 16], BF16)
        pre_win_mark = M.mark()
        win = M.alloc("win", [128, 8, 2560], BF16)
        w_in_v = w_in.rearrange("(k p) n -> p k n", p=128)
        diag = M.alloc("diag", [128, 4, 31, 128], BF16)
        for ct in range(4):
            for w in range(31):
                S.op("act", lambda e, ct=ct, w=w: e.activation(out=diag.ap[:, ct, w, :], in_=ident_f.ap, func=AF.Identity,
                                                               scale=cvec.ap[:, ct, w:w + 1]),
                     reads=[ident_f, cvec], writes=[diag])
        base_mark = M.mark()

        with psum_scope() as pst:
            pA = [pbank(pst, f"pA{i}") for i in range(4)]
            ctile = M.alloc("ctile", [128, D], F32)
            cbf = M.alloc("cbf", [128, D], BF16)
            lT = M.alloc("lT", [128, 8, 17], BF16)
            lT_p = M.alloc("lT_p", [128, 8, 128], BF16)
            gm_bc = M.alloc("gm_bc", [128, D], F32)
            gf_bc = M.alloc("gf_bc", [128, D], F32)
            wblk = [M.alloc(f"wblk{i}", [128, 8, 512], BF16) for i in range(2)]
            wst = [M.alloc(f"wst{i}", [128, 8, 512], F32) for i in range(2)]
            bblk = [M.alloc(f"bblk{i}", [128, 512], F32) for i in range(2)]
            tp = [M.alloc(f"tp{i}", [128, 512], F32) for i in range(2)]
            tsm = [M.alloc(f"tsm{i}", [128, 512], F32) for i in range(2)]
            pTa = pbank(pst, "pTa", BF16)
            S.dma("sp", lambda e: e.dma_start(out=ctile.ap[0:1, :], in_=cpr), writes=[ctile], group="phA")
            S.dma("sp", lambda e: e.dma_start(out=ctile.ap[1:17, :], in_=csm), writes=[ctile], group="phA")
            S.dma("sp", lambda e: e.dma_start(out=gm_bc.ap, in_=g_mix.to_broadcast([128, D])), writes=[gm_bc], group="phA")
            S.dma("sp", lambda e: e.dma_start(out=gf_bc.ap, in_=g_ffn.to_broadcast([128, D])), writes=[gf_bc], group="phA")
            S.op("act", lambda e: e.activation(out=cbf.ap[0:17, :], in_=ctile.ap[0:17, :], func=AF.Silu), reads=[ctile], writes=[cbf])
            for kc in range(8):
                S.op("pe", lambda e, kc=kc: e.transpose(out=pTa.ap[:, kc * 32:kc * 32 + 17], in_=cbf.ap[0:17, kc * 128:(kc + 1) * 128],
                                                        identity=ident.ap[0:17, 0:17]), reads=[cbf, ident], writes=[pTa], inc=(kc == 7))
            S.op("act", lambda e: e.copy(out=lT.ap, in_=pTa.ap[:, 0:256].rearrange("p (c t) -> p c t", t=32)[:, :, 0:17]),
                 reads=[pTa], writes=[lT])
            S.op("dve", lambda e: e.tensor_copy(out=lT_p.ap, in_=lT.ap[:, :, 0:1].to_broadcast([128, 8, 128])),
                 reads=[lT], writes=[lT_p])
            w_ada_v = w_ada.rearrange("(k p) n -> p k n", p=128)
            def a_load_w(nb):
                c0 = nb * 512
                ws = wst[nb % 2]
                S.dma("sp", lambda e: e.dma_start(out=ws.ap, in_=w_ada_v[:, :, c0:c0 + 512]), writes=[ws])

            def a_load_b(nb):
                c0 = nb * 512
                bb = bblk[nb % 2]
                S.dma("sp", lambda e: e.dma_start(out=bb.ap, in_=b_ada[0:1, c0:c0 + 512].to_broadcast([128, 512])), writes=[bb])

            for nb0 in range(2):
                a_load_w(nb0)
                a_load_b(nb0)
            out_dmas = []
            for nb in range(12):
                wb = wblk[nb % 2]
                bb = bblk[nb % 2]
                ws = wst[nb % 2]
                if nb % 2 == 0:
                    S.op("act", lambda e, ws=ws, wb=wb: e.copy(out=wb.ap, in_=ws.ap), reads=[ws], writes=[wb])
                else:
                    S.op("dve", lambda e, ws=ws, wb=wb: e.tensor_copy(out=wb.ap, in_=ws.ap), reads=[ws], writes=[wb])
                if nb + 2 < 12:
                    a_load_w(nb + 2)
                pp = pA[(nb % 2) * 2]
                ps_ = pA[(nb % 2) * 2 + 1]
                for kc in range(8):
                    S.op("pe", lambda e, pp=pp, wb=wb, kc=kc: e.matmul(pp.ap, lhsT=lT_p.ap[:, kc, :], rhs=wb.ap[:, kc, :],
                                                                     start=(kc == 0), stop=(kc == 7)),
                         reads=[lT_p, wb], writes=[pp], inc=(kc == 7))
                for kc in range(8):
                    S.op("pe", lambda e, ps_=ps_, wb=wb, kc=kc: e.matmul(ps_.ap[0:16, :], lhsT=lT.ap[:, kc, 1:17], rhs=wb.ap[:, kc, :],
                                                                       start=(kc == 0), stop=(kc == 7)),
                         reads=[lT, wb], writes=[ps_], inc=(kc == 7))
                mi = nb // 2
                h0 = (nb % 2) * 512
                for (pz, stg, np_, scr) in ((pp, tp[nb % 2], 128, mod_p), (ps_, tsm[nb % 2], 16, mod_s)):
                    if mi in (1, 4):
                        gb = gm_bc if mi == 1 else gf_bc
                        S.op("dve", lambda e, pz=pz, stg=stg, bb=bb, np_=np_: e.scalar_tensor_tensor(
                            out=stg.ap[0:np_, :], in0=pz.ap[0:np_, :], scalar=1.0, in1=bb.ap[0:np_, :],
                            op0=ALU.add, op1=ALU.add), reads=[pz, bb], writes=[stg])
                        S.op("pool", lambda e, stg=stg, gb=gb, np_=np_, h0=h0: e.tensor_tensor(
                            out=stg.ap[0:np_, :], in0=stg.ap[0:np_, :], in1=gb.ap[0:np_, h0:h0 + 512], op=ALU.mult),
                            reads=[stg, gb], writes=[stg])
                    else:
                        S.op("dve", lambda e, pz=pz, stg=stg, bb=bb, np_=np_: e.tensor_tensor(
                            out=stg.ap[0:np_, :], in0=pz.ap[0:np_, :], in1=bb.ap[0:np_, :], op=ALU.add),
                            reads=[pz, bb], writes=[stg])
                    S.dma("sp", lambda e, stg=stg, np_=np_, scr=scr, mi=mi, h0=h0: e.dma_start(
                        out=scr.ap[mi, 0:np_, h0:h0 + 512], in_=stg.ap[0:np_, :]), reads=[stg])
                if nb + 2 < 12:
                    a_load_b(nb + 2)
            wstf = [T(w_.ap.rearrange("p a b -> p (a b)"), f"wstf{i_}") for i_, w_ in enumerate(wst)]
            for j in range(5):
                wstf[j % 2].b = wst[j % 2].b
                load_cast(win, win.ap[:, :, j * 512:(j + 1) * 512], w_in_v[:, :, j * 512:(j + 1) * 512], wstf[j % 2], (128, 8, 512))
            S.barrier()
        M.release(base_mark)

        own = {}
        halo = {}
        for m in range(4):
            halo[11 + 16 * m] = m
            for t in range(4):
                own[12 + 16 * m + t] = (m, t)

        with psum_scope() as pst:
            pT = [pbank(pst, "pT0", BF16)]
            pX2 = pbank(pst, "pX2", BF16)
            pK = pbank(pst, "pK")
            pV = pbank(pst, "pV")
            pQ = pbank(pst, "pQ")
            pCa = pbank(pst, "pCa")
            pCg = pbank(pst, "pCg")
            pX = pbank(pst, "pX", BF16)

            Am_p = M.alloc("Am_p", [128, D], F32)
            shm_p = M.alloc("shm_p", [128, D], F32)
            xb = [M.alloc(f"xb{i}", [128, D], F32) for i in range(2)]
            rope_all = M.alloc("rope_all", [128, NS, 16], F32)
            rp_s = M.alloc("rp_s", [128, 16], F32)
            ssq = [M.alloc(f"ssq{i}", [128, 1], F32) for i in range(2)]
            rstd = [M.alloc(f"rstd{i}", [128, 1], F32) for i in range(2)]
            hpre = M.alloc("hpre", [128, D], F32)
            hb = [M.alloc(f"hb{i}", [128, D], BF16) for i in range(2)]
            hT = [M.alloc(f"hT{i}", [128, 8, 128], BF16) for i in range(2)]
            k32 = [M.alloc(f"k32{i}", [128, 512], F32) for i in range(2)]
            v32 = [M.alloc(f"v32{i}", [128, 512], F32) for i in range(2)]
            q32 = M.alloc("q32", [128, 512], F32)
            sg32 = M.alloc("sg32", [128, 512], F32)
            u32 = [M.alloc(f"u32{i}", [128, 512], F32) for i in range(2)]
            rtmp = M.alloc("rtmp", [128, 4, 8, 8], F32)
            kb2 = [M.alloc(f"kb{i}", [128, 512], BF16) for i in range(2)]
            qb2 = [M.alloc(f"qb{i}", [128, 512], BF16) for i in range(2)]
            ub2 = [M.alloc(f"ub{i}", [128, 512], BF16) for i in range(2)]
            ub = ub2[0]
            bstat = [M.alloc(f"bstat{i}", [128, 12], F32) for i in range(2)]
            mv = [M.alloc(f"mv{i}", [128, 2], F32) for i in range(2)]
            kst = [M.alloc(f"kst{i}", [128, 4, 512], BF16) for i in range(2)]
            vst = [M.alloc(f"vst{i}", [128, 4, 4 * 130], BF16) for i in range(2)]
            uT = M.alloc("uT", [128, 4, 640], BF16)
            conv_mark = M.mark()
            ycv = M.alloc("ycv", [128, 4, 512], F32)
            ycb = M.alloc("ycb", [128, 4, 512], BF16)
            y2b = M.alloc("y2b", [128, 4, 512], BF16)
            mean = M.alloc("mean", [128, 512], F32)
            var = M.alloc("var", [128, 512], F32)
            ztmp = M.alloc("ztmp", [128, 512], F32)
            ztmp2 = M.alloc("ztmp2", [128, 512], F32)

            S.dma("sp", lambda e: e.dma_start(out=rope_all.ap, in_=rope.rearrange("p (s f) -> p s f", f=16)), writes=[rope_all])
            S.dma("sp", lambda e: e.dma_start(out=Am_p.ap, in_=mod_p.ap[1]), writes=[Am_p])
            S.dma("sp", lambda e: e.dma_start(out=shm_p.ap, in_=mod_p.ap[0]), writes=[shm_p])
            def rmsnorm_h(np_, x_t, A_t, sh_t, par):
                sq, rs, hbt = ssq[par], rstd[par], hb[par]
                S.op("act", lambda e: e.activation(out=hpre.ap[0:np_, :], in_=x_t.ap[0:np_, :], func=AF.Square,
                                                   accum_out=sq.ap[0:np_, :]), reads=[x_t], writes=[hpre, sq])
                S.op("dve", lambda e: e.tensor_scalar(out=sq.ap[0:np_, :], in0=sq.ap[0:np_, :], scalar1=1.0 / D, scalar2=EPS,
                                                      op0=ALU.mult, op1=ALU.add), reads=[sq], writes=[sq])
                S.op("act", lambda e: e.activation(out=sq.ap[0:np_, :], in_=sq.ap[0:np_, :], func=AF.Sqrt), reads=[sq], writes=[sq])
                S.op("dve", lambda e: e.reciprocal(out=rs.ap[0:np_, :], in_=sq.ap[0:np_, :]), reads=[sq], writes=[rs])
                S.op("dve", lambda e: e.scalar_tensor_tensor(out=hpre.ap[0:np_, :], in0=x_t.ap[0:np_, :], scalar=rs.ap[0:np_, 0:1],
                                                            in1=A_t.ap[0:np_, :], op0=ALU.mult, op1=ALU.mult),
                     reads=[x_t, rs, A_t], writes=[hpre])
                S.op("pool", lambda e: e.tensor_tensor(out=hbt.ap[0:np_, :], in0=hpre.ap[0:np_, :], in1=sh_t.ap[0:np_, :],
                                                       op=ALU.add), reads=[hpre, sh_t], writes=[hbt])
                return hbt

            def transpose8(np_, src_bf, dst_t, dst_ap, pbankT, nchunk=8, copy_eng="act"):
                for kc in range(nchunk):
                    S.op("pe", lambda e, kc=kc: e.transpose(out=pbankT.ap[:, kc * 128:kc * 128 + np_],
                                                            in_=src_bf.ap[0:np_, kc * 128:(kc + 1) * 128],
                                                            identity=ident.ap[0:np_, 0:np_]),
                         reads=[src_bf, ident], writes=[pbankT], inc=(kc == nchunk - 1))
                src = pbankT.ap[:, 0:nchunk * 128].rearrange("p (c t) -> p c t", t=128)[:, :, 0:np_]
                if copy_eng == "act":
                    S.op("act", lambda e: e.copy(out=dst_ap, in_=src), reads=[pbankT], writes=[dst_t])
                else:
                    S.op("dve", lambda e: e.tensor_copy(out=dst_ap, in_=src), reads=[pbankT], writes=[dst_t])

            def proj(np_, hTt, j, pb):
                for kc in range(8):
                    S.op("pe", lambda e, kc=kc: e.matmul(pb.ap[0:np_, :], lhsT=hTt.ap[:, kc, 0:np_],
                                                         rhs=win.ap[:, kc, j * 512:(j + 1) * 512],
                                                         start=(kc == 0), stop=(kc == 7)),
                         reads=[hTt, win], writes=[pb], inc=(kc == 7))

            def rope_inplace(np_, t32, rp_t, rp_ap):
                v = t32.ap[0:np_, :].rearrange("p (g d) -> p g d", d=64)
                x1 = v[:, :, 0:8]
                x2 = v[:, :, 8:16]
                cos = rp_ap[0:np_, 0:8].unsqueeze(1).to_broadcast([np_, 8, 8])
                sin = rp_ap[0:np_, 8:16].unsqueeze(1).to_broadcast([np_, 8, 8])
                tm = rtmp.ap[0:np_]
                reng = "dve"
                S.op(reng, lambda e: e.tensor_tensor(out=tm[:, 0], in0=x1, in1=cos, op=ALU.mult), reads=[t32, rp_t], writes=[rtmp])
                S.op(reng, lambda e: e.tensor_tensor(out=tm[:, 1], in0=x2, in1=sin, op=ALU.mult), reads=[t32, rp_t], writes=[rtmp])
                S.op(reng, lambda e: e.tensor_tensor(out=tm[:, 2], in0=x2, in1=cos, op=ALU.mult), reads=[t32, rp_t], writes=[rtmp])
                S.op(reng, lambda e: e.tensor_tensor(out=tm[:, 3], in0=x1, in1=sin, op=ALU.mult), reads=[t32, rp_t], writes=[rtmp])
                S.op(reng, lambda e: e.tensor_tensor(out=x1, in0=tm[:, 0], in1=tm[:, 1], op=ALU.subtract), reads=[rtmp], writes=[t32])
                S.op(reng, lambda e: e.tensor_tensor(out=x2, in0=tm[:, 2], in1=tm[:, 3], op=ALU.add), reads=[rtmp], writes=[t32])

            def st0(s):
                xt = xb[s % 2]
                S.dma("sp", lambda e: e.dma_start(out=xt.ap, in_=xs[s * 128:(s + 1) * 128, :]), writes=[xt])

            def st1(s):
                xt, bs, mvt, sq = xb[s % 2], bstat[s % 2], mv[s % 2], ssq[s % 2]
                S.op("dve", lambda e: e.bn_stats(out=bs.ap[:, 0:6], in_=xt.ap[:, 0:512]), reads=[xt], writes=[bs])
                S.op("dve", lambda e: e.bn_stats(out=bs.ap[:, 6:12], in_=xt.ap[:, 512:1024]), reads=[xt], writes=[bs])
                S.op("dve", lambda e: e.bn_aggr(out=mvt.ap, in_=bs.ap), reads=[bs], writes=[mvt])
                S.op("dve", lambda e: e.scalar_tensor_tensor(out=sq.ap, in0=mvt.ap[:, 0:1], scalar=mvt.ap[:, 0:1], in1=mvt.ap[:, 1:2],
                                                            op0=ALU.mult, op1=ALU.add), reads=[mvt], writes=[sq])
                S.op("dve", lambda e: e.tensor_scalar(out=sq.ap, in0=sq.ap, scalar1=EPS, scalar2=None, op0=ALU.add),
                     reads=[sq], writes=[sq])

            def st2(s):
                xt, sq, rs, hbt = xb[s % 2], ssq[s % 2], rstd[s % 2], hb[s % 2]
                S.op("act", lambda e: e.activation(out=sq.ap, in_=sq.ap, func=AF.Sqrt), reads=[sq], writes=[sq])
                S.op("dve", lambda e: e.reciprocal(out=rs.ap, in_=sq.ap), reads=[sq], writes=[rs])
                S.op("dve", lambda e: e.scalar_tensor_tensor(out=hpre.ap, in0=xt.ap, scalar=rs.ap[:, 0:1], in1=Am_p.ap,
                                                            op0=ALU.mult, op1=ALU.mult), reads=[xt, rs, Am_p], writes=[hpre])
                S.op("pool", lambda e: e.tensor_tensor(out=hbt.ap, in0=hpre.ap, in1=shm_p.ap, op=ALU.add),
                     reads=[hpre, shm_p], writes=[hbt])

            def st3(s):
                transpose8(128, hb[s % 2], hT[s % 2], hT[s % 2].ap, pT[0])

            def st4(s):
                hTt, k3, v3 = hT[s % 2], k32[s % 2], v32[s % 2]
                proj(128, hTt, 3, pK)
                S.op("act", lambda e: e.copy(out=k3.ap, in_=pK.ap), reads=[pK], writes=[k3])
                proj(128, hTt, 4, pV)
                g4, grp = s % 4, s // 4
                vstg = vst[grp % 2]
                vdst = vstg.ap[:, :, g4 * 130:g4 * 130 + 129]
                S.op("act", lambda e: e.activation(out=vdst[:, :, 0:128], in_=pV.ap.rearrange("p (h d) -> p h d", d=128),
                                                   func=AF.Identity, scale=valid.ap[:, s:s + 1]), reads=[pV, valid], writes=[vstg])
                S.op("pool", lambda e: e.tensor_copy(
                    out=vdst[:, :, 128:129], in_=valid.ap[:, s:s + 1].unsqueeze(1).to_broadcast([128, 4, 1])),
                    reads=[valid], writes=[vstg])
                if s in own:
                    S.op("act", lambda e: e.copy(out=v3.ap, in_=pV.ap), reads=[pV], writes=[v3])
                if s in own:
                    proj(128, hTt, 2, pQ)
                    S.op("act", lambda e: e.copy(out=q32.ap, in_=pQ.ap), reads=[pQ], writes=[q32])
                if s in own or s in halo:
                    proj(128, hTt, 0, pCa)
                    proj(128, hTt, 1, pCg)
                    S.op("act", lambda e: e.activation(out=sg32.ap, in_=pCg.ap, func=AF.Sigmoid), reads=[pCg], writes=[sg32])

            def st5(s):
                k3, v3, kbt = k32[s % 2], v32[s % 2], kb2[s % 2]
                rpa = rope_all.ap[:, s, :]
                g4, grp = s % 4, s // 4
                vstg = vst[grp % 2]
                rope_inplace(128, k3, rope_all, rpa)
                S.op("dve", lambda e: e.tensor_copy(out=kbt.ap, in_=k3.ap), reads=[k3], writes=[kbt])
                if s in own:
                    i_own = own[s][0] * 4 + own[s][1]
                    S.dma("sp", lambda e: e.dma_start(out=k_o[i_own * 128:(i_own + 1) * 128, :], in_=k3.ap),
                          reads=[k3], semkey="D_ko", is_out=True)
                    S.dma("sp", lambda e: e.dma_start(out=v_o[i_own * 128:(i_own + 1) * 128, :], in_=v3.ap),
                          reads=[v3], semkey="D_vo", is_out=True)
                    qbt = qb2[s % 2]
                    rope_inplace(128, q32, rope_all, rpa)
                    S.op("dve", lambda e: e.tensor_copy(out=qbt.ap, in_=q32.ap), reads=[q32], writes=[qbt])
                if s in own or s in halo:
                    ut, ubt = u32[s % 2], ub2[s % 2]
                    S.op("dve", lambda e: e.scalar_tensor_tensor(out=ut.ap, in0=pCa.ap, scalar=valid.ap[:, s:s + 1], in1=sg32.ap,
                                                                op0=ALU.mult, op1=ALU.mult), reads=[pCa, sg32, valid], writes=[ut])
                    S.op("dve", lambda e: e.tensor_copy(out=ubt.ap, in_=ut.ap), reads=[ut], writes=[ubt])
                    if s in own and own[s] == (3, 3):
                        S.dma("sp", lambda e: e.dma_start(out=cp_o, in_=ut.ap[98:128, :]), reads=[ut], semkey="D_cpo", is_out=True)

            def st6(s):
                kbt = kb2[s % 2]
                g4, grp = s % 4, s // 4
                kstg = kst[grp % 2]
                for h in range(4):
                    S.op("pe", lambda e, h=h: e.transpose(out=pX.ap[:, h * 128:(h + 1) * 128], in_=kbt.ap[:, h * 128:(h + 1) * 128],
                                                          identity=ident.ap), reads=[kbt, ident], writes=[pX], inc=(h == 3))
                S.op("act", lambda e: e.copy(out=kstg.ap[:, :, g4 * 128:(g4 + 1) * 128],
                                             in_=pX.ap[:, 0:512].rearrange("p (h t) -> p h t", t=128)), reads=[pX], writes=[kstg])
                if s in own:
                    m, t = own[s]
                    qbt = qb2[s % 2]
                    for h in range(4):
                        S.op("pe", lambda e, h=h: e.transpose(out=pX2.ap[:, h * 128:(h + 1) * 128], in_=qbt.ap[:, h * 128:(h + 1) * 128],
                                                              identity=ident.ap), reads=[qbt, ident], writes=[pX2], inc=(h == 3))
                    tok0 = (m * 4 + t) * 128
                    S.op("act", lambda e: e.copy(out=QT.ap[:, :, tok0:tok0 + 128],
                                                 in_=pX2.ap[:, 0:512].rearrange("p (h t) -> p h t", t=128)), reads=[pX2], writes=[QT])

            def st7(s):
                g4, grp = s % 4, s // 4
                kstg, vstg = kst[grp % 2], vst[grp % 2]
                if g4 == 3:
                    S.dma("sp", lambda e: e.dma_start(
                        out=kT_scr.ap[:, :, grp * 512:(grp + 1) * 512].rearrange("h p t -> p h t"), in_=kstg.ap),
                        reads=[kstg], semkey=f"D_kst{grp % 2}")
                    S.dma("sp", lambda e: e.dma_start(
                        out=v_scr.ap[:, :, grp * 520:(grp + 1) * 520].rearrange("h p t -> p h t"), in_=vstg.ap),
                        reads=[vstg], semkey=f"D_vst{grp % 2}")
                if s in own or s in halo:
                    ubt = ub2[s % 2]
                    for ct in range(4):
                        S.op("pe", lambda e, ct=ct: e.transpose(out=pX.ap[:, ct * 128:(ct + 1) * 128],
                                                                in_=ubt.ap[:, ct * 128:(ct + 1) * 128], identity=ident.ap),
                             reads=[ubt, ident], writes=[pX], inc=(ct == 3))
                    pos = 0 if s in halo else (1 + own[s][1])
                    S.op("act", lambda e: e.copy(out=uT.ap[:, :, pos * 128:(pos + 1) * 128],
                                                 in_=pX.ap[:, 0:512].rearrange("p (c t) -> p c t", t=128)), reads=[pX], writes=[uT])

            def st8(s):
                if not (s in own and own[s][1] == 3):
                    return
                m = own[s][0]
                cbanks = [pK, pV, pQ, pCa]
                for ct in range(4):
                    pb = cbanks[ct]
                    for w in range(31):
                        S.op("pe", lambda e, ct=ct, w=w, pb=pb: e.matmul(pb.ap, lhsT=diag.ap[:, ct, w, :],
                                                                       rhs=uT.ap[:, ct, 98 + w:98 + w + 512],
                                                                       start=(w == 0), stop=(w == 30)),
                             reads=[diag, uT], writes=[pb], inc=(w == 30))
                    S.op("act", lambda e, ct=ct, pb=pb: e.activation(out=ycv.ap[:, ct, :], in_=pb.ap, func=AF.Identity,
                                                                   bias=cvec.ap[:, ct, 31:32]), reads=[pb, cvec], writes=[ycv])
                    S.op("dve", lambda e, ct=ct: e.tensor_copy(out=ycb.ap[:, ct, :], in_=ycv.ap[:, ct, :]), reads=[ycv], writes=[ycb])
                    S.op("pool", lambda e, ct=ct: e.tensor_tensor(out=y2b.ap[:, ct, :], in0=ycv.ap[:, ct, :], in1=ycv.ap[:, ct, :],
                                                                  op=ALU.mult), reads=[ycv], writes=[y2b])
                for ct in range(4):
                    S.op("pe", lambda e, ct=ct: e.matmul(pCg.ap, lhsT=ones_bf.ap, rhs=ycb.ap[:, ct, :],
                                                         start=(ct == 0), stop=(ct == 3)), reads=[ones_bf, ycb], writes=[pCg],
                         inc=(ct == 3))
                for ct in range(4):
                    S.op("pe", lambda e, ct=ct: e.matmul(pK.ap, lhsT=ones_bf.ap, rhs=y2b.ap[:, ct, :],
                                                         start=(ct == 0), stop=(ct == 3)), reads=[ones_bf, y2b], writes=[pK],
                         inc=(ct == 3))
                S.op("dve", lambda e: e.tensor_scalar(out=mean.ap, in0=pCg.ap, scalar1=1.0 / 512, scalar2=None, op0=ALU.mult),
                     reads=[pCg], writes=[mean])
                S.op("dve", lambda e: e.tensor_tensor(out=ztmp.ap, in0=mean.ap, in1=mean.ap, op=ALU.mult), reads=[mean], writes=[ztmp])
                S.op("dve", lambda e: e.scalar_tensor_tensor(out=var.ap, in0=pK.ap, scalar=1.0 / 512, in1=ztmp.ap,
                                                            op0=ALU.mult, op1=ALU.subtract), reads=[pK, ztmp], writes=[var])
                S.op("dve", lambda e: e.tensor_scalar(out=var.ap, in0=var.ap, scalar1=EPS, scalar2=None, op0=ALU.add),
                     reads=[var], writes=[var])
                S.op("act", lambda e: e.activation(out=var.ap, in_=var.ap, func=AF.Sqrt), reads=[var], writes=[var])
                S.op("dve", lambda e: e.reciprocal(out=var.ap, in_=var.ap), reads=[var], writes=[var])
                for ct in range(4):
                    zt = ztmp if ct % 2 == 0 else ztmp2
                    S.op("dve", lambda e, ct=ct, zt=zt: e.tensor_tensor(out=zt.ap, in0=ycv.ap[:, ct, :], in1=mean.ap, op=ALU.subtract),
                         reads=[ycv, mean], writes=[zt])
                    S.op("pool", lambda e, zt=zt: e.tensor_tensor(out=zt.ap, in0=zt.ap, in1=var.ap, op=ALU.mult),
                         reads=[zt, var], writes=[zt])
                    S.op("act", lambda e, ct=ct, m=m, zt=zt: e.activation(out=convT.ap[:, ct, m * 512:(m + 1) * 512], in_=zt.ap,
                                                                        func=AF.Silu, scale=cvec.ap[:, ct, 32:33],
                                                                        bias=cvec.ap[:, ct, 33:34]),
                         reads=[zt, cvec], writes=[convT])

            stages = [st0, st1, st2, st3, st4, st5, st6, st7, st8]
            for it in range(NS + len(stages) - 1):
                for k in (2, 1, 0, 8, 7, 6, 5, 4, 3):
                    sl = it - k
                    if 0 <= sl < NS:
                        stages[k](sl)

            SP = 16
            xsb = xb[0]
            rps = rp_s
            S.dma("sp", lambda e: e.dma_start(out=xsb.ap[0:SP, :], in_=xsm), writes=[xsb])
            S.dma("sp", lambda e: e.dma_start(out=rps.ap[0:SP, :], in_=rope_s), writes=[rps])
            S.dma("sp", lambda e: e.dma_start(out=Am_p.ap[0:16, :], in_=mod_s.ap[1]), writes=[Am_p])
            S.dma("sp", lambda e: e.dma_start(out=shm_p.ap[0:16, :], in_=mod_s.ap[0]), writes=[shm_p])
            hbs = rmsnorm_h(SP, xsb, Am_p, shm_p, 0)
            hTs = hT[0]
            transpose8(SP, hbs, hTs, hTs.ap[:, :, 0:SP], pT[0])
            ks3, vs3 = k32[0], v32[0]
            proj(SP, hTs, 3, pK)
            S.op("act", lambda e: e.copy(out=ks3.ap[0:SP, :], in_=pK.ap[0:SP, :]), reads=[pK], writes=[ks3])
            rope_inplace(SP, ks3, rps, rps.ap)
            S.dma("sp", lambda e: e.dma_start(out=ks_o, in_=ks3.ap[0:SP, :]), reads=[ks3], semkey="D_kso", is_out=True)
            proj(SP, hTs, 4, pV)
            S.op("act", lambda e: e.copy(out=vs3.ap[0:SP, :], in_=pV.ap[0:SP, :]), reads=[pV], writes=[vs3])
            S.dma("sp", lambda e: e.dma_start(out=vs_o, in_=vs3.ap[0:SP, :]), reads=[vs3], semkey="D_vso", is_out=True)
            proj(SP, hTs, 2, pQ)
            S.op("act", lambda e: e.copy(out=q32.ap[0:SP, :], in_=pQ.ap[0:SP, :]), reads=[pQ], writes=[q32])
            rope_inplace(SP, q32, rps, rps.ap)
            S.dma("sp", lambda e: e.dma_start(out=qs_scr.ap, in_=q32.ap[0:SP, :]), reads=[q32])
            us3 = u32[0]
            proj(SP, hTs, 0, pCa)
            proj(SP, hTs, 1, pCg)
            S.op("act", lambda e: e.activation(out=sg32.ap[0:SP, :], in_=pCg.ap[0:SP, :], func=AF.Sigmoid), reads=[pCg], writes=[sg32])
            S.op("dve", lambda e: e.tensor_tensor(out=us3.ap[0:SP, :], in0=pCa.ap[0:SP, :], in1=sg32.ap[0:SP, :], op=ALU.mult),
                 reads=[pCa, sg32], writes=[us3])
            S.barrier()
            M.release(conv_mark)
            S.dma("sp", lambda e: e.dma_start(out=cs_o[:, 29, :], in_=us3.ap[0:SP, :]), reads=[us3], semkey="D_cso2", is_out=True)
            stc = [M.alloc("stc0", [128, 4, 512], F32)] * 2
            dwt = M.alloc("dwt", [128, 512], F32)
            selt = M.alloc("selt", [128, 16, 16], F32)
            dw30 = M.alloc("dw30", [128, 512], F32)
            dwb_bc = M.alloc("dwb_bc", [128, 512], F32)
            clg_bc = M.alloc("clg_bc", [128, 512], F32)
            clb_bc = M.alloc("clb_bc", [128, 512], F32)
            S.dma("sp", lambda e: e.dma_start(out=dwt.ap[0:30, :], in_=dw_w[0:30, :]), writes=[dwt], group="sconv")
            S.dma("sp", lambda e: e.dma_start(out=selt.ap[0:30], in_=sel_d.rearrange("w (a b) -> w a b", b=16)), writes=[selt], group="sconv")
            S.dma("sp", lambda e: e.dma_start(out=dw30.ap[0:SP, :], in_=dw_w[30:31, :].to_broadcast([SP, 512])), writes=[dw30], group="sconv")
            S.dma("sp", lambda e: e.dma_start(out=dwb_bc.ap[0:SP, :], in_=dw_b.to_broadcast([SP, 512])), writes=[dwb_bc], group="sconv")
            S.dma("sp", lambda e: e.dma_start(out=clg_bc.ap[0:SP, :], in_=cln_g.to_broadcast([SP, 512])), writes=[clg_bc], group="sconv")
            S.dma("sp", lambda e: e.dma_start(out=clb_bc.ap[0:SP, :], in_=cln_b.to_broadcast([SP, 512])), writes=[clb_bc], group="sconv")
            stv = stconv.rearrange("s w c -> w s c")
            for q4 in range(4):
                stq = stc[q4 % 2]
                S.dma("sp", lambda e, stq=stq, q4=q4: e.dma_start(out=stq.ap[0:30], in_=stv[:, q4 * 4:(q4 + 1) * 4, :]), writes=[stq])
                S.dma("sp", lambda e, stq=stq, q4=q4: e.dma_start(out=cs_o[q4 * 4:(q4 + 1) * 4, 0:29, :].rearrange("s w c -> w s c"),
                                                                  in_=stq.ap[1:30]), reads=[stq], semkey="D_cso1", is_out=True)
                S.op("dve", lambda e, stq=stq: e.tensor_tensor(out=stq.ap[0:30], in0=stq.ap[0:30],
                                                               in1=dwt.ap[0:30, :].unsqueeze(1).to_broadcast([30, 4, 512]), op=ALU.mult),
                     reads=[stq, dwt], writes=[stq])
                for s4 in range(4):
                    sm = q4 * 4 + s4
                    S.op("pe", lambda e, sm=sm, s4=s4, stq=stq: e.matmul(pK.ap[0:SP, :], lhsT=selt.ap[0:30, sm, :], rhs=stq.ap[0:30, s4, :],
                                                                       start=(sm == 0), stop=(sm == 15)),
                         reads=[selt, stq], writes=[pK], inc=(s4 == 3))
            cvs = M.alloc("cvs", [128, 512], F32)
            S.op("dve", lambda e: e.tensor_tensor(out=cvs.ap[0:SP, :], in0=us3.ap[0:SP, :], in1=dw30.ap[0:SP, :], op=ALU.mult),
                 reads=[us3, dw30], writes=[cvs])
            S.op("dve", lambda e: e.tensor_tensor(out=cvs.ap[0:SP, :], in0=cvs.ap[0:SP, :], in1=pK.ap[0:SP, :], op=ALU.add),
                 reads=[cvs, pK], writes=[cvs])
            S.op("dve", lambda e: e.tensor_tensor(out=cvs.ap[0:SP, :], in0=cvs.ap[0:SP, :], in1=dwb_bc.ap[0:SP, :], op=ALU.add),
                 reads=[cvs, dwb_bc], writes=[cvs])
            bst = M.alloc("bst", [128, 6], F32)
            bag = M.alloc("bag", [128, 2], F32)
            S.op("dve", lambda e: e.bn_stats(out=bst.ap[0:SP, :], in_=cvs.ap[0:SP, :]), reads=[cvs], writes=[bst])
            S.op("dve", lambda e: e.bn_aggr(out=bag.ap[0:SP, :], in_=bst.ap[0:SP, :]), reads=[bst], writes=[bag])
            S.op("dve", lambda e: e.tensor_scalar(out=bag.ap[0:SP, 1:2], in0=bag.ap[0:SP, 1:2], scalar1=EPS, scalar2=None, op0=ALU.add),
                 reads=[bag], writes=[bag])
            S.op("act", lambda e: e.activation(out=bag.ap[0:SP, 1:2], in_=bag.ap[0:SP, 1:2], func=AF.Sqrt), reads=[bag], writes=[bag])
            S.op("dve", lambda e: e.reciprocal(out=bag.ap[0:SP, 1:2], in_=bag.ap[0:SP, 1:2]), reads=[bag], writes=[bag])
            S.op("dve", lambda e: e.tensor_scalar(out=cvs.ap[0:SP, :], in0=cvs.ap[0:SP, :], scalar1=bag.ap[0:SP, 0:1],
                                                  scalar2=bag.ap[0:SP, 1:2], op0=ALU.subtract, op1=ALU.mult), reads=[cvs, bag], writes=[cvs])
            S.op("dve", lambda e: e.tensor_tensor(out=cvs.ap[0:SP, :], in0=cvs.ap[0:SP, :], in1=clg_bc.ap[0:SP, :], op=ALU.mult),
                 reads=[cvs, clg_bc], writes=[cvs])
            S.op("dve", lambda e: e.tensor_tensor(out=cvs.ap[0:SP, :], in0=cvs.ap[0:SP, :], in1=clb_bc.ap[0:SP, :], op=ALU.add),
                 reads=[cvs, clb_bc], writes=[cvs])
            S.op("act", lambda e: e.activation(out=ub.ap[0:SP, :], in_=cvs.ap[0:SP, :], func=AF.Silu), reads=[cvs], writes=[ub])
            transpose8(SP, ub, catS, catS.ap[:, 0:4, :], pX, nchunk=4)
            S.barrier()
        M.release(pre_win_mark)
        oT = M.alloc("oT", [128, 4, 2048], BF16)
        base_mark = M.mark()

        with psum_scope() as pst:
            pS = [pbank(pst, f"pS{i}") for i in range(3)]
            pOA = pbank(pst, "pOA")
            pOB = pbank(pst, "pOB")
            pXc = pbank(pst, "pXc", BF16)
            pSA = pbank(pst, "pSA")
            pSB = pbank(pst, "pSB")

            KTb = [M.alloc(f"KTb{i}", [128, NS * 128], BF16) for i in range(2)]
            Vb = [M.alloc(f"Vb{i}", [128, NS, 130], BF16) for i in range(2)]
            pb_ = [M.alloc(f"pb{i}", [128, 512], BF16) for i in range(3)]
            oacc = M.alloc("oacc", [128, 4, 129], F32)
            o1n = M.alloc("o1n", [128, 4, 128], F32)
            rden = M.alloc("rden", [128, 4], F32)
            ssn = M.alloc("ssn", [128, 4], F32)
            ojunk = M.alloc("ojunk", [128, 128], F32)
            obf = M.alloc("obf", [128, 4, 128], BF16)

            SP = 16
            qs32 = M.alloc("qs32", [128, 512], F32)
            ksm32 = M.alloc("ksm32", [128, 512], F32)
            vsm32 = M.alloc("vsm32", [128, 512], F32)
            vsa = M.alloc("vsa", [128, 4, 129], BF16)
            pself = M.alloc("pself", [128, 8], F32)
            lself = M.alloc("lself", [128, 4, 64], BF16)
            ptf = M.alloc("ptf", [128, 256], F32)
            pti = M.alloc("pti", [128, 256], I32)
            idx = M.alloc("idx", [128, 256], I32)
            qbc = [M.alloc(f"qbc{i}", [128, 512], F32) for i in range(2)]
            NG = 4
            kpg = [M.alloc(f"kpg{i}", [128, NG, 512], BF16) for i in range(4)]
            vpg = [M.alloc(f"vpg{i}", [128, NG, 512], BF16) for i in range(4)]
            prod = M.alloc("prod", [128, NG, 512], F32)
            scs = [M.alloc(f"scs{i}", [128, 16, 8], F32) for i in range(2)]
            Pz = [M.alloc(f"Pz{i}", [128, 16, 4, 64], BF16) for i in range(2)]
            osm = M.alloc("osm", [128, 4, 129], F32)
            osm2 = M.alloc("osm2", [128, 4, 129], F32)
            osn = M.alloc("osn", [128, 4, 128], F32)

            S.dma("sp", lambda e: e.dma_start(out=qs32.ap[0:SP, :], in_=qs_scr.ap), writes=[qs32], group="sattn")
            S.dma("sp", lambda e: e.dma_start(out=ksm32.ap[0:SP, :], in_=ks_o), writes=[ksm32], group="sattn")
            S.dma("sp", lambda e: e.dma_start(out=vsm32.ap[0:SP, :], in_=vs_o), writes=[vsm32], group="sattn")
            S.dma("sp", lambda e: e.dma_start(out=pti.ap, in_=ptab.to_broadcast([128, 256])), writes=[pti], group="sattn")
            S.op("dve", lambda e: e.tensor_copy(out=ptf.ap, in_=pti.ap), reads=[pti], writes=[ptf])
            S.op("dve", lambda e: e.tensor_scalar(out=ptf.ap, in0=ptf.ap, scalar1=128.0, scalar2=lane.ap[:, 0:1],
                                                  op0=ALU.mult, op1=ALU.add), reads=[ptf, lane], writes=[ptf])
            S.op("dve", lambda e: e.tensor_copy(out=idx.ap, in_=ptf.ap), reads=[ptf], writes=[idx])
            for i in range(2):
                S.op("pool", lambda e, i=i: e.memset(Pz[i].ap, 0.0), writes=[Pz[i]])
            S.op("dve", lambda e: e.memset(pSA.ap, 0.0), writes=[pSA])
            S.op("dve", lambda e: e.memset(pSB.ap, 0.0), writes=[pSB])
            S.op("dve", lambda e: e.tensor_tensor(out=prod.ap[0:SP, 0, :], in0=qs32.ap[0:SP, :], in1=ksm32.ap[0:SP, :], op=ALU.mult),
                 reads=[qs32, ksm32], writes=[prod])
            S.op("dve", lambda e: e.tensor_reduce(out=pself.ap[0:SP, :], in_=prod.ap[0:SP, 0, :].rearrange("p (g d) -> p g d", d=64),
                                                  axis=AX.X, op=ALU.add), reads=[prod], writes=[pself])
            S.op("act", lambda e: e.activation(out=pself.ap[0:SP, :], in_=pself.ap[0:SP, :], func=AF.Exp, scale=SCALE),
                 reads=[pself], writes=[pself])
            vsa2 = vsa.ap.rearrange("p h d -> p (h d)")[:, 0:512]
            S.op("dve", lambda e: e.tensor_copy(out=vsa2[0:SP, :], in_=vsm32.ap[0:SP, :]), reads=[vsm32], writes=[vsa])
            S.op("pool", lambda e: e.memset(lself.ap[0:SP], 0.0), writes=[lself])
            for h in range(4):
                for c in range(2):
                    S.op("dve", lambda e, h=h, c=c: e.tensor_scalar(out=lself.ap[0:SP, h, c * 32:c * 32 + 16], in0=ident_f.ap[0:SP, 0:SP],
                                                                    scalar1=pself.ap[0:SP, 2 * h + c:2 * h + c + 1], scalar2=None,
                                                                    op0=ALU.mult), reads=[ident_f, pself], writes=[lself])
            for hp in range(2):
                lt = lself.ap[0:SP, 2 * hp:2 * hp + 2, :].rearrange("p h x -> p (h x)")
                S.op("pe", lambda e, hp=hp, lt=lt: e.matmul(pSA.ap[:, hp * 256:(hp + 1) * 256], lhsT=lt,
                                                            rhs=vsa2[0:SP, hp * 256:(hp + 1) * 256], start=False, stop=False,
                                                            skip_group_check=True), reads=[lself, vsa], writes=[pSA], inc=False)
                S.op("pe", lambda e, hp=hp, lt=lt: e.matmul(pSB.ap[:, hp:hp + 1], lhsT=lt, rhs=ones_bf.ap[0:SP, 0:1], start=False, stop=False,
                                                            skip_group_check=True), reads=[lself, ones_bf], writes=[pSB], inc=(hp == 1))

            jobs = [(sm, g) for sm in range(16) for g in range(16 // NG)]

            def s_stage0(ji):
                sm, g = jobs[ji]
                if g == 0:
                    qb_t = qbc[sm % 2]
                    S.dma("sp", lambda e: e.dma_start(out=qb_t.ap, in_=qs_scr.ap[sm:sm + 1, :].to_broadcast([128, 512])),
                          writes=[qb_t])
                kg = kpg[ji % 4]
                vg = vpg[ji % 4]
                for pg in range(NG):
                    col = sm * 16 + g * NG + pg
                    S.dma("pool", lambda e, kg=kg, pg=pg, col=col: e.indirect_dma_start(
                        out=kg.ap[:, pg, :], out_offset=None, in_=cache_k,
                        in_offset=bass.IndirectOffsetOnAxis(ap=idx.ap[:, col:col + 1], axis=0)),
                        reads=[idx], writes=[kg], semkey=f"D_kpg{ji % 4}")
                    S.dma("pool", lambda e, vg=vg, pg=pg, col=col: e.indirect_dma_start(
                        out=vg.ap[:, pg, :], out_offset=None, in_=cache_v,
                        in_offset=bass.IndirectOffsetOnAxis(ap=idx.ap[:, col:col + 1], axis=0)),
                        reads=[idx], writes=[vg], semkey=f"D_vpg{ji % 4}")

            def s_stage1(ji):
                sm, g = jobs[ji]
                kg = kpg[ji % 4]
                qb_t = qbc[sm % 2]
                sc = scs[sm % 2]
                pz = Pz[sm % 2]
                S.op("dve", lambda e: e.tensor_tensor(
                    out=prod.ap, in0=kg.ap, in1=qb_t.ap.unsqueeze(1).to_broadcast([128, NG, 512]), op=ALU.mult),
                    reads=[kg, qb_t], writes=[prod])
                S.op("dve", lambda e: e.tensor_reduce(
                    out=sc.ap[:, g * NG:(g + 1) * NG, :], in_=prod.ap.rearrange("p n (g d) -> p n g d", d=64),
                    axis=AX.X, op=ALU.add), reads=[prod], writes=[sc])
                S.op("act", lambda e: e.activation(
                    out=pz.ap[:, g * NG:(g + 1) * NG, :, :].rearrange("p n h (c s) -> p n h c s", s=32)[:, :, :, :, sm],
                    in_=sc.ap[:, g * NG:(g + 1) * NG, :].rearrange("p n (h c) -> p n h c", c=2),
                    func=AF.Exp, scale=SCALE), reads=[sc], writes=[pz])

            def s_stage2(ji):
                sm, g = jobs[ji]
                vg = vpg[ji % 4]
                pz = Pz[sm % 2]
                for pg in range(NG):
                    for hp in range(2):
                        lt = pz.ap[:, g * NG + pg, 2 * hp:2 * hp + 2, :].rearrange("p h x -> p (h x)")
                        S.op("pe", lambda e, pg=pg, hp=hp, lt=lt: e.matmul(
                            pSA.ap[:, hp * 256:(hp + 1) * 256], lhsT=lt, rhs=vg.ap[:, pg, hp * 256:(hp + 1) * 256],
                            start=False, stop=False, skip_group_check=True), reads=[pz, vg], writes=[pSA], inc=False)
                        S.op("pe", lambda e, hp=hp, lt=lt: e.matmul(
                            pSB.ap[:, hp:hp + 1], lhsT=lt, rhs=ones_bf.ap[:, 0:1],
                            start=False, stop=False, skip_group_check=True), reads=[pz, ones_bf], writes=[pSB], inc=(hp == 1))
                if g == 16 // NG - 1:
                    S.op("pool", lambda e: e.memset(
                        pz.ap.rearrange("p n h (c s) -> p n h c s", s=32)[:, :, :, :, sm], 0.0), writes=[pz])

            stick = [0]

            def sample_tick():
                t = stick[0]
                stick[0] += 1
                if t < len(jobs):
                    s_stage0(t)
                if 0 <= t - 1 < len(jobs):
                    s_stage1(t - 1)
                if 0 <= t - 2 < len(jobs):
                    s_stage2(t - 2)

            NTICK = len(jobs) + 2

            tiles = []
            for h in range(4):
                for m in range(4):
                    for c in range(2):
                        nfull = 12 + 16 * m
                        for si in range(nfull + 4):
                            tiles.append(dict(h=h, m=m, c=c, si=si, t=(si - nfull if si >= nfull else 0), diag=(si >= nfull),
                                              first=(si == 0), last=(si == nfull + 3)))
            NT = len(tiles)
            LA = 2
            tick_every = max(1, NT // NTICK)
            deferred = []

            def later(step, delay, fn):
                deferred.append((step + delay, fn))

            def run_due(step):
                k = 0
                while k < len(deferred):
                    if deferred[k][0] <= step:
                        deferred.pop(k)[1]()
                    else:
                        k += 1

            def oslot(qb_i):
                return (pOA, pOA.ap[:, qb_i * 129:(qb_i + 1) * 129]) if qb_i < 3 else (pOB, pOB.ap[:, 0:129])

            loaded_h = [-1]
            qz_raw = ksm32.ap[:, 0:512].bitcast(BF16)
            qz_raw2 = vsm32.ap[:, 0:512].bitcast(BF16)
            qz = [[T(qz_raw[:, 0:512], "qz00"), T(qz_raw[:, 512:1024], "qz01")],
                  [T(qz_raw2[:, 0:512], "qz10"), T(qz_raw2[:, 512:1024], "qz11")]]
            for c_ in range(2):
                for p_ in range(2):
                    S.op("pool", lambda e, c_=c_, p_=p_: e.memset(qz[c_][p_].ap, 0.0),
                         writes=[qz[c_][p_], ksm32, vsm32, vsa, pself, prod])

            def emit_qk(i):
                tl = tiles[i]
                h, m, c, si, q0 = tl["h"], tl["m"], tl["c"], tl["si"], tl["t"] * 128
                if loaded_h[0] < h:
                    loaded_h[0] = h
                    KT_, Vh_ = KTb[h % 2], Vb[h % 2]
                    S.dma("sp", lambda e: e.dma_start(out=KT_.ap, in_=kT_scr.ap[h]), writes=[KT_])
                    S.dma("sp", lambda e: e.dma_start(out=Vh_.ap, in_=v_scr.ap[h].rearrange("p (s d) -> p s d", d=130)), writes=[Vh_])
                KT = KTb[h % 2]
                psb = pS[i % 3]
                gpar = (h * 4 + m) % 2
                if tl["first"] and c == 0:
                    for c_ in range(2):
                        S.op("pool", lambda e, c_=c_: e.tensor_copy(out=qz[c_][gpar].ap[64 * c_:64 * c_ + 64, :],
                                                                    in_=QT.ap[64 * c_:64 * c_ + 64, h, m * 512:(m + 1) * 512]),
                             reads=[QT], writes=[qz[c_][gpar]])
                qzt = qz[c][gpar]
                S.op("pe", lambda e: e.matmul(
                    psb.ap[:, q0:512], lhsT=KT.ap[:, si * 128:(si + 1) * 128],
                    rhs=qzt.ap[:, q0:512], start=True, stop=True),
                    reads=[KT, qzt], writes=[psb])

            def finish_group(step, h, m, c):
                def f0():
                    S.op("act", lambda e: e.copy(out=oacc.ap[:, 0:3, :], in_=pOA.ap[:, 0:387].rearrange("p (q d) -> p q d", d=129)),
                         reads=[pOA], writes=[oacc])
                    S.op("act", lambda e: e.copy(out=oacc.ap[:, 3, :], in_=pOB.ap[:, 0:129]), reads=[pOB], writes=[oacc])
                f0()

                def f1():
                    S.op("dve", lambda e: e.reciprocal(out=rden.ap, in_=oacc.ap[:, :, 128]), reads=[oacc], writes=[rden])
                    if c == 0:
                        S.op("dve", lambda e: e.tensor_tensor(out=o1n.ap, in0=oacc.ap[:, :, 0:128],
                                                              in1=rden.ap.unsqueeze(2).to_broadcast([128, 4, 128]), op=ALU.mult),
                             reads=[oacc, rden], writes=[o1n])
                    else:
                        S.op("dve", lambda e: e.tensor_tensor(out=oacc.ap[:, :, 0:128], in0=oacc.ap[:, :, 0:128],
                                                              in1=rden.ap.unsqueeze(2).to_broadcast([128, 4, 128]), op=ALU.mult),
                             reads=[oacc, rden], writes=[oacc])
                        S.op("dve", lambda e: e.scalar_tensor_tensor(out=o1n.ap, in0=oacc.ap[:, :, 0:128], scalar=neglam.ap[:, 0:1],
                                                                    in1=o1n.ap, op0=ALU.mult, op1=ALU.add),
                             reads=[oacc, neglam, o1n], writes=[o1n])
                later(step, 2, f1)
                if c == 0:
                    return

                def f2():
                    for qb_i in range(4):
                        S.op("act", lambda e, qb_i=qb_i: e.activation(out=ojunk.ap, in_=o1n.ap[:, qb_i, :], func=AF.Square,
                                                                      accum_out=ssn.ap[:, qb_i:qb_i + 1]),
                             reads=[o1n], writes=[ojunk, ssn])
                later(step, 4, f2)

                def f3():
                    S.op("dve", lambda e: e.tensor_scalar(out=ssn.ap, in0=ssn.ap, scalar1=1.0 / 128, scalar2=EPS,
                                                          op0=ALU.mult, op1=ALU.add), reads=[ssn], writes=[ssn])
                later(step, 6, f3)

                def f4():
                    S.op("act", lambda e: e.activation(out=ssn.ap, in_=ssn.ap, func=AF.Sqrt), reads=[ssn], writes=[ssn])
                later(step, 8, f4)

                def f5():
                    S.op("dve", lambda e: e.reciprocal(out=ssn.ap, in_=ssn.ap), reads=[ssn], writes=[ssn])
                    for qb_i in range(4):
                        S.op("dve", lambda e, qb_i=qb_i: e.scalar_tensor_tensor(
                            out=obf.ap[:, qb_i, :], in0=o1n.ap[:, qb_i, :], scalar=ssn.ap[:, qb_i:qb_i + 1], in1=gsub.ap,
                            op0=ALU.mult, op1=ALU.mult), reads=[o1n, ssn, gsub], writes=[obf])
                later(step, 10, f5)

                def f6():
                    for qb_i in range(4):
                        S.op("pe", lambda e, qb_i=qb_i: e.transpose(out=pXc.ap[:, qb_i * 128:(qb_i + 1) * 128], in_=obf.ap[:, qb_i, :],
                                                                    identity=ident.ap), reads=[obf, ident], writes=[pXc],
                             inc=(qb_i == 3))
                later(step, 13, f6)

                def f7():
                    S.op("dve", lambda e: e.tensor_copy(out=oT.ap[:, h, m * 512:(m + 1) * 512], in_=pXc.ap[:, 0:512]),
                         reads=[pXc], writes=[oT])
                later(step, 15, f7)

            def emit_tile(j):
                tl = tiles[j]
                h, m, c, si, t = tl["h"], tl["m"], tl["c"], tl["si"], tl["t"]
                q0 = t * 128
                Vh = Vb[h % 2]
                psb = pS[j % 3]
                pbt = pb_[j % 3]
                if tl["first"]:
                    S.op("dve", lambda e: e.memset(pOA.ap, 0.0), writes=[pOA])
                    S.op("dve", lambda e: e.memset(pOB.ap[:, 0:129], 0.0), writes=[pOB])
                S.op("act", lambda e: e.activation(out=pbt.ap[:, q0:512], in_=psb.ap[:, q0:512], func=AF.Exp, scale=SCALE),
                     reads=[psb], writes=[pbt])
                if tl["diag"]:
                    S.op("dve", lambda e: e.tensor_tensor(out=pbt.ap[:, q0:q0 + 128], in0=pbt.ap[:, q0:q0 + 128],
                                                          in1=tri.ap, op=ALU.mult), reads=[pbt, tri], writes=[pbt])
                for qb_i in range(t, 4):
                    bk, ap_ = oslot(qb_i)
                    S.op("pe", lambda e, ap_=ap_, qb_i=qb_i: e.matmul(
                        ap_, lhsT=pbt.ap[:, qb_i * 128:(qb_i + 1) * 128], rhs=Vh.ap[:, si, 0:129], start=False, stop=False,
                        skip_group_check=True), reads=[pbt, Vh], writes=[bk], inc=(qb_i == 3))
                if tl["last"]:
                    finish_group(j, h, m, c)

            for i in range(NT + LA):
                if i < NT:
                    emit_qk(i)
                j = i - LA
                if j >= 0:
                    emit_tile(j)
                    run_due(j)
                    if j % tick_every == 0 and stick[0] < NTICK:
                        sample_tick()
            while deferred:
                run_due(10 ** 9)
            while stick[0] < NTICK:
                sample_tick()
            for h in range(4):
                hp, hl = h // 2, h % 2
                for c in range(2):
                    r0 = hl * 64 + c * 32
                    dst = osm if c == 0 else osm2
                    eng = "act" if c == 0 else "dve"
                    src_n = pSA.ap[r0:r0 + SP, hp * 256 + hl * 128:hp * 256 + hl * 128 + 128]
                    src_d = pSB.ap[r0:r0 + SP, hp:hp + 1]
                    if eng == "act":
                        S.op("act", lambda e, dst=dst, h=h, src_n=src_n: e.copy(out=dst.ap[0:SP, h, 0:128], in_=src_n), reads=[pSA], writes=[dst])
                        S.op("act", lambda e, dst=dst, h=h, src_d=src_d: e.copy(out=dst.ap[0:SP, h, 128:129], in_=src_d), reads=[pSB], writes=[dst])
                    else:
                        S.op("dve", lambda e, dst=dst, h=h, src_n=src_n: e.tensor_copy(out=dst.ap[0:SP, h, 0:128], in_=src_n), reads=[pSA], writes=[dst])
                        S.op("dve", lambda e, dst=dst, h=h, src_d=src_d: e.tensor_copy(out=dst.ap[0:SP, h, 128:129], in_=src_d), reads=[pSB], writes=[dst])
            S.op("dve", lambda e: e.reciprocal(out=rden.ap[0:SP, :], in_=osm.ap[0:SP, :, 128]), reads=[osm], writes=[rden])
            S.op("dve", lambda e: e.tensor_tensor(out=osn.ap[0:SP], in0=osm.ap[0:SP, :, 0:128],
                                                  in1=rden.ap[0:SP].unsqueeze(2).to_broadcast([SP, 4, 128]), op=ALU.mult),
                 reads=[osm, rden], writes=[osn])
            S.op("dve", lambda e: e.reciprocal(out=rden.ap[0:SP, :], in_=osm2.ap[0:SP, :, 128]), reads=[osm2], writes=[rden])
            S.op("dve", lambda e: e.tensor_tensor(out=osm2.ap[0:SP, :, 0:128], in0=osm2.ap[0:SP, :, 0:128],
                                                  in1=rden.ap[0:SP].unsqueeze(2).to_broadcast([SP, 4, 128]), op=ALU.mult),
                 reads=[osm2, rden], writes=[osm2])
            S.op("dve", lambda e: e.scalar_tensor_tensor(out=osn.ap[0:SP], in0=osm2.ap[0:SP, :, 0:128], scalar=neglam.ap[0:SP, 0:1],
                                                        in1=osn.ap[0:SP], op0=ALU.mult, op1=ALU.add),
                 reads=[osm2, neglam, osn], writes=[osn])
            for h in range(4):
                S.op("act", lambda e, h=h: e.activation(out=ojunk.ap[0:SP, :], in_=osn.ap[0:SP, h, :], func=AF.Square,
                                                        accum_out=ssn.ap[0:SP, h:h + 1]), reads=[osn], writes=[ojunk, ssn])
            S.op("dve", lambda e: e.tensor_scalar(out=ssn.ap[0:SP, :], in0=ssn.ap[0:SP, :], scalar1=1.0 / 128, scalar2=EPS,
                                                  op0=ALU.mult, op1=ALU.add), reads=[ssn], writes=[ssn])
            S.op("act", lambda e: e.activation(out=ssn.ap[0:SP, :], in_=ssn.ap[0:SP, :], func=AF.Sqrt), reads=[ssn], writes=[ssn])
            S.op("dve", lambda e: e.reciprocal(out=ssn.ap[0:SP, :], in_=ssn.ap[0:SP, :]), reads=[ssn], writes=[ssn])
            for h in range(4):
                S.op("dve", lambda e, h=h: e.scalar_tensor_tensor(out=obf.ap[0:SP, h, :], in0=osn.ap[0:SP, h, :],
                                                                  scalar=ssn.ap[0:SP, h:h + 1], in1=gsub.ap[0:SP, :],
                                                                  op0=ALU.mult, op1=ALU.mult), reads=[osn, ssn, gsub], writes=[obf])
            for h in range(4):
                S.op("pe", lambda e, h=h: e.transpose(out=pXc.ap[:, h * 128:h * 128 + SP], in_=obf.ap[0:SP, h, :],
                                                      identity=ident.ap[0:SP, 0:SP]), reads=[obf, ident], writes=[pXc], inc=(h == 3))
            S.op("dve", lambda e: e.tensor_copy(out=catS.ap[:, 4:8, :],
                                                in_=pXc.ap[:, 0:512].rearrange("p (h t) -> p h t", t=128)[:, :, 0:SP]),
                 reads=[pXc], writes=[catS])
            S.barrier()
        M.release(base_mark)

        with psum_scope() as pst:
            pM = [pbank(pst, f"pM{i}") for i in range(4)]
            wo = M.alloc("wo", [128, 8, D], BF16)
            gtm_p = M.alloc("gtm_p", [128, D], F32)
            gtm_s = M.alloc("gtm_s", [128, D], F32)
            xd = [M.alloc(f"xd{i}", [128, D], F32) for i in range(2)]
            x1t = [M.alloc(f"x1t{i}", [128, D], F32) for i in range(2)]
            w_out_v = w_out.rearrange("(k p) n -> p k n", p=128)
            wos = [M.alloc(f"wos{i_}", [128, 4096], F32) for i_ in range(2)]
            for j in range(2):
                load_cast(wo, wo.ap[:, :, j * 512:(j + 1) * 512], w_out_v[:, :, j * 512:(j + 1) * 512], wos[j], (128, 8, 512))
            S.dma("sp", lambda e: e.dma_start(out=gtm_p.ap, in_=mod_p.ap[2]), writes=[gtm_p], group="d1mods")
            S.dma("sp", lambda e: e.dma_start(out=gtm_s.ap[0:16, :], in_=mod_s.ap[2]), writes=[gtm_s], group="d1mods")
            for i in range(17):
                np_ = 128 if i < 16 else 16
                par = i % 2
                xt = xd[par]
                x1 = x1t[par]
                if i < 16:
                    m, t = i // 4, i % 4
                    sl = 12 + 16 * m + t
                    S.dma("sp", lambda e, xt=xt, sl=sl: e.dma_start(out=xt.ap, in_=xs[sl * 128:(sl + 1) * 128, :]), writes=[xt])
                    gt = gtm_p
                else:
                    S.dma("sp", lambda e, xt=xt: e.dma_start(out=xt.ap[0:16, :], in_=xsm), writes=[xt])
                    gt = gtm_s
                for nb in range(2):
                    pm = pM[par * 2 + nb]
                    for fc in range(8):
                        if i < 16:
                            lt = (convT.ap[:, fc, i * 128:(i + 1) * 128] if fc < 4 else oT.ap[:, fc - 4, i * 128:(i + 1) * 128])
                            rd = convT if fc < 4 else oT
                        else:
                            lt = catS.ap[:, fc, :]
                            rd = catS
                        S.op("pe", lambda e, pm=pm, lt=lt, fc=fc, nb=nb, np_=np_: e.matmul(
                            pm.ap[0:np_, :], lhsT=lt, rhs=wo.ap[:, fc, nb * 512:(nb + 1) * 512], start=(fc == 0), stop=(fc == 7)),
                            reads=[rd, wo], writes=[pm], inc=(fc == 7))
                    S.op("dve", lambda e, pm=pm, x1=x1, gt=gt, nb=nb, np_=np_: e.tensor_tensor(
                        out=x1.ap[0:np_, nb * 512:(nb + 1) * 512], in0=pm.ap[0:np_, :], in1=gt.ap[0:np_, nb * 512:(nb + 1) * 512],
                        op=ALU.mult), reads=[pm, gt], writes=[x1])
                S.op("pool", lambda e, x1=x1, xt=xt, np_=np_: e.tensor_tensor(out=x1.ap[0:np_, :], in0=x1.ap[0:np_, :], in1=xt.ap[0:np_, :],
                                                                            op=ALU.add), reads=[x1, xt], writes=[x1])
                S.dma("sp", lambda e, x1=x1, i=i, np_=np_: e.dma_start(out=x1_scr.ap[i * 128:i * 128 + np_, :], in_=x1.ap[0:np_, :]),
                      reads=[x1], semkey=f"D_x1t{par}")
            S.barrier()
        M.release(0)

        with psum_scope() as pst:
            pT2 = pbank(pst, "pT2", BF16)
            pG = [pbank(pst, f"pG{i}") for i in range(2)]
            pU = [pbank(pst, f"pU{i}") for i in range(2)]
            pAT = pbank(pst, "pAT", BF16)
            pO = [pbank(pst, f"pO{i}") for i in range(2)]

            identb = M.alloc("identb", [128, 128], BF16)
            identf2 = M.alloc("identf2", [128, 128], F32)
            mh2 = M.alloc("mh2", [128, 1], F32)
            S.op("pool", lambda e: e.memset(identf2.ap, 0.0), writes=[identf2])
            S.op("pool", lambda e: e.affine_select(out=identf2.ap, in_=identf2.ap, pattern=[[-1, 128]],
                                                   compare_op=ALU.not_equal, fill=1.0, base=0, channel_multiplier=1),
                 reads=[identf2], writes=[identf2])
            S.op("dve", lambda e: e.tensor_copy(out=identb.ap, in_=identf2.ap), reads=[identf2], writes=[identb])
            S.op("pool", lambda e: e.memset(mh2.ap, -0.5), writes=[mh2])
            wf1 = M.alloc("wf1", [128, 8, 2 * DFF], BF16)
            wf2 = M.alloc("wf2", [128, 22, D], BF16)
            w_f1_v = w_f1.rearrange("(k p) n -> p k n", p=128)
            w_f2_v = w_f2.rearrange("(k p) n -> p k n", p=128)
            stg_mark = M.mark()
            wfs = [M.alloc(f"wfs{i_}", [128, 4096], F32) for i_ in range(3)]
            for j in range(11):
                load_cast(wf1, wf1.ap[:, :, j * 512:(j + 1) * 512], w_f1_v[:, :, j * 512:(j + 1) * 512], wfs[j % 3], (128, 8, 512))
            for j in range(6):
                k0, k1 = j * 4, min(22, j * 4 + 4)
                load_cast(wf2, wf2.ap[:, k0:k1, :], w_f2_v[:, k0:k1, :], wfs[(11 + j) % 3], (128, k1 - k0, 1024))
            S.barrier()
            M.release(stg_mark)
            Af_t = M.alloc("Af_t", [128, D], F32)
            shf_t = M.alloc("shf_t", [128, D], F32)
            gtf_t = M.alloc("gtf_t", [128, D], F32)
            gfin = M.alloc("gfin", [128, D], F32)
            x1b = [M.alloc(f"x1b{i}", [128, D], F32) for i in range(2)]
            x1e = M.alloc("x1e", [128, D], F32)
            sq2 = [M.alloc(f"sq2{i}", [128, 1], F32) for i in range(2)]
            rs2 = [M.alloc(f"rs2{i}", [128, 1], F32) for i in range(2)]
            sq3 = [M.alloc(f"sq3{i}", [128, 1], F32) for i in range(2)]
            rs3 = [M.alloc(f"rs3{i}", [128, 1], F32) for i in range(2)]
            hp2 = M.alloc("hp2", [128, D], F32)
            h2b = [M.alloc(f"h2b{i}", [128, D], BF16) for i in range(2)]
            h2T = [M.alloc(f"h2T{i}", [128, 8, 128], BF16) for i in range(2)]
            sgt = [M.alloc(f"sgt{i}", [128, 352], F32) for i in range(2)]
            actb = [M.alloc(f"actb{i}", [128, DFF], BF16) for i in range(2)]
            actT = [M.alloc(f"actT{i}", [128, 22, 128], BF16) for i in range(2)]
            x2t = [M.alloc(f"x2t{i}", [128, D], F32) for i in range(2)]

            S.dma("sp", lambda e: e.dma_start(out=Af_t.ap, in_=mod_p.ap[4]), writes=[Af_t], group="d2mods")
            S.dma("sp", lambda e: e.dma_start(out=shf_t.ap, in_=mod_p.ap[3]), writes=[shf_t], group="d2mods")
            S.dma("sp", lambda e: e.dma_start(out=gtf_t.ap, in_=mod_p.ap[5]), writes=[gtf_t], group="d2mods")
            S.dma("sp", lambda e: e.dma_start(out=gfin.ap, in_=g_fin.to_broadcast([128, D])), writes=[gfin], group="d2mods")

            NTL = 17

            def npof(i):
                return 128 if i < 16 else 16

            def e0(i):
                np_, x1 = npof(i), x1b[i % 2]
                S.dma("sp", lambda e: e.dma_start(out=x1.ap[0:np_, :], in_=x1_scr.ap[i * 128:i * 128 + np_, :]), writes=[x1])

            def e1(i):
                np_, x1, sq, jk = npof(i), x1b[i % 2], sq2[i % 2], h2b[i % 2]
                S.op("act", lambda e: e.activation(out=jk.ap[0:np_, :], in_=x1.ap[0:np_, :], func=AF.Square,
                                                   accum_out=sq.ap[0:np_, :]), reads=[x1], writes=[jk, sq])
                S.op("dve", lambda e: e.tensor_scalar(out=sq.ap[0:np_, :], in0=sq.ap[0:np_, :], scalar1=1.0 / D, scalar2=EPS,
                                                      op0=ALU.mult, op1=ALU.add), reads=[sq], writes=[sq])

            def e2(i):
                np_, x1, sq, rs, hbt = npof(i), x1b[i % 2], sq2[i % 2], rs2[i % 2], h2b[i % 2]
                if i == 16:
                    S.dma("sp", lambda e: e.dma_start(out=Af_t.ap[0:16, :], in_=mod_s.ap[4]), writes=[Af_t], semkey="D_afs")
                    S.dma("sp", lambda e: e.dma_start(out=shf_t.ap[0:16, :], in_=mod_s.ap[3]), writes=[shf_t], semkey="D_shs")
                S.op("act", lambda e: e.activation(out=sq.ap[0:np_, :], in_=sq.ap[0:np_, :], func=AF.Sqrt), reads=[sq], writes=[sq])
                S.op("dve", lambda e: e.reciprocal(out=rs.ap[0:np_, :], in_=sq.ap[0:np_, :]), reads=[sq], writes=[rs])
                S.op("dve", lambda e: e.scalar_tensor_tensor(out=hp2.ap[0:np_, :], in0=x1.ap[0:np_, :], scalar=rs.ap[0:np_, 0:1],
                                                            in1=Af_t.ap[0:np_, :], op0=ALU.mult, op1=ALU.mult),
                     reads=[x1, rs, Af_t], writes=[hp2])
                S.op("pool", lambda e: e.tensor_tensor(out=hbt.ap[0:np_, :], in0=hp2.ap[0:np_, :], in1=shf_t.ap[0:np_, :], op=ALU.add),
                     reads=[hp2, shf_t], writes=[hbt])

            def e3(i):
                np_, hbt, hTt = npof(i), h2b[i % 2], h2T[i % 2]
                for kc in range(8):
                    S.op("pe", lambda e, kc=kc: e.transpose(out=pT2.ap[:, kc * 128:kc * 128 + np_],
                                                            in_=hbt.ap[0:np_, kc * 128:(kc + 1) * 128],
                                                            identity=identb.ap[0:np_, 0:np_]),
                         reads=[hbt, identb], writes=[pT2], inc=(kc == 7))
                S.op("act", lambda e: e.copy(out=hTt.ap[:, :, 0:np_],
                                             in_=pT2.ap[:, 0:1024].rearrange("p (c t) -> p c t", t=128)[:, :, 0:np_]),
                     reads=[pT2], writes=[hTt])

            def e4(i):
                np_, hTt, ab = npof(i), h2T[i % 2], actb[i % 2]
                for blk in range(8):
                    pg_, pu_ = pG[blk % 2], pU[blk % 2]
                    c0 = blk * 352
                    for kc in range(8):
                        S.op("pe", lambda e, kc=kc, pg_=pg_, c0=c0: e.matmul(
                            pg_.ap[0:np_, 0:352], lhsT=hTt.ap[:, kc, 0:np_], rhs=wf1.ap[:, kc, c0:c0 + 352],
                            start=(kc == 0), stop=(kc == 7)), reads=[hTt, wf1], writes=[pg_], inc=(kc == 7))
                    for kc in range(8):
                        S.op("pe", lambda e, kc=kc, pu_=pu_, c0=c0: e.matmul(
                            pu_.ap[0:np_, 0:352], lhsT=hTt.ap[:, kc, 0:np_], rhs=wf1.ap[:, kc, DFF + c0:DFF + c0 + 352],
                            start=(kc == 0), stop=(kc == 7)), reads=[hTt, wf1], writes=[pu_], inc=(kc == 7))
                    sg = sgt[blk % 2]
                    S.op("act", lambda e, sg=sg, pg_=pg_: e.activation(out=sg.ap[0:np_, :], in_=pg_.ap[0:np_, 0:352], func=AF.Silu),
                         reads=[pg_], writes=[sg])
                    S.op("dve", lambda e, sg=sg, pu_=pu_, c0=c0: e.tensor_tensor(
                        out=ab.ap[0:np_, c0:c0 + 352], in0=pu_.ap[0:np_, 0:352], in1=sg.ap[0:np_, :], op=ALU.mult),
                        reads=[pu_, sg], writes=[ab])

            def e5(i):
                np_, ab, aT = npof(i), actb[i % 2], actT[i % 2]
                S.dma("sp", lambda e: e.dma_start(out=x1e.ap[0:np_, :], in_=x1_scr.ap[i * 128:i * 128 + np_, :]), writes=[x1e])
                if i == 16:
                    S.dma("sp", lambda e: e.dma_start(out=gtf_t.ap[0:16, :], in_=mod_s.ap[5]), writes=[gtf_t], semkey="D_gts")
                for r0 in range(0, 22, 8):
                    nr = min(8, 22 - r0)
                    for k2 in range(nr):
                        fc = r0 + k2
                        S.op("pe", lambda e, fc=fc, k2=k2: e.transpose(
                            out=pAT.ap[:, k2 * 128:k2 * 128 + np_], in_=ab.ap[0:np_, fc * 128:(fc + 1) * 128],
                            identity=identb.ap[0:np_, 0:np_]), reads=[ab, identb], writes=[pAT], inc=(k2 == nr - 1))
                    src = pAT.ap[:, 0:nr * 128].rearrange("p (c t) -> p c t", t=128)[:, :, 0:np_]
                    dst = aT.ap[:, r0:r0 + nr, 0:np_]
                    if (r0 // 8) % 2 == 0:
                        S.op("act", lambda e, src=src, dst=dst: e.copy(out=dst, in_=src), reads=[pAT], writes=[aT])
                    else:
                        S.op("dve", lambda e, src=src, dst=dst: e.tensor_copy(out=dst, in_=src), reads=[pAT], writes=[aT])

            def e6(i):
                np_, aT, x2 = npof(i), actT[i % 2], x2t[i % 2]
                for nb in range(2):
                    po = pO[nb]
                    for fc in range(22):
                        S.op("pe", lambda e, fc=fc, po=po, nb=nb: e.matmul(
                            po.ap[0:np_, :], lhsT=aT.ap[:, fc, 0:np_], rhs=wf2.ap[:, fc, nb * 512:(nb + 1) * 512],
                            start=(fc == 0), stop=(fc == 21)), reads=[aT, wf2], writes=[po], inc=(fc == 21))
                    S.op("dve", lambda e, po=po, nb=nb: e.tensor_tensor(
                        out=x2.ap[0:np_, nb * 512:(nb + 1) * 512], in0=po.ap[0:np_, :], in1=gtf_t.ap[0:np_, nb * 512:(nb + 1) * 512],
                        op=ALU.mult), reads=[po, gtf_t], writes=[x2])
                S.op("pool", lambda e: e.tensor_tensor(out=x2.ap[0:np_, :], in0=x2.ap[0:np_, :], in1=x1e.ap[0:np_, :], op=ALU.add),
                     reads=[x2, x1e], writes=[x2])

            def e7(i):
                np_, x2, sq = npof(i), x2t[i % 2], sq3[i % 2]
                S.op("act", lambda e: e.activation(out=hp2.ap[0:np_, :], in_=x2.ap[0:np_, :], func=AF.Square,
                                                   accum_out=sq.ap[0:np_, :]), reads=[x2], writes=[hp2, sq])
                S.op("dve", lambda e: e.tensor_scalar(out=sq.ap[0:np_, :], in0=sq.ap[0:np_, :], scalar1=1.0 / D, scalar2=EPS,
                                                      op0=ALU.mult, op1=ALU.add), reads=[sq], writes=[sq])

            def e8(i):
                np_, x2, sq, rs = npof(i), x2t[i % 2], sq3[i % 2], rs3[i % 2]
                S.op("act", lambda e: e.activation(out=sq.ap[0:np_, :], in_=sq.ap[0:np_, :], func=AF.Sqrt), reads=[sq], writes=[sq])
                S.op("dve", lambda e: e.reciprocal(out=rs.ap[0:np_, :], in_=sq.ap[0:np_, :]), reads=[sq], writes=[rs])
                S.op("dve", lambda e: e.scalar_tensor_tensor(out=x2.ap[0:np_, :], in0=x2.ap[0:np_, :], scalar=rs.ap[0:np_, 0:1],
                                                            in1=gfin.ap[0:np_, :], op0=ALU.mult, op1=ALU.mult),
                     reads=[x2, rs, gfin], writes=[x2])
                if i < 16:
                    S.dma("sp", lambda e: e.dma_start(out=y_o[i * 128:(i + 1) * 128, :], in_=x2.ap), reads=[x2],
                          semkey="D_yo", is_out=True)
                else:
                    S.dma("sp", lambda e: e.dma_start(out=ys_o, in_=x2.ap[0:16, :]), reads=[x2], semkey="D_yso", is_out=True)

            estages = [e0, e1, e2, e3, e4, e5, e6, e7, e8]
            for it in range(NTL + len(estages) - 1):
                for k in range(len(estages) - 1, -1, -1):
                    sl = it - k
                    if 0 <= sl < NTL:
                        estages[k](sl)

        need = {}
        for k, v in S.out_evs:
            need[k] = max(need.get(k, 0), v)
        S.prog["sp"].append((list(need.items()), None, None, 0))
        S.replay()
    return nc


_CACHE = {}


def _rope_table(pos):
    inv = (500000.0 ** (-np.arange(0, 16, 2, dtype=np.float32) / 16)).astype(np.float32)
    ang = pos.astype(np.float32)[:, None] * inv[None, :]
    return np.concatenate([np.cos(ang), np.sin(ang)], axis=1).astype(np.float32)


def kernel(x_prompt, x_sample, cache_k, cache_v, state_conv, page_table, c_prompt, c_sample,
           norm_mix_g, norm_ffn_g, norm_final_g, w_ada, b_ada, w_in, conv_dw_w, conv_dw_b,
           conv_ln_g, conv_ln_b, lambda_q1, lambda_k1, lambda_q2, lambda_k2, subln_g,
           w_out, w_ffn_in, w_ffn_out):
    f = lambda a: np.ascontiguousarray(np.asarray(a), dtype=np.float32)
    x_prompt = f(x_prompt)
    npool = int(np.asarray(cache_k).shape[1])
    past = int(np.asarray(page_table).shape[1]) * 128
    if npool not in _CACHE:
        _CACHE[npool] = build_program(npool)
    nc = _CACHE[npool]
    ck = f(cache_k).reshape(npool * 128, 512)
    cv = f(cache_v).reshape(npool * 128, 512)
    sel = np.zeros((30, 16, 16), np.float32)
    for s in range(16):
        sel[:, s, s] = 1.0
    shared = {
        "cache_k": ck, "cache_v": cv,
        "w_ada": f(w_ada)[0], "b_ada": f(b_ada), "w_in": f(w_in)[0], "w_out": f(w_out)[0],
        "w_f1": f(w_ffn_in)[0], "w_f2": f(w_ffn_out)[0],
        "g_mix": f(norm_mix_g), "g_ffn": f(norm_ffn_g), "g_fin": f(norm_final_g).reshape(1, D),
        "dw_w": f(conv_dw_w)[0], "dw_b": f(conv_dw_b), "cln_g": f(conv_ln_g), "cln_b": f(conv_ln_b),
        "lam4": np.concatenate([f(lambda_q1), f(lambda_k1), f(lambda_q2), f(lambda_k2)], axis=1),
        "subg": f(subln_g), "sel": sel.reshape(30, 256),
        "lane": np.arange(128, dtype=np.float32).reshape(128, 1),
        "rope_s": np.repeat(_rope_table(np.array([past])), 16, axis=0),
    }
    pt = np.asarray(page_table).astype(np.int32)
    in_maps = []
    tile_of = []
    for c in range(8):
        b, j = c // 4, c % 4
        npad = 12 - 4 * j
        xs = np.zeros((NS, 128, D), np.float32)
        pos = np.zeros((NS, 128), np.float32)
        val = np.zeros((128, NS), np.float32)
        xt = x_prompt[b].reshape(64, 128, D)
        gl = []
        for s in range(NS):
            g = s - npad
            gl.append(g)
            if g >= 0:
                xs[s] = xt[g]
                pos[s] = g * 128 + np.arange(128)
                val[:, s] = 1.0
        tile_of.append(gl)
        m = dict(shared)
        m.update({
            "xs": xs.reshape(NS * 128, D), "rope": np.ascontiguousarray(_rope_table(pos.reshape(-1)).reshape(NS, 128, 16).transpose(1, 0, 2)).reshape(128, NS * 16), "valid": val,
            "xsm": f(x_sample)[16 * c:16 * c + 16, 0, :], "csm": f(c_sample)[16 * c:16 * c + 16],
            "cpr": f(c_prompt)[b:b + 1], "ptab": pt[16 * c:16 * c + 16].reshape(1, 256),
            "stconv": f(state_conv)[0, 16 * c:16 * c + 16],
        })
        in_maps.append(m)
    res = run_bass_kernel_spmd(nc, in_maps, core_ids=list(range(8))).results
    B, SEQ = x_prompt.shape[0], x_prompt.shape[1]
    y_prompt = np.zeros((B, SEQ, D), np.float32)
    k_prompt = np.zeros((1, B, SEQ, 4, 128), np.float32)
    v_prompt = np.zeros((1, B, SEQ, 4, 128), np.float32)
    conv_prompt = np.zeros((1, B, 30, 512), np.float32)
    y_sample = np.zeros((128, 1, D), np.float32)
    k_sample = np.zeros((1, 128, 1, 4, 128), np.float32)
    v_sample = np.zeros((1, 128, 1, 4, 128), np.float32)
    conv_sample = np.zeros((1, 128, 30, 512), np.float32)
    for c in range(8):
        b, j = c // 4, c % 4
        r = res[c]
        for m in range(4):
            for t in range(4):
                i = m * 4 + t
                g = tile_of[c][12 + 16 * m + t]
                y_prompt[b, g * 128:(g + 1) * 128] = r["y_o"][i * 128:(i + 1) * 128]
                k_prompt[0, b, g * 128:(g + 1) * 128] = r["k_o"][i * 128:(i + 1) * 128].reshape(128, 4, 128)
                v_prompt[0, b, g * 128:(g + 1) * 128] = r["v_o"][i * 128:(i + 1) * 128].reshape(128, 4, 128)
        if j == 3:
            conv_prompt[0, b] = r["cp_o"]
        y_sample[16 * c:16 * c + 16, 0] = r["ys_o"]
        k_sample[0, 16 * c:16 * c + 16, 0] = r["ks_o"].reshape(16, 4, 128)
        v_sample[0, 16 * c:16 * c + 16, 0] = r["vs_o"].reshape(16, 4, 128)
        conv_sample[0, 16 * c:16 * c + 16] = r["cs_o"]
    return (y_prompt, y_sample, k_prompt, v_prompt, conv_prompt, k_sample, v_sample, conv_sample)
```
